# Optimizing a Trainium2 kernel written in Bass

```python
import jax, jax.numpy as jnp
from jax import lax
import numpy as np

D_MODEL = 1024
BATCH = 8
SEQ = 2048
DEPTH = 4

HEAD_DIM = 64
A_HEADS = 8
A_KV_HEADS = 2
A_GROUP = A_HEADS // A_KV_HEADS
A_WINDOW = 128
A_BLOCK = 128
A_BAND_BLOCKS = 3
B_HEADS = 8
GRID_W = 64
B_MAX_ROWS = 8
B_WIN_C = 16
B_QBLOCK_C = 16
B_KBAND_C = 32
FFN_HIDDEN = -(-8 * D_MODEL // (3 * 256)) * 256
ROPE_THETA = 10000.0
EPS = 1e-6
NEG = -1e30

A_Q_W = A_HEADS * HEAD_DIM
A_KV_W = A_KV_HEADS * HEAD_DIM
B_W = B_HEADS * HEAD_DIM
IN_WIDTHS = [A_Q_W, A_KV_W, A_KV_W, B_W, B_W, B_W, D_MODEL, D_MODEL]
IN_COLS = sum(IN_WIDTHS)
IN_SPLITS = np.cumsum(IN_WIDTHS)[:-1].tolist()

kernel_name = "hybrid_window_gqa_neighbourhood_encoder"


def rms_norm(x, g):
    xf = x.astype(jnp.float32)
    y = xf * lax.rsqrt(jnp.mean(xf * xf, axis=-1, keepdims=True) + EPS)
    return (y * g.astype(jnp.float32)).astype(x.dtype)


def rope_tables(positions):
    half = HEAD_DIM // 2
    inv = ROPE_THETA ** (-jnp.arange(half, dtype=jnp.float32) / half)
    ang = positions.astype(jnp.float32)[:, None] * inv[None, :]
    return jnp.cos(ang)[None, :, None, :], jnp.sin(ang)[None, :, None, :]


def apply_rope(x, cos, sin):
    half = HEAD_DIM // 2
    xf = x.astype(jnp.float32)
    x1, x2 = xf[..., :half], xf[..., half:]
    out = jnp.concatenate([x1 * cos - x2 * sin, x2 * cos + x1 * sin], axis=-1)
    return out.astype(x.dtype)


def window_gqa(q, k, v, sink):
    b, s = q.shape[:2]
    nb = s // A_BLOCK
    pad = ((0, 0), (A_BLOCK, A_BLOCK), (0, 0), (0, 0))
    kp = jnp.pad(k, pad).reshape(b, nb + 2, A_BLOCK, A_KV_HEADS, HEAD_DIM)
    vp = jnp.pad(v, pad).reshape(b, nb + 2, A_BLOCK, A_KV_HEADS, HEAD_DIM)
    kb = jnp.concatenate([kp[:, o:o + nb] for o in range(A_BAND_BLOCKS)], axis=2)
    vb = jnp.concatenate([vp[:, o:o + nb] for o in range(A_BAND_BLOCKS)], axis=2)
    qb = q.reshape(b, nb, A_BLOCK, A_KV_HEADS, A_GROUP, HEAD_DIM)
    scores = jnp.einsum('bnqkgd,bnjkd->bnkgqj', qb, kb).astype(jnp.float32) * (HEAD_DIM ** -0.5)
    n_i = np.arange(nb)[:, None, None]
    r_i = np.arange(A_BLOCK)[None, :, None]
    j_i = np.arange(A_BAND_BLOCKS * A_BLOCK)[None, None, :]
    kpos = n_i * A_BLOCK - A_BLOCK + j_i
    qpos = n_i * A_BLOCK + r_i
    valid = (np.abs(kpos - qpos) <= A_WINDOW) & (kpos >= 0) & (kpos < s)
    scores = jnp.where(valid[None, :, None, None], scores, NEG)
    sink_col = jnp.broadcast_to(
        sink.astype(jnp.float32).reshape(A_KV_HEADS, A_GROUP)[None, None, :, :, None, None],
        scores.shape[:-1] + (1,))
    p = jax.nn.softmax(jnp.concatenate([scores, sink_col], axis=-1), axis=-1)[..., :-1]
    out = jnp.einsum('bnkgqj,bnjkd->bnqkgd', p.astype(v.dtype), vb)
    return out.reshape(b, s, A_Q_W)


def neighbourhood_attn(q, k, v, rel_bias):
    b, s = q.shape[:2]
    rows = s // GRID_W
    wr = min(B_MAX_ROWS, rows)
    ncb = GRID_W // B_QBLOCK_C
    r = np.arange(rows)
    rs = np.clip(r - wr // 2, 0, rows - wr)
    row_idx = rs[:, None] + np.arange(wr)[None, :]
    jb = np.arange(ncb)
    kc0 = np.clip(jb * B_QBLOCK_C - B_WIN_C // 2, 0, GRID_W - B_KBAND_C)
    col_idx = kc0[:, None] + np.arange(B_KBAND_C)[None, :]
    qcol = jb[:, None] * B_QBLOCK_C + np.arange(B_QBLOCK_C)[None, :]
    cs = np.clip(qcol - B_WIN_C // 2, 0, GRID_W - B_WIN_C)
    valid = (col_idx[:, None, :] >= cs[..., None]) & (col_idx[:, None, :] < cs[..., None] + B_WIN_C)
    dr_i = row_idx - r[:, None] + B_MAX_ROWS - 1
    dc_i = np.clip(col_idx[:, None, :] - qcol[..., None] + B_WIN_C - 1, 0, 2 * B_WIN_C - 2)
    bias = rel_bias[:, dr_i[:, None, None, :, None], dc_i[None, :, :, None, :]]
    bias = jnp.where(valid[None, None, :, :, None, :], bias.astype(jnp.float32), NEG)
    kg = k.reshape(b, rows, GRID_W, B_HEADS, HEAD_DIM)
    vg = v.reshape(b, rows, GRID_W, B_HEADS, HEAD_DIM)
    gr = row_idx[:, None, :, None]
    gc = col_idx[None, :, None, :]
    kb = kg[:, gr, gc]
    vb = vg[:, gr, gc]
    qg = q.reshape(b, rows, ncb, B_QBLOCK_C, B_HEADS, HEAD_DIM)
    scores = jnp.einsum('brjqhd,brjwkhd->bhrjqwk', qg, kb).astype(jnp.float32) * (HEAD_DIM ** -0.5)
    scores = scores + bias[None]
    shp = scores.shape
    p = jax.nn.softmax(scores.reshape(shp[:-2] + (wr * B_KBAND_C,)), axis=-1).reshape(shp)
    out = jnp.einsum('bhrjqwk,brjwkhd->brjqhd', p.astype(v.dtype), vb)
    return out.reshape(b, s, B_W)


def setup_inputs(seed: int = 0) -> dict:
    key = jax.random.key(seed)
    ks = jax.random.split(key, 16)
    f32 = jnp.float32

    def w(k, shape, fan_in):
        return jax.random.normal(k, shape, f32) * (fan_in ** -0.5)

    def gain(k, shape):
        return 1.0 + 0.02 * jax.random.normal(k, shape, f32)

    return {
        "x": jax.random.normal(ks[0], (BATCH, SEQ, D_MODEL), f32),
        "positions": jnp.arange(SEQ, dtype=jnp.int32),
        "attn_norm_g": gain(ks[1], (DEPTH, D_MODEL)),
        "w_in": w(ks[2], (DEPTH, D_MODEL, IN_COLS), D_MODEL),
        "q_norm_a": gain(ks[3], (DEPTH, HEAD_DIM)),
        "k_norm_a": gain(ks[4], (DEPTH, HEAD_DIM)),
        "sink_a": 0.5 * jax.random.normal(ks[5], (DEPTH, A_HEADS), f32),
        "q_norm_b": gain(ks[6], (DEPTH, HEAD_DIM)),
        "k_norm_b": gain(ks[7], (DEPTH, HEAD_DIM)),
        "rel_bias_b": 0.1 * jax.random.normal(ks[8], (DEPTH, B_HEADS, 2 * B_MAX_ROWS - 1, 2 * B_WIN_C - 1), f32),
        "w_proj_a": w(ks[9], (DEPTH, A_Q_W, D_MODEL), A_Q_W),
        "w_proj_b": w(ks[10], (DEPTH, B_W, D_MODEL), B_W),
        "w_out": w(ks[11], (DEPTH, D_MODEL, D_MODEL), D_MODEL),
        "ffn_norm_g": gain(ks[12], (DEPTH, D_MODEL)),
        "w_gate": w(ks[13], (DEPTH, D_MODEL, FFN_HIDDEN), D_MODEL),
        "w_up": w(ks[14], (DEPTH, D_MODEL, FFN_HIDDEN), D_MODEL),
        "w_down": w(ks[15], (DEPTH, FFN_HIDDEN, D_MODEL), FFN_HIDDEN),
    }


def reference(x, positions, attn_norm_g, w_in, q_norm_a, k_norm_a, sink_a, q_norm_b, k_norm_b,
              rel_bias_b, w_proj_a, w_proj_b, w_out, ffn_norm_g, w_gate, w_up, w_down):
    b, s, _ = x.shape
    cos, sin = rope_tables(positions)
    for l in range(DEPTH):
        h = rms_norm(x, attn_norm_g[l])
        z = jnp.einsum('bsd,dc->bsc', h, w_in[l])
        qa, ka, va, qb, kb, vb, ga, gb = jnp.split(z, IN_SPLITS, axis=-1)
        qa = apply_rope(rms_norm(qa.reshape(b, s, A_HEADS, HEAD_DIM), q_norm_a[l]), cos, sin)
        ka = apply_rope(rms_norm(ka.reshape(b, s, A_KV_HEADS, HEAD_DIM), k_norm_a[l]), cos, sin)
        va = va.reshape(b, s, A_KV_HEADS, HEAD_DIM)
        ya = jnp.einsum('bsc,cd->bsd', window_gqa(qa, ka, va, sink_a[l]), w_proj_a[l])
        qb = rms_norm(qb.reshape(b, s, B_HEADS, HEAD_DIM), q_norm_b[l])
        kb = rms_norm(kb.reshape(b, s, B_HEADS, HEAD_DIM), k_norm_b[l])
        vb = vb.reshape(b, s, B_HEADS, HEAD_DIM)
        yb = jnp.einsum('bsc,cd->bsd', neighbourhood_attn(qb, kb, vb, rel_bias_b[l]), w_proj_b[l])
        mixed = jax.nn.sigmoid(ga) * ya + jax.nn.sigmoid(gb) * yb
        x = x + jnp.einsum('bsd,de->bse', mixed, w_out[l])
        h = rms_norm(x, ffn_norm_g[l])
        u = jax.nn.silu(jnp.einsum('bsd,df->bsf', h, w_gate[l])) * jnp.einsum('bsd,df->bsf', h, w_up[l])
        x = x + jnp.einsum('bsf,fd->bsd', u, w_down[l])
    return x
```

```python
import numpy as np
from contextlib import ExitStack
import concourse.bass as bass
import concourse.mybir as mybir
from concourse.bass_utils import run_bass_kernel_spmd

F32 = mybir.dt.float32
BF16 = mybir.dt.bfloat16
I32 = mybir.dt.int32
ALU = mybir.AluOpType
AF = mybir.ActivationFunctionType
AX = mybir.AxisListType

S = 2048
D = 1024
NT = 16
FF = 2816
NFC = 22
INC = 4352
NEG = -30000.0
EPS = 1e-6
DEPTH = 4
FUSED = True

ENGS = ["pe", "act", "dve", "pool", "sp"]


class Op:
    __slots__ = ("eng", "fn", "deps", "marked", "count", "sem", "is_dma", "idx")

    def __init__(self, eng, fn):
        self.eng = eng
        self.fn = fn
        self.deps = []
        self.marked = False
        self.count = 0
        self.sem = None
        self.is_dma = False


class Rec:
    def __init__(self, nc, stack):
        self.nc = nc
        self.stack = stack
        self.ops = {e: [] for e in ENGS}
        self.hist = {}
        self.names = {}
        self.engsem = {e: stack.enter_context(nc.semaphore("s_" + e)) for e in ENGS}
        self.dmasems = {}
        self.dmacnt = {}
        self.nops = 0

    def res(self, name):
        if name not in self.names:
            self.names[name] = len(self.names)
        i = self.names[name]
        return ("N", i, i + 1)

    def _sem_for(self, key):
        if key not in self.dmasems:
            self.dmasems[key] = self.stack.enter_context(self.nc.semaphore("d_%d" % len(self.dmasems)))
            self.dmacnt[key] = 0
        return self.dmasems[key]

    def add(self, eng, fn, reads=(), writes=(), dma=None):
        op = Op(eng, fn)
        op.idx = self.nops
        self.nops += 1
        deps = {}
        for (sp, lo, hi) in reads:
            for rec in self.hist.get(sp, ()):
                if rec[3] and rec[0] < hi and lo < rec[1]:
                    deps[id(rec[2])] = rec[2]
        for (sp, lo, hi) in writes:
            for rec in self.hist.get(sp, ()):
                if rec[0] < hi and lo < rec[1]:
                    deps[id(rec[2])] = rec[2]
        for d in deps.values():
            if d.eng == "pe" and eng == "pe":
                continue
            d.marked = True
            op.deps.append(d)
        if dma is not None:
            op.is_dma = True
            op.sem = self._sem_for(dma)
            self.dmacnt[dma] += 16
            op.count = self.dmacnt[dma]
        for (sp, lo, hi) in writes:
            h = self.hist.setdefault(sp, [])
            h[:] = [r for r in h if not (lo <= r[0] and r[1] <= hi)]
            h.append([lo, hi, op, True])
        for (sp, lo, hi) in reads:
            h = self.hist.setdefault(sp, [])
            if not op.is_dma:
                h[:] = [r for r in h if not ((not r[3]) and r[0] == lo and r[1] == hi
                                             and r[2].eng == eng and not r[2].is_dma)]
            h.append([lo, hi, op, False])
        self.ops[eng].append(op)
        return op

    def emit(self, final_waits=()):
        nc = self.nc
        for e in ENGS:
            c = 0
            for op in self.ops[e]:
                if op.is_dma:
                    continue
                if op.marked:
                    c += 1
                    op.count = c
                    op.sem = self.engsem[e]
        ops = self.ops
        engsem = self.engsem

        def make_body(e):
            def body(eng):
                waited = {}
                for op in ops[e]:
                    need = {}
                    for d in op.deps:
                        k = id(d.sem)
                        if k not in need or need[k][1] < d.count:
                            need[k] = (d.sem, d.count)
                    for k, (sem, val) in need.items():
                        if waited.get(k, 0) >= val:
                            continue
                        eng.wait_ge(sem, val)
                        waited[k] = val
                    ins = op.fn(eng)
                    if op.is_dma:
                        ins.then_inc(op.sem, 16)
                    elif op.marked:
                        ins.then_inc(engsem[e], 1)
                if e == "sp":
                    for (sem, val) in final_waits:
                        eng.wait_ge(sem, val)
            return body

        with nc.Block() as block:
            block.tensor(make_body("pe"))
            block.scalar(make_body("act"))
            block.vector(make_body("dve"))
            block.gpsimd(make_body("pool"))
            block.sync(make_body("sp"))


def bc(ap, shape):
    return ap.to_broadcast(list(shape))


def build(nl, dbg=None):
    nc = bass.Bass("TRN2", target_bir_lowering=False)
    dt = nc.dram_tensor
    x_d = dt("x", [S, D], F32, kind="ExternalInput").ap()
    pos_d = dt("pos", [128, NT], I32, kind="ExternalInput").ap()
    ang_d = dt("attn_norm_g", [nl, D], F32, kind="ExternalInput").ap()
    win_d = dt("w_in", [nl, D, INC], F32, kind="ExternalInput").ap()
    qna_d = dt("q_norm_a", [nl, 64], F32, kind="ExternalInput").ap()
    kna_d = dt("k_norm_a", [nl, 64], F32, kind="ExternalInput").ap()
    snk_d = dt("sink_a", [nl, 8], F32, kind="ExternalInput").ap()
    qnb_d = dt("q_norm_b", [nl, 64], F32, kind="ExternalInput").ap()
    knb_d = dt("k_norm_b", [nl, 64], F32, kind="ExternalInput").ap()
    bias_d = dt("bias_t", [nl, 4, 128, 4096], F32, kind="ExternalInput").ap()
    wpa_d = dt("w_proj_a", [nl, 512, D], F32, kind="ExternalInput").ap()
    wpb_d = dt("w_proj_b", [nl, 512, D], F32, kind="ExternalInput").ap()
    wout_d = dt("w_out", [nl, D, D], F32, kind="ExternalInput").ap()
    fng_d = dt("ffn_norm_g", [nl, D], F32, kind="ExternalInput").ap()
    wg_d = dt("w_gate", [nl, D, FF], F32, kind="ExternalInput").ap()
    wu_d = dt("w_up", [nl, D, FF], F32, kind="ExternalInput").ap()
    wd_d = dt("w_down", [nl, FF, D], F32, kind="ExternalInput").ap()
    ident_d = dt("ident", [128, 128], F32, kind="ExternalInput").ap()
    mask_d = dt("mask_a", [128, 1024], F32, kind="ExternalInput").ap()
    invf_d = dt("invf", [128, 32], F32, kind="ExternalInput").ap()
    out_d = dt("out", [S, D], F32, kind="ExternalOutput").ap()

    with ExitStack() as st:
        R = Rec(nc, st)
        sbt = lambda n, s, d: st.enter_context(nc.sbuf_tensor(n, s, d))
        x_sb = sbt("x_sb", [128, NT, D], F32)
        AR_N = 47104
        arena = sbt("arena", [128, AR_N], BF16)
        slots = [sbt("slot%d" % i, [128, 4096], BF16) for i in range(3)]
        scr = [sbt("scr%d" % i, [128, 512], F32) for i in range(6)]
        scrb = [s_.bitcast(BF16) for s_ in scr]
        g_rep = sbt("g_rep", [128, D], F32)
        ident = sbt("ident_sb", [128, 128], BF16)
        maskA = sbt("maskA_sb", [128, 1024], BF16)
        cos_t = sbt("cos_t", [128, NT, 32], F32)
        sin_t = sbt("sin_t", [128, NT, 32], F32)
        gains = sbt("gains", [128, 4, 64], F32)
        esink = sbt("esink", [128, 8], F32)
        stats = sbt("stats", [128, 16, 16], F32)
        pos_i = sbt("pos_i", [128, NT], I32)
        ps = [st.enter_context(nc.psum_tensor("ps%d" % i, [128, 512], F32)) for i in range(8)]
        psb = [p_.bitcast(BF16) for p_ in ps]

        HT0 = 0
        QT0 = 16384
        KT0 = QT0 + 8192
        V0 = KT0 + 2048
        AA0 = V0 + 4096
        AB0 = AA0 + 8192
        MX0 = QT0
        UT0 = QT0
        assert AB0 + 8192 == AR_N

        def AV(lo, n):
            return arena[:, lo:lo + n]

        def Ar(lo, n):
            return ("A", lo, lo + n)

        def X(t):
            return ("X", t, t + 1)

        def PB(b):
            return ("P", b, b + 1)

        def SL(s_):
            return ("W", s_, s_ + 1)

        def SC(i):
            return ("S", i, i + 1)

        r = R.res
        state = {"bank": 0, "scr": 0, "stat": 0, "slot": 0}

        def nbank():
            b = state["bank"]
            state["bank"] = (b + 1) % 7
            return b

        def nscr():
            i = state["scr"]
            state["scr"] = (i + 1) % 6
            return i

        def nstat():
            i = state["stat"]
            state["stat"] = (i + 1) % 16
            return i

        def nslot():
            i = state["slot"]
            state["slot"] = (i + 1) % 3
            return i

        def ST(i):
            return ("T", i, i + 1)

        def MM(out, lhsT, rhs, start, stop, reads, writes, sgc=False):
            if sgc:
                R.add("pe", lambda e: e.matmul(out, lhsT=lhsT, rhs=rhs, start=start, stop=stop, skip_group_check=True),
                      reads, writes)
            else:
                R.add("pe", lambda e: e.matmul(out, lhsT=lhsT, rhs=rhs, start=start, stop=stop), reads, writes)

        def TR(out, in_, idn, reads, writes):
            R.add("pe", lambda e: e.transpose(out=out, in_=in_, identity=idn), reads + [r("ident")], writes)

        def ACTF(out, in_, func, reads, writes, scale=None, accum=None):
            kw = {}
            if scale is not None:
                kw["scale"] = scale
            if accum is not None:
                kw["accum_out"] = accum
            R.add("act", lambda e: e.activation(out=out, in_=in_, func=func, **kw), reads, writes)

        def TT(out, in0, in1, op, reads, writes):
            R.add("dve", lambda e: e.tensor_tensor(out=out, in0=in0, in1=in1, op=op), reads, writes)

        def TS(out, in0, s1, s2, op0, op1, reads, writes):
            if s2 is None:
                R.add("dve", lambda e: e.tensor_scalar(out=out, in0=in0, scalar1=s1, scalar2=None, op0=op0), reads, writes)
            else:
                R.add("dve", lambda e: e.tensor_scalar(out=out, in0=in0, scalar1=s1, scalar2=s2, op0=op0, op1=op1), reads, writes)

        def STT(out, in0, scalar, in1, op0, op1, reads, writes):
            R.add("dve", lambda e: e.scalar_tensor_tensor(out=out, in0=in0, scalar=scalar, in1=in1, op0=op0, op1=op1), reads, writes)

        def CP(eng, out, in_, reads, writes):
            if eng == "act":
                R.add("act", lambda e: e.copy(out=out, in_=in_), reads, writes)
            else:
                R.add("dve", lambda e: e.tensor_copy(out=out, in_=in_), reads, writes)

        def DMA(eng, out, in_, reads, writes, key):
            R.add(eng, lambda e: e.dma_start(out=out, in_=in_), reads, writes, dma=key)

        def MEMSET(ap, val, writes):
            R.add("dve", lambda e: e.memset(ap, val), (), writes)

        def RECIP(out, in_, reads, writes):
            R.add("dve", lambda e: e.reciprocal(out=out, in_=in_), reads, writes)

        def RSUM(out, in_, reads, writes):
            R.add("dve", lambda e: e.reduce_sum(out=out, in_=in_, axis=AX.X), reads, writes)

        for t in range(NT):
            DMA("sp", x_sb[:, t, :], x_d[t * 128:(t + 1) * 128, :], [], [X(t)], "x%d" % t)
        DMA("pool", ident[:], ident_d, [], [r("ident")], "c_ident")
        DMA("pool", maskA[:], mask_d, [], [r("maskA")], "c_mask")
        DMA("sp", pos_i[:], pos_d, [], [r("pos_i")], "c_pos")
        invf = scr[5][:, 0:32]
        DMA("sp", invf, invf_d, [], [SC(5)], "c_invf")
        posf = stats[:, 15, :]
        CP("dve", posf, pos_i[:], [r("pos_i")], [ST(15)])
        TWO_PI = 2.0 * np.pi
        kf = scr[4][:, 0:512].rearrange("p (a b) -> p a b", a=NT)
        ki = scr[3].bitcast(I32)[:, 0:512].rearrange("p (a b) -> p a b", a=NT)
        for (tab, shift, nm) in ((sin_t, 0.0, "tab_s"), (cos_t, 0.5 * np.pi, "tab_c")):
            TT(tab[:], bc(posf.unsqueeze(2), [128, NT, 32]), bc(invf.unsqueeze(1), [128, NT, 32]), ALU.mult,
               [ST(15), SC(5)], [r(nm)])
            if shift != 0.0:
                TS(tab[:], tab[:], float(shift), None, ALU.add, None, [r(nm)], [r(nm)])
            TS(kf, tab[:], float(1.0 / TWO_PI), None, ALU.mult, None, [r(nm)], [SC(4)])
            CP("dve", ki, kf, [SC(4)], [SC(3)])
            CP("dve", kf, ki, [SC(3)], [SC(4)])
            STT(tab[:], kf, float(-TWO_PI), tab[:], ALU.mult, ALU.add, [SC(4), r(nm)], [r(nm)])
            TS(tab[:], tab[:], float(np.pi), float(-np.pi), ALU.min, ALU.max, [r(nm)], [r(nm)])
            ACTF(tab[:], tab[:], AF.Sin, [r(nm)], [r(nm)])
        TABS = [r("tab_s"), r("tab_c")]

        def load_slot(parts):
            s_ = nslot()
            for (dst_fn, src) in parts:
                DMA("pool", dst_fn(slots[s_]), src, [], [SL(s_)], "slot%d" % s_)
            return s_

        def rms_stats(ss_ap, out_ap, res_in, res_out, mult, add):
            TS(out_ap, ss_ap, float(mult), float(add), ALU.mult, ALU.add, [res_in], [res_out])
            ACTF(out_ap, out_ap, AF.Ln, [res_out], [res_out])
            ACTF(out_ap, out_ap, AF.Exp, [res_out], [res_out], scale=-0.5)

        def hT_ap(k, lo, n):
            return arena[:, HT0 + k * 2048 + lo: HT0 + k * 2048 + lo + n]

        def hT_res(lo, n):
            return [("A", HT0 + k * 2048 + lo, HT0 + k * 2048 + lo + n) for k in range(8)]

        def norm_phase(l, g_d):
            DMA("sp", g_rep[:], g_d[l].partition_broadcast(128), [], [r("g_rep")], "g_rep")
            si = nstat()
            so = nstat()
            for t in range(NT):
                j = nscr()
                ACTF(scrb[j][:, 0:1024], x_sb[:, t, :], AF.Square, [X(t)], [SC(j), ST(si)], accum=stats[:, si, t:t + 1])
            rms_stats(stats[:, si, :], stats[:, so, :], ST(si), ST(so), 1.0 / D, EPS)
            for t in range(NT):
                j = nscr()
                STT(scrb[j][:, 0:1024], x_sb[:, t, :], stats[:, so, t:t + 1], g_rep[:], ALU.mult, ALU.mult,
                    [X(t), ST(so), r("g_rep")], [SC(j)])
                b = nbank()
                for c in range(8):
                    TR(psb[b][:, c * 128:(c + 1) * 128], scrb[j][:, c * 128:(c + 1) * 128], ident[:], [SC(j)], [PB(b)])
                dst = AV(HT0, 16384).rearrange("p (c s) -> p c s", c=8)[:, :, t * 128:(t + 1) * 128]
                src = psb[b][:, 0:1024].rearrange("p (c s) -> p c s", c=8)
                CP("act" if t % 2 == 0 else "dve", dst, src, [PB(b)], hT_res(t * 128, 128))

        def z_matmuls(b, s_, tok_lo, ncols):
            for k in range(8):
                MM(ps[b][:, 0:ncols], hT_ap(k, tok_lo, 128), slots[s_][:, k * 512: k * 512 + ncols],
                   (k == 0), (k == 7), hT_res(tok_lo, 128) + [SL(s_)], [PB(b)])

        def qk_norm(b, col_lo, nh, gain_idx, t, rope, mult, add, out_ap, out_res):
            n = nh * 64
            zin = ps[b][:, col_lo:col_lo + n]
            j1 = nscr()
            si = nstat()
            so = nstat()
            ACTF(scr[j1][:, 0:n], zin, AF.Square, [PB(b)], [SC(j1)])
            RSUM(stats[:, si, 0:nh], scr[j1][:, 0:n].rearrange("p (h d) -> p h d", h=nh), [SC(j1)], [ST(si)])
            rms_stats(stats[:, si, 0:nh], stats[:, so, 0:nh], ST(si), ST(so), mult, add)
            j2 = nscr()
            t1 = scr[j2][:, 0:n].rearrange("p (h d) -> p h d", h=nh)
            TT(t1, zin.rearrange("p (h d) -> p h d", h=nh), bc(stats[:, so, 0:nh].unsqueeze(2), [128, nh, 64]), ALU.mult,
               [PB(b), ST(so)], [SC(j2)])
            gb = bc(gains[:, gain_idx, :].unsqueeze(1), [128, nh, 64])
            o3 = out_ap.rearrange("p (h d) -> p h d", h=nh)
            if not rope:
                TT(o3, t1, gb, ALU.mult, [SC(j2), r("gains")], out_res)
                return
            TT(t1, t1, gb, ALU.mult, [SC(j2), r("gains")], [SC(j2)])
            j3 = nscr()
            tmp = scr[j3][:, 0:n].rearrange("p (h d) -> p h d", h=nh)
            x1 = t1[:, :, 0:32]
            x2 = t1[:, :, 32:64]
            cb = bc(cos_t[:, t, :].unsqueeze(1), [128, nh, 32])
            sb_ = bc(sin_t[:, t, :].unsqueeze(1), [128, nh, 32])
            TT(tmp[:, :, 0:32], x1, cb, ALU.mult, [SC(j2)] + TABS, [SC(j3)])
            TT(tmp[:, :, 32:64], x2, sb_, ALU.mult, [SC(j2)] + TABS, [SC(j3)])
            TT(o3[:, :, 0:32], tmp[:, :, 0:32], tmp[:, :, 32:64], ALU.subtract, [SC(j3)], out_res)
            j4 = nscr()
            tmp2 = scr[j4][:, 0:n].rearrange("p (h d) -> p h d", h=nh)
            TT(tmp2[:, :, 0:32], x2, cb, ALU.mult, [SC(j2)] + TABS, [SC(j4)])
            TT(tmp2[:, :, 32:64], x1, sb_, ALU.mult, [SC(j2)] + TABS, [SC(j4)])
            TT(o3[:, :, 32:64], tmp2[:, :, 0:32], tmp2[:, :, 32:64], ALU.add, [SC(j4)], out_res)

        def transpose_to(src_aps, src_res, dst_ap, dst_res, evac_eng):
            b = nbank()
            for i, sap in enumerate(src_aps):
                TR(psb[b][:, i * 128:(i + 1) * 128], sap, ident[:], src_res, [PB(b)])
            CP(evac_eng, dst_ap, psb[b][:, 0:len(src_aps) * 128], [PB(b)], dst_res)

        for l in range(nl):
            for gi, gd in enumerate((qna_d, kna_d, qnb_d, knb_d)):
                DMA("sp", gains[:, gi, :], gd[l].partition_broadcast(128), [], [r("gains")], "gains")
            DMA("sp", esink[:], snk_d[l].partition_broadcast(128), [], [r("esink")], "esink")
            ACTF(esink[:], esink[:], AF.Exp, [r("esink")], [r("esink")])

            norm_phase(l, ang_d)

            VA = AV(V0, 2080).rearrange("p (t k d) -> p t k d", t=NT, k=2)
            MEMSET(VA[:, :, :, 64:65], 1.0, [Ar(V0, 2080)])
            s_q = load_slot([(lambda sl: sl[:].rearrange("p (k n) -> p k n", k=8),
                              win_d[l, :, 0:512].rearrange("(k p) n -> p k n", p=128))])
            s_kv = load_slot([(lambda sl: sl[:].rearrange("p (k n) -> p k n", k=8)[:, :, 0:256],
                               win_d[l, :, 512:768].rearrange("(k p) n -> p k n", p=128))])
            for t in range(NT):
                b = nbank()
                z_matmuls(b, s_q, t * 128, 512)
                jq = nscr()
                qn = scrb[jq][:, 0:512]
                qk_norm(b, 0, 8, 0, t, True, 1.0, 64.0 * EPS, qn, [SC(jq)])
                transpose_to([qn[:, i * 128:(i + 1) * 128] for i in range(4)], [SC(jq)],
                             AV(QT0 + t * 512, 512), [Ar(QT0 + t * 512, 512)], "act")
            for t in range(NT):
                b = nbank()
                z_matmuls(b, s_kv, t * 128, 256)
                jk = nscr()
                kn = scrb[jk][:, 0:128]
                qk_norm(b, 0, 2, 1, t, True, 1.0 / 64, EPS, kn, [SC(jk)])
                transpose_to([kn], [SC(jk)], AV(KT0 + t * 128, 128), [Ar(KT0 + t * 128, 128)], "act")
                CP("act", VA[:, t, :, 0:64], ps[b][:, 128:256].rearrange("p (k d) -> p k d", k=2),
                   [PB(b)], [Ar(V0 + t * 130, 130)])

            for n in range(NT):
                for kv in range(2):
                    kbs = [kb for kb in (n - 1, n, n + 1) if 0 <= kb < NT]
                    pts = []
                    for kb in kbs:
                        b = nbank()
                        MM(ps[b][:, 0:512], arena[kv * 64:(kv + 1) * 64, KT0 + kb * 128: KT0 + (kb + 1) * 128],
                           arena[kv * 64:(kv + 1) * 64, QT0 + n * 512: QT0 + (n + 1) * 512], True, (kb == n),
                           [Ar(KT0 + kb * 128, 128), Ar(QT0 + n * 512, 512)], [PB(b)], sgc=True)
                        if kb != n:
                            w = 0 if kb < n else 1
                            MM(ps[b][:, 0:512], ident[:], maskA[:, w * 512:(w + 1) * 512], False, True,
                               [r("ident"), r("maskA")], [PB(b)], sgc=True)
                        jp = nscr()
                        ACTF(scrb[jp][:, 0:512], ps[b][:, 0:512], AF.Exp, [PB(b)], [SC(jp)])
                        pts.append((kb, jp))
                    bo = nbank()
                    for g in range(4):
                        for idx, (kb, jp) in enumerate(pts):
                            MM(ps[bo][:, g * 65:(g + 1) * 65], scrb[jp][:, g * 128:(g + 1) * 128], VA[:, kb, kv, :],
                               (idx == 0), (idx == len(pts) - 1), [SC(jp), Ar(V0 + kb * 130, 130)], [PB(bo)], sgc=True)
                    pv = ps[bo][:, 0:260].rearrange("p (g d) -> p g d", g=4)
                    sd = nstat()
                    TT(stats[:, sd, 0:4].unsqueeze(2), pv[:, :, 64:65], esink[:, kv * 4:(kv + 1) * 4].unsqueeze(2), ALU.add,
                       [PB(bo), r("esink")], [ST(sd)])
                    RECIP(stats[:, sd, 0:4], stats[:, sd, 0:4], [ST(sd)], [ST(sd)])
                    ja = nscr()
                    at = scrb[ja][:, 0:256]
                    TT(at.rearrange("p (g d) -> p g d", g=4), pv[:, :, 0:64], bc(stats[:, sd, 0:4].unsqueeze(2), [128, 4, 64]),
                       ALU.mult, [PB(bo), ST(sd)], [SC(ja)])
                    dst = AV(AA0, 8192).rearrange("p (c s) -> p c s", c=4)[:, kv * 2:kv * 2 + 2, n * 128:(n + 1) * 128]
                    dres = [("A", AA0 + (kv * 2 + c_) * 2048 + n * 128, AA0 + (kv * 2 + c_) * 2048 + (n + 1) * 128) for c_ in range(2)]
                    bt = nbank()
                    for i in range(2):
                        TR(psb[bt][:, i * 128:(i + 1) * 128], at[:, i * 128:(i + 1) * 128], ident[:], [SC(ja)], [PB(bt)])
                    CP("act", dst, psb[bt][:, 0:256].rearrange("p (c s) -> p c s", c=2), [PB(bt)], dres)

            VB = AV(V0, 2080).rearrange("p (t k d) -> p t k d", t=NT, k=2)
            VS = AV(V0 + 2080, 1950).rearrange("p (t k d) -> p t k d", t=15, k=2)
            for hp in range(4):
                MEMSET(VB[:, :, :, 64:65], 1.0, [Ar(V0, 2080)])
                MEMSET(VS[:, :, :, 64:65], 1.0, [Ar(V0 + 2080, 1950)])
                s_b = load_slot([
                    (lambda sl: sl[:].rearrange("p (k n) -> p k n", k=8)[:, :, 0:128],
                     win_d[l, :, 768 + hp * 128: 768 + (hp + 1) * 128].rearrange("(k p) n -> p k n", p=128)),
                    (lambda sl: sl[:].rearrange("p (k n) -> p k n", k=8)[:, :, 128:256],
                     win_d[l, :, 1280 + hp * 128: 1280 + (hp + 1) * 128].rearrange("(k p) n -> p k n", p=128)),
                    (lambda sl: sl[:].rearrange("p (k n) -> p k n", k=8)[:, :, 256:384],
                     win_d[l, :, 1792 + hp * 128: 1792 + (hp + 1) * 128].rearrange("(k p) n -> p k n", p=128)),
                ])
                s_bias = load_slot([(lambda sl: sl[:], bias_d[l, hp])])
                for t in range(NT):
                    b = nbank()
                    z_matmuls(b, s_b, t * 128, 384)
                    jq = nscr()
                    qkn = scrb[jq][:, 0:256]
                    qk_norm(b, 0, 2, 2, t, False, 1.0, 64.0 * EPS, qkn[:, 0:128], [SC(jq)])
                    qk_norm(b, 128, 2, 3, t, False, 1.0 / 64, EPS, qkn[:, 128:256], [SC(jq)])
                    transpose_to([qkn[:, 0:128]], [SC(jq)], AV(QT0 + t * 128, 128), [Ar(QT0 + t * 128, 128)], "act")
                    transpose_to([qkn[:, 128:256]], [SC(jq)], AV(KT0 + t * 128, 128), [Ar(KT0 + t * 128, 128)], "act")
                    CP("act", VB[:, t, :, 0:64], ps[b][:, 256:384].rearrange("p (k d) -> p k d", k=2),
                       [PB(b)], [Ar(V0 + t * 130, 130)])
                for t in range(15):
                    b = nbank()
                    for k in range(8):
                        MM(ps[b][:, 0:128], hT_ap(k, 64 + t * 128, 128), slots[s_b][:, k * 512 + 256: k * 512 + 384],
                           (k == 0), (k == 7), hT_res(64 + t * 128, 128) + [SL(s_b)], [PB(b)])
                    CP("act", VS[:, t, :, 0:64], ps[b][:, 0:128].rearrange("p (k d) -> p k d", k=2),
                       [PB(b)], [Ar(V0 + 2080 + t * 130, 130)])
                for rg in range(4):
                    btr = 7
                    for rr in range(8):
                        rw = rg * 8 + rr
                        rs = min(max(rw - 4, 0), 24)
                        j0 = rs - rw + 7
                        b = nbank()
                        for hh in range(2):
                            for i in range(4):
                                a = rs + 2 * i
                                MM(ps[b][:, hh * 256 + i * 64: hh * 256 + (i + 1) * 64],
                                   arena[hh * 64:(hh + 1) * 64, KT0 + a * 64: KT0 + a * 64 + 128],
                                   arena[hh * 64:(hh + 1) * 64, QT0 + rw * 64: QT0 + (rw + 1) * 64], (hh == 0 and i == 0), False,
                                   [Ar(KT0 + a * 64, 128), Ar(QT0 + rw * 64, 64)], [PB(b)], sgc=True)
                            boff = hh * 2048 + j0 * 256
                            MM(ps[b][:, hh * 256:(hh + 1) * 256], ident[:], slots[s_bias][:, boff:boff + 256], False, True,
                               [r("ident"), SL(s_bias)], [PB(b)], sgc=True)
                        jp = nscr()
                        ACTF(scrb[jp][:, 0:512], ps[b][:, 0:512], AF.Exp, [PB(b)], [SC(jp)])
                        bo = nbank()
                        for hh in range(2):
                            for i in range(4):
                                a = rs + 2 * i
                                if a % 2 == 0:
                                    vap = VB[:, a // 2, hh, :]
                                    vres = Ar(V0 + (a // 2) * 130, 130)
                                else:
                                    vap = VS[:, (a - 1) // 2, hh, :]
                                    vres = Ar(V0 + 2080 + ((a - 1) // 2) * 130, 130)
                                MM(ps[bo][0:64, hh * 65:(hh + 1) * 65], scrb[jp][:, hh * 256 + i * 64: hh * 256 + (i + 1) * 64], vap,
                                   (i == 0), (i == 3), [SC(jp), vres], [PB(bo)], sgc=True)
                        pv = ps[bo][0:64, 0:130].rearrange("p (g d) -> p g d", g=2)
                        sd = nstat()
                        RECIP(stats[0:64, sd, 0:2].unsqueeze(2), pv[:, :, 64:65], [PB(bo)], [ST(sd)])
                        ja = nscr()
                        at = scrb[ja][0:64, 0:128]
                        TT(at.rearrange("p (g d) -> p g d", g=2), pv[:, :, 0:64], bc(stats[0:64, sd, 0:2].unsqueeze(2), [64, 2, 64]),
                           ALU.mult, [PB(bo), ST(sd)], [SC(ja)])
                        TR(psb[btr][:, rr * 64:(rr + 1) * 64], at, ident[0:64, 0:64], [SC(ja)], [PB(btr)])
                    dlo = AB0 + hp * 2048 + rg * 512
                    CP("act", AV(dlo, 512), psb[btr][:, 0:512], [PB(btr)], [Ar(dlo, 512)])

            for c in range(8):
                s_w = load_slot([
                    (lambda sl: sl[:, 0:1024].rearrange("p (k n) -> p k n", k=8),
                     win_d[l, :, 2304 + c * 128: 2304 + (c + 1) * 128].rearrange("(k p) n -> p k n", p=128)),
                    (lambda sl: sl[:, 1024:2048].rearrange("p (k n) -> p k n", k=8),
                     win_d[l, :, 3328 + c * 128: 3328 + (c + 1) * 128].rearrange("(k p) n -> p k n", p=128)),
                    (lambda sl: sl[:, 2048:2560].rearrange("p (k n) -> p k n", k=4),
                     wpa_d[l, :, c * 128:(c + 1) * 128].rearrange("(k p) n -> p k n", p=128)),
                    (lambda sl: sl[:, 2560:3072].rearrange("p (k n) -> p k n", k=4),
                     wpb_d[l, :, c * 128:(c + 1) * 128].rearrange("(k p) n -> p k n", p=128)),
                ])
                for tg in range(4):
                    bga, bgb, bya, byb = nbank(), nbank(), nbank(), nbank()
                    for (bb, woff) in ((bga, 0), (bgb, 1024)):
                        for k in range(8):
                            MM(ps[bb][:, 0:512], slots[s_w][:, woff + k * 128: woff + (k + 1) * 128], hT_ap(k, tg * 512, 512),
                               (k == 0), (k == 7), hT_res(tg * 512, 512) + [SL(s_w)], [PB(bb)])
                    for (bb, woff, a0) in ((bya, 2048, AA0), (byb, 2560, AB0)):
                        for k in range(4):
                            lo = a0 + k * 2048 + tg * 512
                            MM(ps[bb][:, 0:512], slots[s_w][:, woff + k * 128: woff + (k + 1) * 128], arena[:, lo:lo + 512],
                               (k == 0), (k == 3), [("A", lo, lo + 512), SL(s_w)], [PB(bb)])
                    j1, j2 = nscr(), nscr()
                    ACTF(scr[j1][:, 0:512], ps[bga][:, 0:512], AF.Sigmoid, [PB(bga)], [SC(j1)])
                    ACTF(scr[j2][:, 0:512], ps[bgb][:, 0:512], AF.Sigmoid, [PB(bgb)], [SC(j2)])
                    TT(scr[j1][:, 0:512], scr[j1][:, 0:512], ps[bya][:, 0:512], ALU.mult, [SC(j1), PB(bya)], [SC(j1)])
                    TT(scr[j2][:, 0:512], scr[j2][:, 0:512], ps[byb][:, 0:512], ALU.mult, [SC(j2), PB(byb)], [SC(j2)])
                    mlo = MX0 + c * 2048 + tg * 512
                    TT(AV(mlo, 512), scr[j1][:, 0:512], scr[j2][:, 0:512], ALU.add, [SC(j1), SC(j2)], [Ar(mlo, 512)])

            for hf in range(2):
                s_o = load_slot([(lambda sl: sl[:].rearrange("p (k n) -> p k n", k=8),
                                  wout_d[l, :, hf * 512:(hf + 1) * 512].rearrange("(k p) n -> p k n", p=128))])
                for t in range(NT):
                    b = nbank()
                    for k in range(8):
                        lo = MX0 + k * 2048 + t * 128
                        MM(ps[b][:, 0:512], arena[:, lo:lo + 128], slots[s_o][:, k * 512:(k + 1) * 512],
                           (k == 0), (k == 7), [("A", lo, lo + 128), SL(s_o)], [PB(b)])
                    xs = x_sb[:, t, hf * 512:(hf + 1) * 512]
                    TT(xs, xs, ps[b][:, 0:512], ALU.add, [X(t), PB(b)], [X(t)])

            if dbg == 'attn':
                dbg_d = dt('dbg_ab', [128, 16384], BF16, kind='ExternalOutput').ap()
                DMA('sp', dbg_d, AV(AA0, 16384), [Ar(AA0, 16384)], [], 'dbgout')
                continue
            norm_phase(l, fng_d)

            f0 = 0
            for fgn in (8, 8, 6):
                for fp in range(fgn // 2):
                    fa = f0 + fp * 2
                    s_gu = load_slot([
                        (lambda sl: sl[:, 0:2048].rearrange("p (k n) -> p k n", k=8),
                         wg_d[l, :, fa * 128:(fa + 2) * 128].rearrange("(k p) n -> p k n", p=128)),
                        (lambda sl: sl[:, 2048:4096].rearrange("p (k n) -> p k n", k=8),
                         wu_d[l, :, fa * 128:(fa + 2) * 128].rearrange("(k p) n -> p k n", p=128)),
                    ])
                    for q2 in range(2):
                        fl = fp * 2 + q2
                        for tg in range(4):
                            bg, bu = nbank(), nbank()
                            for (bb, woff) in ((bg, 0), (bu, 2048)):
                                for k in range(8):
                                    wlo = woff + k * 256 + q2 * 128
                                    MM(ps[bb][:, 0:512], slots[s_gu][:, wlo:wlo + 128], hT_ap(k, tg * 512, 512),
                                       (k == 0), (k == 7), hT_res(tg * 512, 512) + [SL(s_gu)], [PB(bb)])
                            j1 = nscr()
                            ACTF(scr[j1][:, 0:512], ps[bg][:, 0:512], AF.Silu, [PB(bg)], [SC(j1)])
                            ulo = UT0 + fl * 2048 + tg * 512
                            TT(AV(ulo, 512), scr[j1][:, 0:512], ps[bu][:, 0:512], ALU.mult, [SC(j1), PB(bu)], [Ar(ulo, 512)])
                dsl = []
                for j in range(0, fgn, 4):
                    nch = min(4, fgn - j)
                    s_d = load_slot([(lambda sl, nch=nch: sl[:, 0:nch * 1024].rearrange("p (k n) -> p k n", k=nch),
                                      wd_d[l, (f0 + j) * 128:(f0 + j + nch) * 128, :].rearrange("(k p) n -> p k n", p=128))])
                    dsl.append(s_d)
                for t in range(NT):
                    for hf in range(2):
                        b = nbank()
                        for j in range(fgn):
                            s_d = dsl[j // 4]
                            lo = UT0 + j * 2048 + t * 128
                            wlo = (j % 4) * 1024 + hf * 512
                            MM(ps[b][:, 0:512], arena[:, lo:lo + 128], slots[s_d][:, wlo:wlo + 512],
                               (j == 0), (j == fgn - 1), [("A", lo, lo + 128), SL(s_d)], [PB(b)])
                        xs = x_sb[:, t, hf * 512:(hf + 1) * 512]
                        TT(xs, xs, ps[b][:, 0:512], ALU.add, [X(t), PB(b)], [X(t)])
                f0 += fgn

        for t in range(NT):
            DMA("sp", out_d[t * 128:(t + 1) * 128, :], x_sb[:, t, :], [X(t)], [], "out")
        R.emit(final_waits=[(R.dmasems["out"], R.dmacnt["out"])])
    return nc


def _consts():
    ident = np.eye(128, dtype=np.float32)
    j = np.arange(128)[:, None]
    rq = np.arange(128)[None, :]
    m_prev = np.where(j >= rq, 0.0, NEG).astype(np.float32)
    m_next = np.where(j <= rq, 0.0, NEG).astype(np.float32)
    mask = np.concatenate([np.tile(m_prev, (1, 4)), np.tile(m_next, (1, 4))], axis=1)
    half = 32
    inv = (10000.0 ** (-np.arange(half, dtype=np.float32) / np.float32(half))).astype(np.float32)
    invf = np.tile(inv[None, :], (128, 1)).astype(np.float32)
    return ident, np.ascontiguousarray(mask), invf


def _bias_tiles(rel_bias):
    L = rel_bias.shape[0]
    o = np.arange(2)
    j0 = np.arange(8)
    i = np.arange(4)
    DR = np.minimum(j0[None, :, None] + 2 * i[None, None, :] + o[:, None, None], 14)
    kc = np.arange(64)[:, None]
    qc = np.arange(64)[None, :]
    DC = np.clip(kc - qc + 15, 0, 30)
    cs = np.clip(qc - 8, 0, 48)
    valid = (kc >= cs) & (kc < cs + 16)
    g = rel_bias[:, :, DR[:, None, :, :, None], DC[None, :, None, None, :]]
    g = np.where(valid[None, None, None, :, None, None, :], g, np.float32(NEG)).astype(np.float32)
    g = g.reshape(L, 4, 2, 128, 8, 4, 64).transpose(0, 1, 3, 2, 4, 5, 6)
    return np.ascontiguousarray(g.reshape(L, 4, 128, 4096))


def _perm_w_in(w_in):
    L = w_in.shape[0]
    qa = w_in[:, :, 0:512].reshape(L, D, 2, 4, 64).transpose(0, 1, 3, 2, 4).reshape(L, D, 512)
    return np.ascontiguousarray(np.concatenate([qa, w_in[:, :, 512:]], axis=2))


_NC_CACHE = {}


def _get_nc(nl):
    if nl not in _NC_CACHE:
        _NC_CACHE[nl] = build(nl)
    return _NC_CACHE[nl]


def kernel(x, positions, attn_norm_g, w_in, q_norm_a, k_norm_a, sink_a, q_norm_b, k_norm_b,
           rel_bias_b, w_proj_a, w_proj_b, w_out, ffn_norm_g, w_gate, w_up, w_down):
    f = lambda a: np.ascontiguousarray(np.asarray(a, dtype=np.float32))
    x = f(x)
    pos = np.ascontiguousarray(np.asarray(positions, dtype=np.int32).reshape(NT, 128).T)
    ident, mask, invf = _consts()
    w_in_p = _perm_w_in(f(w_in))
    bias_t = _bias_tiles(f(rel_bias_b))
    per_layer = {
        "attn_norm_g": f(attn_norm_g), "w_in": w_in_p, "q_norm_a": f(q_norm_a), "k_norm_a": f(k_norm_a),
        "sink_a": f(sink_a), "q_norm_b": f(q_norm_b), "k_norm_b": f(k_norm_b), "bias_t": bias_t,
        "w_proj_a": f(w_proj_a), "w_proj_b": f(w_proj_b), "w_out": f(w_out), "ffn_norm_g": f(ffn_norm_g),
        "w_gate": f(w_gate), "w_up": f(w_up), "w_down": f(w_down),
    }
    consts = {"pos": pos, "ident": ident, "mask_a": mask, "invf": invf}
    B = x.shape[0]
    if FUSED:
        nc = _get_nc(DEPTH)
        in_maps = []
        for b in range(B):
            m = {"x": x[b]}
            m.update(consts)
            m.update(per_layer)
            in_maps.append(m)
        res = run_bass_kernel_spmd(nc, in_maps, core_ids=list(range(B)))
        return np.stack([res.results[b]["out"] for b in range(B)], axis=0)
    nc = _get_nc(1)
    cur = [x[b] for b in range(B)]
    for l in range(DEPTH):
        in_maps = []
        for b in range(B):
            m = {"x": cur[b]}
            m.update(consts)
            m.update({k: np.ascontiguousarray(v[l:l + 1]) for k, v in per_layer.items()})
            in_maps.append(m)
        res = run_bass_kernel_spmd(nc, in_maps, core_ids=list(range(B)))
        cur = [np.ascontiguousarray(res.results[b]["out"]) for b in range(B)]
    return np.stack(cur, axis=0)
```

```python
import numpy as np
from contextlib import ExitStack
import concourse.bass as bass
import concourse.mybir as mybir
from concourse.bass_utils import run_bass_kernel_spmd

F32 = mybir.dt.float32
BF16 = mybir.dt.bfloat16
I32 = mybir.dt.int32
ALU = mybir.AluOpType
AF = mybir.ActivationFunctionType
AX = mybir.AxisListType

S = 2048
D = 1024
NT = 16
FF = 2816
NFC = 22
INC = 4352
NEG = -30000.0
EPS = 1e-6
DEPTH = 4
FUSED = True

ENGS = ["pe", "act", "dve", "pool", "sp"]


class Op:
    __slots__ = ("eng", "fn", "deps", "marked", "count", "sem", "is_dma", "idx")

    def __init__(self, eng, fn):
        self.eng = eng
        self.fn = fn
        self.deps = []
        self.marked = False
        self.count = 0
        self.sem = None
        self.is_dma = False


class Rec:
    def __init__(self, nc, stack):
        self.nc = nc
        self.stack = stack
        self.ops = {e: [] for e in ENGS}
        self.hist = {}
        self.names = {}
        self.engsem = {e: stack.enter_context(nc.semaphore("s_" + e)) for e in ENGS}
        self.dmasems = {}
        self.dmacnt = {}
        self.nops = 0

    def res(self, name):
        if name not in self.names:
            self.names[name] = len(self.names)
        i = self.names[name]
        return ("N", i, i + 1)

    def _sem_for(self, key):
        if key not in self.dmasems:
            self.dmasems[key] = self.stack.enter_context(self.nc.semaphore("d_%d" % len(self.dmasems)))
            self.dmacnt[key] = 0
        return self.dmasems[key]

    def add(self, eng, fn, reads=(), writes=(), dma=None):
        op = Op(eng, fn)
        op.idx = self.nops
        self.nops += 1
        deps = {}
        for (sp, lo, hi) in reads:
            for rec in self.hist.get(sp, ()):
                if rec[3] and rec[0] < hi and lo < rec[1]:
                    deps[id(rec[2])] = rec[2]
        for (sp, lo, hi) in writes:
            for rec in self.hist.get(sp, ()):
                if rec[0] < hi and lo < rec[1]:
                    deps[id(rec[2])] = rec[2]
        for d in deps.values():
            if d.eng == "pe" and eng == "pe":
                continue
            d.marked = True
            op.deps.append(d)
        if dma is not None:
            op.is_dma = True
            op.sem = self._sem_for(dma)
            self.dmacnt[dma] += 16
            op.count = self.dmacnt[dma]
        for (sp, lo, hi) in writes:
            h = self.hist.setdefault(sp, [])
            h[:] = [r for r in h if not (lo <= r[0] and r[1] <= hi)]
            h.append([lo, hi, op, True])
        for (sp, lo, hi) in reads:
            h = self.hist.setdefault(sp, [])
            if not op.is_dma:
                h[:] = [r for r in h if not ((not r[3]) and r[0] == lo and r[1] == hi
                                             and r[2].eng == eng and not r[2].is_dma)]
            h.append([lo, hi, op, False])
        self.ops[eng].append(op)
        return op

    def emit(self, final_waits=()):
        nc = self.nc
        for e in ENGS:
            c = 0
            for op in self.ops[e]:
                if op.is_dma:
                    continue
                if op.marked:
                    c += 1
                    op.count = c
                    op.sem = self.engsem[e]
        ops = self.ops
        engsem = self.engsem

        def make_body(e):
            def body(eng):
                waited = {}
                for op in ops[e]:
                    need = {}
                    for d in op.deps:
                        k = id(d.sem)
                        if k not in need or need[k][1] < d.count:
                            need[k] = (d.sem, d.count)
                    for k, (sem, val) in need.items():
                        if waited.get(k, 0) >= val:
                            continue
                        eng.wait_ge(sem, val)
                        waited[k] = val
                    ins = op.fn(eng)
                    if op.is_dma:
                        ins.then_inc(op.sem, 16)
                    elif op.marked:
                        ins.then_inc(engsem[e], 1)
                if e == "sp":
                    for (sem, val) in final_waits:
                        eng.wait_ge(sem, val)
            return body

        with nc.Block() as block:
            block.tensor(make_body("pe"))
            block.scalar(make_body("act"))
            block.vector(make_body("dve"))
            block.gpsimd(make_body("pool"))
            block.sync(make_body("sp"))


def bc(ap, shape):
    return ap.to_broadcast(list(shape))


def build(nl, dbg=None):
    nc = bass.Bass("TRN2", target_bir_lowering=False)
    dt = nc.dram_tensor
    x_d = dt("x", [S, D], F32, kind="ExternalInput").ap()
    pos_d = dt("pos", [128, NT], I32, kind="ExternalInput").ap()
    ang_d = dt("attn_norm_g", [nl, D], F32, kind="ExternalInput").ap()
    win_d = dt("w_in", [nl, D, INC], F32, kind="ExternalInput").ap()
    qna_d = dt("q_norm_a", [nl, 64], F32, kind="ExternalInput").ap()
    kna_d = dt("k_norm_a", [nl, 64], F32, kind="ExternalInput").ap()
    snk_d = dt("sink_a", [nl, 8], F32, kind="ExternalInput").ap()
    qnb_d = dt("q_norm_b", [nl, 64], F32, kind="ExternalInput").ap()
    knb_d = dt("k_norm_b", [nl, 64], F32, kind="ExternalInput").ap()
    bias_d = dt("bias_t", [nl, 4, 128, 4096], F32, kind="ExternalInput").ap()
    wpa_d = dt("w_proj_a", [nl, 512, D], F32, kind="ExternalInput").ap()
    wpb_d = dt("w_proj_b", [nl, 512, D], F32, kind="ExternalInput").ap()
    wout_d = dt("w_out", [nl, D, D], F32, kind="ExternalInput").ap()
    fng_d = dt("ffn_norm_g", [nl, D], F32, kind="ExternalInput").ap()
    wg_d = dt("w_gate", [nl, D, FF], F32, kind="ExternalInput").ap()
    wu_d = dt("w_up", [nl, D, FF], F32, kind="ExternalInput").ap()
    wd_d = dt("w_down", [nl, FF, D], F32, kind="ExternalInput").ap()
    ident_d = dt("ident", [128, 128], F32, kind="ExternalInput").ap()
    mask_d = dt("mask_a", [128, 1024], F32, kind="ExternalInput").ap()
    invf_d = dt("invf", [128, 32], F32, kind="ExternalInput").ap()
    out_d = dt("out", [S, D], F32, kind="ExternalOutput").ap()

    with ExitStack() as st:
        R = Rec(nc, st)
        sbt = lambda n, s, d: st.enter_context(nc.sbuf_tensor(n, s, d))
        x_sb = sbt("x_sb", [128, NT, D], F32)
        AR_N = 47104
        arena = sbt("arena", [128, AR_N], BF16)
        slots = [sbt("slot%d" % i, [128, 4096], BF16) for i in range(3)]
        scr = [sbt("scr%d" % i, [128, 512], F32) for i in range(6)]
        scrb = [s_.bitcast(BF16) for s_ in scr]
        pts_ = [sbt("pt%d" % i, [128, 512], BF16) for i in range(3)]
        g_rep = sbt("g_rep", [128, D], F32)
        ident = sbt("ident_sb", [128, 128], BF16)
        maskA = sbt("maskA_sb", [128, 1024], BF16)
        cos_t = sbt("cos_t", [128, NT, 32], F32)
        sin_t = sbt("sin_t", [128, NT, 32], F32)
        gains = sbt("gains", [128, 4, 64], F32)
        esink = sbt("esink", [128, 8], F32)
        stats = sbt("stats", [128, 16, 16], F32)
        pos_i = sbt("pos_i", [128, NT], I32)
        ps = [st.enter_context(nc.psum_tensor("ps%d" % i, [128, 512], F32)) for i in range(8)]
        psb = [p_.bitcast(BF16) for p_ in ps]

        HT0 = 0
        QT0 = 16384
        KT0 = QT0 + 8192
        V0 = KT0 + 2048
        AA0 = V0 + 4096
        AB0 = AA0 + 8192
        MX0 = QT0
        UT0 = QT0
        assert AB0 + 8192 == AR_N

        def AV(lo, n):
            return arena[:, lo:lo + n]

        def Ar(lo, n):
            return ("A", lo, lo + n)

        def X(t):
            return ("X", t, t + 1)

        def PB(b):
            return ("P", b, b + 1)

        def SL(s_):
            return ("W", s_, s_ + 1)

        def SC(i):
            return ("S", i, i + 1)

        r = R.res
        state = {"bank": 0, "scr": 0, "stat": 0, "slot": 0, "pt": 0}

        def nbank():
            b = state["bank"]
            state["bank"] = (b + 1) % 7
            return b

        def nscr():
            i = state["scr"]
            state["scr"] = (i + 1) % 6
            return i

        def nstat():
            i = state["stat"]
            state["stat"] = (i + 1) % 16
            return i

        def nslot():
            i = state["slot"]
            state["slot"] = (i + 1) % 3
            return i

        def ST(i):
            return ("T", i, i + 1)

        def PT(i):
            return ("Q", i, i + 1)

        def npt():
            i = state["pt"]
            state["pt"] = (i + 1) % 3
            return i

        def MM(out, lhsT, rhs, start, stop, reads, writes, sgc=False):
            if sgc:
                R.add("pe", lambda e: e.matmul(out, lhsT=lhsT, rhs=rhs, start=start, stop=stop, skip_group_check=True),
                      reads, writes)
            else:
                R.add("pe", lambda e: e.matmul(out, lhsT=lhsT, rhs=rhs, start=start, stop=stop), reads, writes)

        def TR(out, in_, idn, reads, writes):
            R.add("pe", lambda e: e.transpose(out=out, in_=in_, identity=idn), reads + [r("ident")], writes)

        def ACTF(out, in_, func, reads, writes, scale=None, accum=None):
            kw = {}
            if scale is not None:
                kw["scale"] = scale
            if accum is not None:
                kw["accum_out"] = accum
            R.add("act", lambda e: e.activation(out=out, in_=in_, func=func, **kw), reads, writes)

        def TT(out, in0, in1, op, reads, writes):
            R.add("dve", lambda e: e.tensor_tensor(out=out, in0=in0, in1=in1, op=op), reads, writes)

        def TS(out, in0, s1, s2, op0, op1, reads, writes):
            if s2 is None:
                R.add("dve", lambda e: e.tensor_scalar(out=out, in0=in0, scalar1=s1, scalar2=None, op0=op0), reads, writes)
            else:
                R.add("dve", lambda e: e.tensor_scalar(out=out, in0=in0, scalar1=s1, scalar2=s2, op0=op0, op1=op1), reads, writes)

        def STT(out, in0, scalar, in1, op0, op1, reads, writes):
            R.add("dve", lambda e: e.scalar_tensor_tensor(out=out, in0=in0, scalar=scalar, in1=in1, op0=op0, op1=op1), reads, writes)

        def CP(eng, out, in_, reads, writes):
            if eng == "act":
                R.add("act", lambda e: e.copy(out=out, in_=in_), reads, writes)
            else:
                R.add("dve", lambda e: e.tensor_copy(out=out, in_=in_), reads, writes)

        def DMA(eng, out, in_, reads, writes, key):
            R.add(eng, lambda e: e.dma_start(out=out, in_=in_), reads, writes, dma=key)

        def MEMSET(ap, val, writes):
            R.add("dve", lambda e: e.memset(ap, val), (), writes)

        def RECIP(out, in_, reads, writes):
            R.add("dve", lambda e: e.reciprocal(out=out, in_=in_), reads, writes)

        def RSUM(out, in_, reads, writes):
            R.add("dve", lambda e: e.reduce_sum(out=out, in_=in_, axis=AX.X), reads, writes)

        for t in range(NT):
            DMA("sp", x_sb[:, t, :], x_d[t * 128:(t + 1) * 128, :], [], [X(t)], "x%d" % t)
        DMA("pool", ident[:], ident_d, [], [r("ident")], "c_ident")
        DMA("pool", maskA[:], mask_d, [], [r("maskA")], "c_mask")
        DMA("sp", pos_i[:], pos_d, [], [r("pos_i")], "c_pos")
        invf = scr[5][:, 0:32]
        DMA("sp", invf, invf_d, [], [SC(5)], "c_invf")
        posf = stats[:, 15, :]
        CP("dve", posf, pos_i[:], [r("pos_i")], [ST(15)])
        TWO_PI = 2.0 * np.pi
        kf = scr[4][:, 0:512].rearrange("p (a b) -> p a b", a=NT)
        ki = scr[3].bitcast(I32)[:, 0:512].rearrange("p (a b) -> p a b", a=NT)
        for (tab, shift, nm) in ((sin_t, 0.0, "tab_s"), (cos_t, 0.5 * np.pi, "tab_c")):
            TT(tab[:], bc(posf.unsqueeze(2), [128, NT, 32]), bc(invf.unsqueeze(1), [128, NT, 32]), ALU.mult,
               [ST(15), SC(5)], [r(nm)])
            if shift != 0.0:
                TS(tab[:], tab[:], float(shift), None, ALU.add, None, [r(nm)], [r(nm)])
            TS(kf, tab[:], float(1.0 / TWO_PI), None, ALU.mult, None, [r(nm)], [SC(4)])
            CP("dve", ki, kf, [SC(4)], [SC(3)])
            CP("dve", kf, ki, [SC(3)], [SC(4)])
            STT(tab[:], kf, float(-TWO_PI), tab[:], ALU.mult, ALU.add, [SC(4), r(nm)], [r(nm)])
            TS(tab[:], tab[:], float(np.pi), float(-np.pi), ALU.min, ALU.max, [r(nm)], [r(nm)])
            ACTF(tab[:], tab[:], AF.Sin, [r(nm)], [r(nm)])
        TABS = [r("tab_s"), r("tab_c")]

        def load_slot(parts):
            s_ = nslot()
            for (dst_fn, src) in parts:
                DMA("pool", dst_fn(slots[s_]), src, [], [SL(s_)], "slot%d" % s_)
            return s_

        def rms_stats(ss_ap, out_ap, res_in, res_out, mult, add):
            TS(out_ap, ss_ap, float(mult), float(add), ALU.mult, ALU.add, [res_in], [res_out])
            ACTF(out_ap, out_ap, AF.Ln, [res_out], [res_out])
            ACTF(out_ap, out_ap, AF.Exp, [res_out], [res_out], scale=-0.5)

        def hT_ap(k, lo, n):
            return arena[:, HT0 + k * 2048 + lo: HT0 + k * 2048 + lo + n]

        def hT_res(lo, n):
            return [("A", HT0 + k * 2048 + lo, HT0 + k * 2048 + lo + n) for k in range(8)]

        def norm_phase(l, g_d):
            DMA("sp", g_rep[:], g_d[l].partition_broadcast(128), [], [r("g_rep")], "g_rep")
            si = nstat()
            so = nstat()
            for t in range(NT):
                j = nscr()
                ACTF(scrb[j][:, 0:1024], x_sb[:, t, :], AF.Square, [X(t)], [SC(j), ST(si)], accum=stats[:, si, t:t + 1])
            rms_stats(stats[:, si, :], stats[:, so, :], ST(si), ST(so), 1.0 / D, EPS)
            for t in range(NT):
                j = nscr()
                STT(scrb[j][:, 0:1024], x_sb[:, t, :], stats[:, so, t:t + 1], g_rep[:], ALU.mult, ALU.mult,
                    [X(t), ST(so), r("g_rep")], [SC(j)])
                b = nbank()
                for c in range(8):
                    TR(psb[b][:, c * 128:(c + 1) * 128], scrb[j][:, c * 128:(c + 1) * 128], ident[:], [SC(j)], [PB(b)])
                dst = AV(HT0, 16384).rearrange("p (c s) -> p c s", c=8)[:, :, t * 128:(t + 1) * 128]
                src = psb[b][:, 0:1024].rearrange("p (c s) -> p c s", c=8)
                CP("act" if t % 2 == 0 else "dve", dst, src, [PB(b)], hT_res(t * 128, 128))

        def z_matmuls(b, s_, tok_lo, ncols):
            for k in range(8):
                MM(ps[b][:, 0:ncols], hT_ap(k, tok_lo, 128), slots[s_][:, k * 512: k * 512 + ncols],
                   (k == 0), (k == 7), hT_res(tok_lo, 128) + [SL(s_)], [PB(b)])

        def qk_norm(b, col_lo, nh, gain_idx, t, rope, mult, add, out_ap, out_res):
            n = nh * 64
            zin = ps[b][:, col_lo:col_lo + n]
            j1 = nscr()
            si = nstat()
            so = nstat()
            ACTF(scr[j1][:, 0:n], zin, AF.Square, [PB(b)], [SC(j1)])
            RSUM(stats[:, si, 0:nh], scr[j1][:, 0:n].rearrange("p (h d) -> p h d", h=nh), [SC(j1)], [ST(si)])
            rms_stats(stats[:, si, 0:nh], stats[:, so, 0:nh], ST(si), ST(so), mult, add)
            t1 = scr[j1][:, 0:n].rearrange("p (h d) -> p h d", h=nh)
            TT(t1, zin.rearrange("p (h d) -> p h d", h=nh), bc(stats[:, so, 0:nh].unsqueeze(2), [128, nh, 64]), ALU.mult,
               [PB(b), ST(so)], [SC(j1)])
            gb = bc(gains[:, gain_idx, :].unsqueeze(1), [128, nh, 64])
            o3 = out_ap.rearrange("p (h d) -> p h d", h=nh)
            if not rope:
                TT(o3, t1, gb, ALU.mult, [SC(j1), r("gains")], out_res)
                return
            TT(t1, t1, gb, ALU.mult, [SC(j1), r("gains")], [SC(j1)])
            j3 = nscr()
            tmp = scr[j3][:, 0:n].rearrange("p (h d) -> p h d", h=nh)
            x1 = t1[:, :, 0:32]
            x2 = t1[:, :, 32:64]
            cb = bc(cos_t[:, t, :].unsqueeze(1), [128, nh, 32])
            sb_ = bc(sin_t[:, t, :].unsqueeze(1), [128, nh, 32])
            TT(tmp[:, :, 0:32], x1, cb, ALU.mult, [SC(j1)] + TABS, [SC(j3)])
            TT(tmp[:, :, 32:64], x2, sb_, ALU.mult, [SC(j1)] + TABS, [SC(j3)])
            TT(o3[:, :, 0:32], tmp[:, :, 0:32], tmp[:, :, 32:64], ALU.subtract, [SC(j3)], out_res)
            TT(tmp[:, :, 0:32], x2, cb, ALU.mult, [SC(j1)] + TABS, [SC(j3)])
            TT(tmp[:, :, 32:64], x1, sb_, ALU.mult, [SC(j1)] + TABS, [SC(j3)])
            TT(o3[:, :, 32:64], tmp[:, :, 0:32], tmp[:, :, 32:64], ALU.add, [SC(j3)], out_res)

        def transpose_to(src_aps, src_res, dst_ap, dst_res, evac_eng):
            b = nbank()
            for i, sap in enumerate(src_aps):
                TR(psb[b][:, i * 128:(i + 1) * 128], sap, ident[:], src_res, [PB(b)])
            CP(evac_eng, dst_ap, psb[b][:, 0:len(src_aps) * 128], [PB(b)], dst_res)

        for l in range(nl):
            for gi, gd in enumerate((qna_d, kna_d, qnb_d, knb_d)):
                DMA("sp", gains[:, gi, :], gd[l].partition_broadcast(128), [], [r("gains")], "gains")
            DMA("sp", esink[:], snk_d[l].partition_broadcast(128), [], [r("esink")], "esink")
            ACTF(esink[:], esink[:], AF.Exp, [r("esink")], [r("esink")])

            norm_phase(l, ang_d)

            VA = AV(V0, 2080).rearrange("p (t k d) -> p t k d", t=NT, k=2)
            MEMSET(VA[:, :, :, 64:65], 1.0, [Ar(V0, 2080)])
            s_q = load_slot([(lambda sl: sl[:].rearrange("p (k n) -> p k n", k=8),
                              win_d[l, :, 0:512].rearrange("(k p) n -> p k n", p=128))])
            s_kv = load_slot([(lambda sl: sl[:].rearrange("p (k n) -> p k n", k=8)[:, :, 0:256],
                               win_d[l, :, 512:768].rearrange("(k p) n -> p k n", p=128))])
            pend = None
            for t in range(NT + 1):
                if t < NT:
                    b = nbank()
                    z_matmuls(b, s_q, t * 128, 512)
                    jq = nscr()
                    qn = scrb[jq][:, 0:512]
                    qk_norm(b, 0, 8, 0, t, True, 1.0, 64.0 * EPS, qn, [SC(jq)])
                if pend is not None:
                    (pt_, pqn, pjq) = pend
                    transpose_to([pqn[:, i * 128:(i + 1) * 128] for i in range(4)], [SC(pjq)],
                                 AV(QT0 + pt_ * 512, 512), [Ar(QT0 + pt_ * 512, 512)], "act")
                pend = (t, qn, jq) if t < NT else None
            pend = None
            for t in range(NT + 1):
                if t < NT:
                    b = nbank()
                    z_matmuls(b, s_kv, t * 128, 256)
                    jk = nscr()
                    kn = scrb[jk][:, 0:128]
                    qk_norm(b, 0, 2, 1, t, True, 1.0 / 64, EPS, kn, [SC(jk)])
                    CP("act", VA[:, t, :, 0:64], ps[b][:, 128:256].rearrange("p (k d) -> p k d", k=2),
                       [PB(b)], [Ar(V0 + t * 130, 130)])
                if pend is not None:
                    (pt_, pkn, pjk) = pend
                    transpose_to([pkn], [SC(pjk)], AV(KT0 + pt_ * 128, 128), [Ar(KT0 + pt_ * 128, 128)], "act")
                pend = (t, kn, jk) if t < NT else None

            units = []
            for n in range(NT):
                for kv in range(2):
                    kbs = [kb for kb in (n - 1, n, n + 1) if 0 <= kb < NT]
                    for ii, kb in enumerate(kbs):
                        units.append((n, kv, kb, ii == 0, ii == len(kbs) - 1))
            ctxs = {}
            grp_bo = {}
            due = []

            def a_s1(v):
                (n, kv, kb, first, last) = units[v]
                b = nbank()
                MM(ps[b][:, 0:512], arena[kv * 64:(kv + 1) * 64, KT0 + kb * 128: KT0 + (kb + 1) * 128],
                   arena[kv * 64:(kv + 1) * 64, QT0 + n * 512: QT0 + (n + 1) * 512], True, (kb == n),
                   [Ar(KT0 + kb * 128, 128), Ar(QT0 + n * 512, 512)], [PB(b)], sgc=True)
                if kb != n:
                    w = 0 if kb < n else 1
                    MM(ps[b][:, 0:512], ident[:], maskA[:, w * 512:(w + 1) * 512], False, True,
                       [r("ident"), r("maskA")], [PB(b)], sgc=True)
                jp = npt()
                ACTF(pts_[jp][:, 0:512], ps[b][:, 0:512], AF.Exp, [PB(b)], [PT(jp)])
                ctxs[v] = jp

            def a_s3(v, step):
                (n, kv, kb, first, last) = units[v]
                jp = ctxs.pop(v)
                if first:
                    grp_bo[(n, kv)] = nbank()
                bo = grp_bo[(n, kv)]
                for g in range(4):
                    MM(ps[bo][:, g * 65:(g + 1) * 65], pts_[jp][:, g * 128:(g + 1) * 128], VA[:, kb, kv, :],
                       (first and g == 0), last, [PT(jp), Ar(V0 + kb * 130, 130)], [PB(bo)], sgc=True)
                if not last:
                    return
                pv = ps[bo][:, 0:260].rearrange("p (g d) -> p g d", g=4)
                sd = nstat()
                TT(stats[:, sd, 0:4].unsqueeze(2), pv[:, :, 64:65], esink[:, kv * 4:(kv + 1) * 4].unsqueeze(2), ALU.add,
                   [PB(bo), r("esink")], [ST(sd)])
                RECIP(stats[:, sd, 0:4], stats[:, sd, 0:4], [ST(sd)], [ST(sd)])
                ja = nscr()
                at = scrb[ja][:, 0:256]
                TT(at.rearrange("p (g d) -> p g d", g=4), pv[:, :, 0:64], bc(stats[:, sd, 0:4].unsqueeze(2), [128, 4, 64]),
                   ALU.mult, [PB(bo), ST(sd)], [SC(ja)])

                def s5():
                    dst = AV(AA0, 8192).rearrange("p (c s) -> p c s", c=4)[:, kv * 2:kv * 2 + 2, n * 128:(n + 1) * 128]
                    dres = [("A", AA0 + (kv * 2 + c_) * 2048 + n * 128, AA0 + (kv * 2 + c_) * 2048 + (n + 1) * 128) for c_ in range(2)]
                    bt = nbank()
                    for i in range(2):
                        TR(psb[bt][:, i * 128:(i + 1) * 128], at[:, i * 128:(i + 1) * 128], ident[:], [SC(ja)], [PB(bt)])
                    CP("act", dst, psb[bt][:, 0:256].rearrange("p (c s) -> p c s", c=2), [PB(bt)], dres)
                due.append((step + 2, s5))

            nun = len(units)
            for step in range(nun + 4):
                if step < nun:
                    a_s1(step)
                if 1 <= step <= nun:
                    a_s3(step - 1, step)
                for (ds, fn) in [d_ for d_ in due if d_[0] <= step]:
                    fn()
                due[:] = [d_ for d_ in due if d_[0] > step]
            assert not due

            VB = AV(V0, 2080).rearrange("p (t k d) -> p t k d", t=NT, k=2)
            VS = AV(V0 + 2080, 1950).rearrange("p (t k d) -> p t k d", t=15, k=2)
            for hp in range(4):
                MEMSET(VB[:, :, :, 64:65], 1.0, [Ar(V0, 2080)])
                MEMSET(VS[:, :, :, 64:65], 1.0, [Ar(V0 + 2080, 1950)])
                s_b = load_slot([
                    (lambda sl: sl[:].rearrange("p (k n) -> p k n", k=8)[:, :, 0:128],
                     win_d[l, :, 768 + hp * 128: 768 + (hp + 1) * 128].rearrange("(k p) n -> p k n", p=128)),
                    (lambda sl: sl[:].rearrange("p (k n) -> p k n", k=8)[:, :, 128:256],
                     win_d[l, :, 1280 + hp * 128: 1280 + (hp + 1) * 128].rearrange("(k p) n -> p k n", p=128)),
                    (lambda sl: sl[:].rearrange("p (k n) -> p k n", k=8)[:, :, 256:384],
                     win_d[l, :, 1792 + hp * 128: 1792 + (hp + 1) * 128].rearrange("(k p) n -> p k n", p=128)),
                ])
                s_bias = load_slot([(lambda sl: sl[:], bias_d[l, hp])])
                pend = None
                for t in range(NT + 1):
                    if t < NT:
                        b = nbank()
                        z_matmuls(b, s_b, t * 128, 384)
                        jq = nscr()
                        qkn = scrb[jq][:, 0:256]
                        qk_norm(b, 0, 2, 2, t, False, 1.0, 64.0 * EPS, qkn[:, 0:128], [SC(jq)])
                        qk_norm(b, 128, 2, 3, t, False, 1.0 / 64, EPS, qkn[:, 128:256], [SC(jq)])
                        CP("act", VB[:, t, :, 0:64], ps[b][:, 256:384].rearrange("p (k d) -> p k d", k=2),
                           [PB(b)], [Ar(V0 + t * 130, 130)])
                    if pend is not None:
                        (pt_, pq, pjq) = pend
                        transpose_to([pq[:, 0:128]], [SC(pjq)], AV(QT0 + pt_ * 128, 128), [Ar(QT0 + pt_ * 128, 128)], "act")
                        transpose_to([pq[:, 128:256]], [SC(pjq)], AV(KT0 + pt_ * 128, 128), [Ar(KT0 + pt_ * 128, 128)], "act")
                    pend = (t, qkn, jq) if t < NT else None
                for t in range(15):
                    b = nbank()
                    for k in range(8):
                        MM(ps[b][:, 0:128], hT_ap(k, 64 + t * 128, 128), slots[s_b][:, k * 512 + 256: k * 512 + 384],
                           (k == 0), (k == 7), hT_res(64 + t * 128, 128) + [SL(s_b)], [PB(b)])
                    CP("act", VS[:, t, :, 0:64], ps[b][:, 0:128].rearrange("p (k d) -> p k d", k=2),
                       [PB(b)], [Ar(V0 + 2080 + t * 130, 130)])
                btr = 7
                bctx = {}

                def b_s1(rw):
                    rs = min(max(rw - 4, 0), 24)
                    j0 = rs - rw + 7
                    b = nbank()
                    for hh in range(2):
                        for i in range(4):
                            a_ = rs + 2 * i
                            MM(ps[b][:, hh * 256 + i * 64: hh * 256 + (i + 1) * 64],
                               arena[hh * 64:(hh + 1) * 64, KT0 + a_ * 64: KT0 + a_ * 64 + 128],
                               arena[hh * 64:(hh + 1) * 64, QT0 + rw * 64: QT0 + (rw + 1) * 64], (hh == 0 and i == 0), False,
                               [Ar(KT0 + a_ * 64, 128), Ar(QT0 + rw * 64, 64)], [PB(b)], sgc=True)
                        boff = hh * 2048 + j0 * 256
                        MM(ps[b][:, hh * 256:(hh + 1) * 256], ident[:], slots[s_bias][:, boff:boff + 256], False, True,
                           [r("ident"), SL(s_bias)], [PB(b)], sgc=True)
                    jp = npt()
                    ACTF(pts_[jp][:, 0:512], ps[b][:, 0:512], AF.Exp, [PB(b)], [PT(jp)])
                    bctx[rw] = (rs, jp)

                def b_s3(rw):
                    (rs, jp) = bctx[rw]
                    bo = nbank()
                    for hh in range(2):
                        for i in range(4):
                            a_ = rs + 2 * i
                            if a_ % 2 == 0:
                                vap = VB[:, a_ // 2, hh, :]
                                vres = Ar(V0 + (a_ // 2) * 130, 130)
                            else:
                                vap = VS[:, (a_ - 1) // 2, hh, :]
                                vres = Ar(V0 + 2080 + ((a_ - 1) // 2) * 130, 130)
                            MM(ps[bo][0:64, hh * 65:(hh + 1) * 65], pts_[jp][:, hh * 256 + i * 64: hh * 256 + (i + 1) * 64], vap,
                               (i == 0), (i == 3), [PT(jp), vres], [PB(bo)], sgc=True)
                    pv = ps[bo][0:64, 0:130].rearrange("p (g d) -> p g d", g=2)
                    sd = nstat()
                    RECIP(stats[0:64, sd, 0:2].unsqueeze(2), pv[:, :, 64:65], [PB(bo)], [ST(sd)])
                    ja = nscr()
                    at = scrb[ja][0:64, 0:128]
                    TT(at.rearrange("p (g d) -> p g d", g=2), pv[:, :, 0:64], bc(stats[0:64, sd, 0:2].unsqueeze(2), [64, 2, 64]),
                       ALU.mult, [PB(bo), ST(sd)], [SC(ja)])
                    bctx[rw] = (at, ja)

                def b_s5(rw):
                    (at, ja) = bctx.pop(rw)
                    rr = rw % 8
                    TR(psb[btr][:, rr * 64:(rr + 1) * 64], at, ident[0:64, 0:64], [SC(ja)], [PB(btr)])
                    if rr == 7:
                        dlo = AB0 + hp * 2048 + (rw // 8) * 512
                        CP("act", AV(dlo, 512), psb[btr][:, 0:512], [PB(btr)], [Ar(dlo, 512)])

                for step in range(32 + 2):
                    if step < 32:
                        b_s1(step)
                    if 1 <= step <= 32:
                        b_s3(step - 1)
                    if 2 <= step:
                        b_s5(step - 2)

            for c in range(8):
                s_w = load_slot([
                    (lambda sl: sl[:, 0:1024].rearrange("p (k n) -> p k n", k=8),
                     win_d[l, :, 2304 + c * 128: 2304 + (c + 1) * 128].rearrange("(k p) n -> p k n", p=128)),
                    (lambda sl: sl[:, 1024:2048].rearrange("p (k n) -> p k n", k=8),
                     win_d[l, :, 3328 + c * 128: 3328 + (c + 1) * 128].rearrange("(k p) n -> p k n", p=128)),
                    (lambda sl: sl[:, 2048:2560].rearrange("p (k n) -> p k n", k=4),
                     wpa_d[l, :, c * 128:(c + 1) * 128].rearrange("(k p) n -> p k n", p=128)),
                    (lambda sl: sl[:, 2560:3072].rearrange("p (k n) -> p k n", k=4),
                     wpb_d[l, :, c * 128:(c + 1) * 128].rearrange("(k p) n -> p k n", p=128)),
                ])
                for tg in range(4):
                    bga, bgb, bya, byb = nbank(), nbank(), nbank(), nbank()
                    for (bb, woff) in ((bga, 0), (bgb, 1024)):
                        for k in range(8):
                            MM(ps[bb][:, 0:512], slots[s_w][:, woff + k * 128: woff + (k + 1) * 128], hT_ap(k, tg * 512, 512),
                               (k == 0), (k == 7), hT_res(tg * 512, 512) + [SL(s_w)], [PB(bb)])
                    for (bb, woff, a0) in ((bya, 2048, AA0), (byb, 2560, AB0)):
                        for k in range(4):
                            lo = a0 + k * 2048 + tg * 512
                            MM(ps[bb][:, 0:512], slots[s_w][:, woff + k * 128: woff + (k + 1) * 128], arena[:, lo:lo + 512],
                               (k == 0), (k == 3), [("A", lo, lo + 512), SL(s_w)], [PB(bb)])
                    j1, j2 = nscr(), nscr()
                    ACTF(scr[j1][:, 0:512], ps[bga][:, 0:512], AF.Sigmoid, [PB(bga)], [SC(j1)])
                    ACTF(scr[j2][:, 0:512], ps[bgb][:, 0:512], AF.Sigmoid, [PB(bgb)], [SC(j2)])
                    TT(scr[j1][:, 0:512], scr[j1][:, 0:512], ps[bya][:, 0:512], ALU.mult, [SC(j1), PB(bya)], [SC(j1)])
                    TT(scr[j2][:, 0:512], scr[j2][:, 0:512], ps[byb][:, 0:512], ALU.mult, [SC(j2), PB(byb)], [SC(j2)])
                    mlo = MX0 + c * 2048 + tg * 512
                    TT(AV(mlo, 512), scr[j1][:, 0:512], scr[j2][:, 0:512], ALU.add, [SC(j1), SC(j2)], [Ar(mlo, 512)])

            for hf in range(2):
                s_o = load_slot([(lambda sl: sl[:].rearrange("p (k n) -> p k n", k=8),
                                  wout_d[l, :, hf * 512:(hf + 1) * 512].rearrange("(k p) n -> p k n", p=128))])
                for t in range(NT):
                    b = nbank()
                    for k in range(8):
                        lo = MX0 + k * 2048 + t * 128
                        MM(ps[b][:, 0:512], arena[:, lo:lo + 128], slots[s_o][:, k * 512:(k + 1) * 512],
                           (k == 0), (k == 7), [("A", lo, lo + 128), SL(s_o)], [PB(b)])
                    xs = x_sb[:, t, hf * 512:(hf + 1) * 512]
                    TT(xs, xs, ps[b][:, 0:512], ALU.add, [X(t), PB(b)], [X(t)])

            if dbg == 'attn':
                dbg_d = dt('dbg_ab', [128, 16384], BF16, kind='ExternalOutput').ap()
                DMA('sp', dbg_d, AV(AA0, 16384), [Ar(AA0, 16384)], [], 'dbgout')
                continue
            norm_phase(l, fng_d)

            f0 = 0
            for fgn in (8, 8, 6):
                for fp in range(fgn // 2):
                    fa = f0 + fp * 2
                    s_gu = load_slot([
                        (lambda sl: sl[:, 0:2048].rearrange("p (k n) -> p k n", k=8),
                         wg_d[l, :, fa * 128:(fa + 2) * 128].rearrange("(k p) n -> p k n", p=128)),
                        (lambda sl: sl[:, 2048:4096].rearrange("p (k n) -> p k n", k=8),
                         wu_d[l, :, fa * 128:(fa + 2) * 128].rearrange("(k p) n -> p k n", p=128)),
                    ])
                    for q2 in range(2):
                        fl = fp * 2 + q2
                        for tg in range(4):
                            bg, bu = nbank(), nbank()
                            for (bb, woff) in ((bg, 0), (bu, 2048)):
                                for k in range(8):
                                    wlo = woff + k * 256 + q2 * 128
                                    MM(ps[bb][:, 0:512], slots[s_gu][:, wlo:wlo + 128], hT_ap(k, tg * 512, 512),
                                       (k == 0), (k == 7), hT_res(tg * 512, 512) + [SL(s_gu)], [PB(bb)])
                            j1 = nscr()
                            ACTF(scr[j1][:, 0:512], ps[bg][:, 0:512], AF.Silu, [PB(bg)], [SC(j1)])
                            ulo = UT0 + fl * 2048 + tg * 512
                            TT(AV(ulo, 512), scr[j1][:, 0:512], ps[bu][:, 0:512], ALU.mult, [SC(j1), PB(bu)], [Ar(ulo, 512)])
                dsl = []
                for j in range(0, fgn, 4):
                    nch = min(4, fgn - j)
                    s_d = load_slot([(lambda sl, nch=nch: sl[:, 0:nch * 1024].rearrange("p (k n) -> p k n", k=nch),
                                      wd_d[l, (f0 + j) * 128:(f0 + j + nch) * 128, :].rearrange("(k p) n -> p k n", p=128))])
                    dsl.append(s_d)
                for t in range(NT):
                    for hf in range(2):
                        b = nbank()
                        for j in range(fgn):
                            s_d = dsl[j // 4]
                            lo = UT0 + j * 2048 + t * 128
                            wlo = (j % 4) * 1024 + hf * 512
                            MM(ps[b][:, 0:512], arena[:, lo:lo + 128], slots[s_d][:, wlo:wlo + 512],
                               (j == 0), (j == fgn - 1), [("A", lo, lo + 128), SL(s_d)], [PB(b)])
                        xs = x_sb[:, t, hf * 512:(hf + 1) * 512]
                        TT(xs, xs, ps[b][:, 0:512], ALU.add, [X(t), PB(b)], [X(t)])
                f0 += fgn

        for t in range(NT):
            DMA("sp", out_d[t * 128:(t + 1) * 128, :], x_sb[:, t, :], [X(t)], [], "out")
        R.emit(final_waits=[(R.dmasems["out"], R.dmacnt["out"])])
    return nc


def _consts():
    ident = np.eye(128, dtype=np.float32)
    j = np.arange(128)[:, None]
    rq = np.arange(128)[None, :]
    m_prev = np.where(j >= rq, 0.0, NEG).astype(np.float32)
    m_next = np.where(j <= rq, 0.0, NEG).astype(np.float32)
    mask = np.concatenate([np.tile(m_prev, (1, 4)), np.tile(m_next, (1, 4))], axis=1)
    half = 32
    inv = (10000.0 ** (-np.arange(half, dtype=np.float32) / np.float32(half))).astype(np.float32)
    invf = np.tile(inv[None, :], (128, 1)).astype(np.float32)
    return ident, np.ascontiguousarray(mask), invf


def _bias_tiles(rel_bias):
    L = rel_bias.shape[0]
    o = np.arange(2)
    j0 = np.arange(8)
    i = np.arange(4)
    DR = np.minimum(j0[None, :, None] + 2 * i[None, None, :] + o[:, None, None], 14)
    kc = np.arange(64)[:, None]
    qc = np.arange(64)[None, :]
    DC = np.clip(kc - qc + 15, 0, 30)
    cs = np.clip(qc - 8, 0, 48)
    valid = (kc >= cs) & (kc < cs + 16)
    g = rel_bias[:, :, DR[:, None, :, :, None], DC[None, :, None, None, :]]
    g = np.where(valid[None, None, None, :, None, None, :], g, np.float32(NEG)).astype(np.float32)
    g = g.reshape(L, 4, 2, 128, 8, 4, 64).transpose(0, 1, 3, 2, 4, 5, 6)
    return np.ascontiguousarray(g.reshape(L, 4, 128, 4096))


def _perm_w_in(w_in):
    L = w_in.shape[0]
    qa = w_in[:, :, 0:512].reshape(L, D, 2, 4, 64).transpose(0, 1, 3, 2, 4).reshape(L, D, 512)
    return np.ascontiguousarray(np.concatenate([qa, w_in[:, :, 512:]], axis=2))


_NC_CACHE = {}


def _get_nc(nl):
    if nl not in _NC_CACHE:
        _NC_CACHE[nl] = build(nl)
    return _NC_CACHE[nl]


def kernel(x, positions, attn_norm_g, w_in, q_norm_a, k_norm_a, sink_a, q_norm_b, k_norm_b,
           rel_bias_b, w_proj_a, w_proj_b, w_out, ffn_norm_g, w_gate, w_up, w_down):
    f = lambda a: np.ascontiguousarray(np.asarray(a, dtype=np.float32))
    x = f(x)
    pos = np.ascontiguousarray(np.asarray(positions, dtype=np.int32).reshape(NT, 128).T)
    ident, mask, invf = _consts()
    w_in_p = _perm_w_in(f(w_in))
    bias_t = _bias_tiles(f(rel_bias_b))
    per_layer = {
        "attn_norm_g": f(attn_norm_g), "w_in": w_in_p, "q_norm_a": f(q_norm_a), "k_norm_a": f(k_norm_a),
        "sink_a": f(sink_a), "q_norm_b": f(q_norm_b), "k_norm_b": f(k_norm_b), "bias_t": bias_t,
        "w_proj_a": f(w_proj_a), "w_proj_b": f(w_proj_b), "w_out": f(w_out), "ffn_norm_g": f(ffn_norm_g),
        "w_gate": f(w_gate), "w_up": f(w_up), "w_down": f(w_down),
    }
    consts = {"pos": pos, "ident": ident, "mask_a": mask, "invf": invf}
    B = x.shape[0]
    if FUSED:
        nc = _get_nc(DEPTH)
        in_maps = []
        for b in range(B):
            m = {"x": x[b]}
            m.update(consts)
            m.update(per_layer)
            in_maps.append(m)
        res = run_bass_kernel_spmd(nc, in_maps, core_ids=list(range(B)))
        return np.stack([res.results[b]["out"] for b in range(B)], axis=0)
    nc = _get_nc(1)
    cur = [x[b] for b in range(B)]
    for l in range(DEPTH):
        in_maps = []
        for b in range(B):
            m = {"x": cur[b]}
            m.update(consts)
            m.update({k: np.ascontiguousarray(v[l:l + 1]) for k, v in per_layer.items()})
            in_maps.append(m)
        res = run_bass_kernel_spmd(nc, in_maps, core_ids=list(range(B)))
        cur = [np.ascontiguousarray(res.results[b]["out"]) for b in range(B)]
    return np.stack(cur, axis=0)
```

```python
import numpy as np
from contextlib import ExitStack
import concourse.bass as bass
import concourse.mybir as mybir
from concourse.bass_utils import run_bass_kernel_spmd

F32 = mybir.dt.float32
BF16 = mybir.dt.bfloat16
I32 = mybir.dt.int32
ALU = mybir.AluOpType
AF = mybir.ActivationFunctionType
AX = mybir.AxisListType

S = 2048
D = 1024
NT = 16
FF = 2816
NFC = 22
INC = 4352
NEG = -30000.0
EPS = 1e-6
DEPTH = 4
FUSED = True
ROPE2_ENG = "pool"
VS_DMA = True

ENGS = ["pe", "act", "dve", "pool", "sp"]


class Op:
    __slots__ = ("eng", "fn", "deps", "marked", "count", "sem", "is_dma", "idx")

    def __init__(self, eng, fn):
        self.eng = eng
        self.fn = fn
        self.deps = []
        self.marked = False
        self.count = 0
        self.sem = None
        self.is_dma = False


class Rec:
    def __init__(self, nc, stack):
        self.nc = nc
        self.stack = stack
        self.ops = {e: [] for e in ENGS}
        self.hist = {}
        self.names = {}
        self.engsem = {e: stack.enter_context(nc.semaphore("s_" + e)) for e in ENGS}
        self.dmasems = {}
        self.dmacnt = {}
        self.nops = 0
        self.phases = []

    def phase(self, name):
        self.phases.append((name, len(self.ops['pe'])))

    def res(self, name):
        if name not in self.names:
            self.names[name] = len(self.names)
        i = self.names[name]
        return ("N", i, i + 1)

    def _sem_for(self, key):
        if key not in self.dmasems:
            self.dmasems[key] = self.stack.enter_context(self.nc.semaphore("d_%d" % len(self.dmasems)))
            self.dmacnt[key] = 0
        return self.dmasems[key]

    def add(self, eng, fn, reads=(), writes=(), dma=None):
        op = Op(eng, fn)
        op.idx = self.nops
        self.nops += 1
        deps = {}
        for (sp, lo, hi) in reads:
            for rec in self.hist.get(sp, ()):
                if rec[0] < hi and lo < rec[1] and (rec[3] or (sp == "P" and rec[2].eng != eng)):
                    deps[id(rec[2])] = rec[2]
        for (sp, lo, hi) in writes:
            for rec in self.hist.get(sp, ()):
                if rec[0] < hi and lo < rec[1]:
                    deps[id(rec[2])] = rec[2]
        for d in deps.values():
            if d.eng == "pe" and eng == "pe":
                continue
            d.marked = True
            op.deps.append(d)
        if dma is not None:
            op.is_dma = True
            op.sem = self._sem_for(dma)
            self.dmacnt[dma] += 16
            op.count = self.dmacnt[dma]
        for (sp, lo, hi) in writes:
            h = self.hist.setdefault(sp, [])
            h[:] = [r for r in h if not (lo <= r[0] and r[1] <= hi)]
            h.append([lo, hi, op, True])
        for (sp, lo, hi) in reads:
            h = self.hist.setdefault(sp, [])
            if not op.is_dma:
                h[:] = [r for r in h if not ((not r[3]) and r[0] == lo and r[1] == hi
                                             and r[2].eng == eng and not r[2].is_dma)]
            h.append([lo, hi, op, False])
        self.ops[eng].append(op)
        return op

    def emit(self, final_waits=()):
        nc = self.nc
        for e in ENGS:
            c = 0
            for op in self.ops[e]:
                if op.is_dma:
                    continue
                if op.marked:
                    c += 1
                    op.count = c
                    op.sem = self.engsem[e]
        ops = self.ops
        engsem = self.engsem

        def make_body(e):
            def body(eng):
                waited = {}
                for op in ops[e]:
                    need = {}
                    for d in op.deps:
                        k = id(d.sem)
                        if k not in need or need[k][1] < d.count:
                            need[k] = (d.sem, d.count)
                    for k, (sem, val) in need.items():
                        if waited.get(k, 0) >= val:
                            continue
                        eng.wait_ge(sem, val)
                        waited[k] = val
                    ins = op.fn(eng)
                    if op.is_dma:
                        ins.then_inc(op.sem, 16)
                    elif op.marked:
                        ins.then_inc(engsem[e], 1)
                if e == "sp":
                    for (sem, val) in final_waits:
                        eng.wait_ge(sem, val)
            return body

        with nc.Block() as block:
            block.tensor(make_body("pe"))
            block.scalar(make_body("act"))
            block.vector(make_body("dve"))
            block.gpsimd(make_body("pool"))
            block.sync(make_body("sp"))


def bc(ap, shape):
    return ap.to_broadcast(list(shape))


def build(nl, dbg=None):
    nc = bass.Bass("TRN2", target_bir_lowering=False)
    dt = nc.dram_tensor
    x_d = dt("x", [S, D], F32, kind="ExternalInput").ap()
    pos_d = dt("pos", [128, NT], I32, kind="ExternalInput").ap()
    ang_d = dt("attn_norm_g", [nl, D], F32, kind="ExternalInput").ap()
    win_d = dt("w_in", [nl, D, INC], F32, kind="ExternalInput").ap()
    qna_d = dt("q_norm_a", [nl, 64], F32, kind="ExternalInput").ap()
    kna_d = dt("k_norm_a", [nl, 64], F32, kind="ExternalInput").ap()
    snk_d = dt("sink_a", [nl, 8], F32, kind="ExternalInput").ap()
    qnb_d = dt("q_norm_b", [nl, 64], F32, kind="ExternalInput").ap()
    knb_d = dt("k_norm_b", [nl, 64], F32, kind="ExternalInput").ap()
    bias_d = dt("bias_t", [nl, 4, 128, 4096], F32, kind="ExternalInput").ap()
    wpa_d = dt("w_proj_a", [nl, 512, D], F32, kind="ExternalInput").ap()
    wpb_d = dt("w_proj_b", [nl, 512, D], F32, kind="ExternalInput").ap()
    wout_d = dt("w_out", [nl, D, D], F32, kind="ExternalInput").ap()
    fng_d = dt("ffn_norm_g", [nl, D], F32, kind="ExternalInput").ap()
    wg_d = dt("w_gate", [nl, D, FF], F32, kind="ExternalInput").ap()
    wu_d = dt("w_up", [nl, D, FF], F32, kind="ExternalInput").ap()
    wd_d = dt("w_down", [nl, FF, D], F32, kind="ExternalInput").ap()
    ident_d = dt("ident", [128, 128], F32, kind="ExternalInput").ap()
    mask_d = dt("mask_a", [128, 1024], F32, kind="ExternalInput").ap()
    invf_d = dt("invf", [128, 32], F32, kind="ExternalInput").ap()
    out_d = dt("out", [S, D], F32, kind="ExternalOutput").ap()

    with ExitStack() as st:
        R = Rec(nc, st)
        sbt = lambda n, s, d: st.enter_context(nc.sbuf_tensor(n, s, d))
        x_sb = sbt("x_sb", [128, NT, D], F32)
        AR_N = 47104
        arena = sbt("arena", [128, AR_N], BF16)
        slots = [sbt("slot%d" % i, [128, 4096], BF16) for i in range(3)]
        scr = [sbt("scr%d" % i, [128, 512], F32) for i in range(6)]
        scrb = [s_.bitcast(BF16) for s_ in scr]
        pts_ = [sbt("pt%d" % i, [128, 512], BF16) for i in range(3)]
        g_rep = sbt("g_rep", [128, D], F32)
        ident = sbt("ident_sb", [128, 128], BF16)
        maskA = sbt("maskA_sb", [128, 1024], BF16)
        cos_t = sbt("cos_t", [128, NT, 32], F32)
        sin_t = sbt("sin_t", [128, NT, 32], F32)
        gains = sbt("gains", [128, 6, 64], F32)
        epsb = sbt("epsb", [128, 2], F32)
        esink = sbt("esink", [128, 8], F32)
        stats = sbt("stats", [128, 16, 16], F32)
        ps = [st.enter_context(nc.psum_tensor("ps%d" % i, [128, 512], F32)) for i in range(8)]
        psb = [p_.bitcast(BF16) for p_ in ps]

        HT0 = 0
        QT0 = 16384
        KT0 = QT0 + 8192
        V0 = KT0 + 2048
        AA0 = V0 + 4096
        AB0 = AA0 + 8192
        MX0 = QT0
        UT0 = QT0
        assert AB0 + 8192 == AR_N

        def AV(lo, n):
            return arena[:, lo:lo + n]

        def Ar(lo, n):
            return ("A", lo, lo + n)

        def X(t):
            return ("X", t, t + 1)

        def PB(b):
            return ("P", b, b + 1)

        def SL(s_):
            return ("W", s_, s_ + 1)

        def SC(i):
            return ("S", i, i + 1)

        r = R.res
        state = {"bank": 0, "scr": 0, "stat": 0, "slot": 0, "pt": 0}

        def nbank():
            b = state["bank"]
            state["bank"] = (b + 1) % 7
            return b

        def nscr():
            i = state["scr"]
            state["scr"] = (i + 1) % 6
            return i

        def nstat():
            i = state["stat"]
            state["stat"] = (i + 1) % 16
            return i

        def nslot():
            i = state["slot"]
            state["slot"] = (i + 1) % 3
            return i

        def ST(i):
            return ("T", i, i + 1)

        def PT(i):
            return ("Q", i, i + 1)

        def npt():
            i = state["pt"]
            state["pt"] = (i + 1) % 3
            return i

        def MM(out, lhsT, rhs, start, stop, reads, writes, sgc=False):
            if sgc:
                R.add("pe", lambda e: e.matmul(out, lhsT=lhsT, rhs=rhs, start=start, stop=stop, skip_group_check=True),
                      reads, writes)
            else:
                R.add("pe", lambda e: e.matmul(out, lhsT=lhsT, rhs=rhs, start=start, stop=stop), reads, writes)

        def TR(out, in_, idn, reads, writes):
            R.add("pe", lambda e: e.transpose(out=out, in_=in_, identity=idn), reads + [r("ident")], writes)

        def ACTF(out, in_, func, reads, writes, scale=None, accum=None, bias=None):
            kw = {}
            if bias is not None:
                kw["bias"] = bias
            if scale is not None:
                kw["scale"] = scale
            if accum is not None:
                kw["accum_out"] = accum
            R.add("act", lambda e: e.activation(out=out, in_=in_, func=func, **kw), reads, writes)

        def TT(out, in0, in1, op, reads, writes):
            R.add("dve", lambda e: e.tensor_tensor(out=out, in0=in0, in1=in1, op=op), reads, writes)

        def PTT(out, in0, in1, op, reads, writes):
            R.add(ROPE2_ENG, lambda e: e.tensor_tensor(out=out, in0=in0, in1=in1, op=op), reads, writes)

        def TS(out, in0, s1, s2, op0, op1, reads, writes):
            if s2 is None:
                R.add("dve", lambda e: e.tensor_scalar(out=out, in0=in0, scalar1=s1, scalar2=None, op0=op0), reads, writes)
            else:
                R.add("dve", lambda e: e.tensor_scalar(out=out, in0=in0, scalar1=s1, scalar2=s2, op0=op0, op1=op1), reads, writes)

        def STT(out, in0, scalar, in1, op0, op1, reads, writes):
            R.add("dve", lambda e: e.scalar_tensor_tensor(out=out, in0=in0, scalar=scalar, in1=in1, op0=op0, op1=op1), reads, writes)

        def CP(eng, out, in_, reads, writes):
            if eng == "act":
                R.add("act", lambda e: e.copy(out=out, in_=in_), reads, writes)
            else:
                R.add("dve", lambda e: e.tensor_copy(out=out, in_=in_), reads, writes)

        def DMA(eng, out, in_, reads, writes, key):
            R.add(eng, lambda e: e.dma_start(out=out, in_=in_), reads, writes, dma=key)

        def MEMSET(ap, val, writes):
            R.add("dve", lambda e: e.memset(ap, val), (), writes)

        def RECIP(out, in_, reads, writes):
            R.add("dve", lambda e: e.reciprocal(out=out, in_=in_), reads, writes)

        def RSUM(out, in_, reads, writes):
            R.add("dve", lambda e: e.reduce_sum(out=out, in_=in_, axis=AX.X), reads, writes)

        for t in range(NT):
            DMA("sp", x_sb[:, t, :], x_d[t * 128:(t + 1) * 128, :], [], [X(t)], "x%d" % t)
        DMA("pool", ident[:], ident_d, [], [r("ident")], "c_ident")
        DMA("pool", maskA[:], mask_d, [], [r("maskA")], "c_mask")
        pos_i = scr[2].bitcast(I32)[:, 0:NT]
        DMA("sp", pos_i, pos_d, [], [SC(2)], "c_pos")
        invf = scr[5][:, 0:32]
        DMA("sp", invf, invf_d, [], [SC(5)], "c_invf")
        MEMSET(epsb[:, 0:1], EPS, [r("epsb")])
        MEMSET(epsb[:, 1:2], 64.0 * EPS, [r("epsb")])
        posf = stats[:, 15, :]
        CP("dve", posf, pos_i, [SC(2)], [ST(15)])
        TWO_PI = 2.0 * np.pi
        kf = scr[4][:, 0:512].rearrange("p (a b) -> p a b", a=NT)
        ki = scr[3].bitcast(I32)[:, 0:512].rearrange("p (a b) -> p a b", a=NT)
        for (tab, shift, nm) in ((sin_t, 0.0, "tab_s"), (cos_t, 0.5 * np.pi, "tab_c")):
            TT(tab[:], bc(posf.unsqueeze(2), [128, NT, 32]), bc(invf.unsqueeze(1), [128, NT, 32]), ALU.mult,
               [ST(15), SC(5)], [r(nm)])
            if shift != 0.0:
                TS(tab[:], tab[:], float(shift), None, ALU.add, None, [r(nm)], [r(nm)])
            TS(kf, tab[:], float(1.0 / TWO_PI), None, ALU.mult, None, [r(nm)], [SC(4)])
            CP("dve", ki, kf, [SC(4)], [SC(3)])
            CP("dve", kf, ki, [SC(3)], [SC(4)])
            STT(tab[:], kf, float(-TWO_PI), tab[:], ALU.mult, ALU.add, [SC(4), r(nm)], [r(nm)])
            TS(tab[:], tab[:], float(np.pi), float(-np.pi), ALU.min, ALU.max, [r(nm)], [r(nm)])
            ACTF(tab[:], tab[:], AF.Sin, [r(nm)], [r(nm)])
        TABS = [r("tab_s"), r("tab_c")]

        def load_slot(parts):
            s_ = nslot()
            for (dst_fn, src) in parts:
                DMA("pool", dst_fn(slots[s_]), src, [], [SL(s_)], "slot%d" % s_)
            return s_

        def rms_stats(ss_ap, out_ap, res_in, res_out, mult, which):
            ACTF(out_ap, ss_ap, AF.Ln, [res_in, r("epsb")], [res_out], scale=float(mult), bias=epsb[:, which:which + 1])
            ACTF(out_ap, out_ap, AF.Exp, [res_out], [res_out], scale=-0.5)

        def hT_ap(k, lo, n):
            return arena[:, HT0 + k * 2048 + lo: HT0 + k * 2048 + lo + n]

        def hT_res(lo, n):
            return [("A", HT0 + k * 2048 + lo, HT0 + k * 2048 + lo + n) for k in range(8)]

        def norm_phase(l, g_d):
            DMA("sp", g_rep[:], g_d[l].partition_broadcast(128), [], [r("g_rep")], "g_rep")
            si = nstat()
            so = nstat()
            for t in range(NT):
                j = nscr()
                ACTF(scrb[j][:, 0:1024], x_sb[:, t, :], AF.Square, [X(t)], [SC(j), ST(si)], accum=stats[:, si, t:t + 1])
            rms_stats(stats[:, si, :], stats[:, so, :], ST(si), ST(so), 1.0 / D, 0)
            for t in range(NT):
                j = nscr()
                STT(scrb[j][:, 0:1024], x_sb[:, t, :], stats[:, so, t:t + 1], g_rep[:], ALU.mult, ALU.mult,
                    [X(t), ST(so), r("g_rep")], [SC(j)])
                b = nbank()
                for c in range(8):
                    TR(psb[b][:, c * 128:(c + 1) * 128], scrb[j][:, c * 128:(c + 1) * 128], ident[:], [SC(j)], [PB(b)])
                dst = AV(HT0, 16384).rearrange("p (c s) -> p c s", c=8)[:, :, t * 128:(t + 1) * 128]
                src = psb[b][:, 0:1024].rearrange("p (c s) -> p c s", c=8)
                CP("act" if t % 2 == 0 else "dve", dst, src, [PB(b)], hT_res(t * 128, 128))

        def z_matmuls(b, s_, tok_lo, ncols):
            for k in range(8):
                MM(ps[b][:, 0:ncols], hT_ap(k, tok_lo, 128), slots[s_][:, k * 512: k * 512 + ncols],
                   (k == 0), (k == 7), hT_res(tok_lo, 128) + [SL(s_)], [PB(b)])

        def qk_norm(b, col_lo, nh, gain_ap, t, rope, out_ap, out_res):
            n = nh * 64
            zin = ps[b][:, col_lo:col_lo + n]
            j1 = nscr()
            si = nstat()
            so = nstat()
            ACTF(scr[j1][:, 0:n], zin, AF.Square, [PB(b)], [SC(j1)])
            RSUM(stats[:, si, 0:nh], scr[j1][:, 0:n].rearrange("p (h d) -> p h d", h=nh), [SC(j1)], [ST(si)])
            rms_stats(stats[:, si, 0:nh], stats[:, so, 0:nh], ST(si), ST(so), 1.0, 1)
            t1 = scr[j1][:, 0:n].rearrange("p (h d) -> p h d", h=nh)
            TT(t1, zin.rearrange("p (h d) -> p h d", h=nh), bc(stats[:, so, 0:nh].unsqueeze(2), [128, nh, 64]), ALU.mult,
               [PB(b), ST(so)], [SC(j1)])
            o3 = out_ap.rearrange("p (h d) -> p h d", h=nh)
            if not rope:
                TT(o3, t1, gain_ap, ALU.mult, [SC(j1), r("gains")], out_res)
                return
            TT(t1, t1, gain_ap, ALU.mult, [SC(j1), r("gains")], [SC(j1)])
            j3 = nscr()
            tmp = scr[j3][:, 0:n].rearrange("p (h d) -> p h d", h=nh)
            x1 = t1[:, :, 0:32]
            x2 = t1[:, :, 32:64]
            cb = bc(cos_t[:, t, :].unsqueeze(1), [128, nh, 32])
            sb_ = bc(sin_t[:, t, :].unsqueeze(1), [128, nh, 32])
            TT(tmp[:, :, 0:32], x1, cb, ALU.mult, [SC(j1)] + TABS, [SC(j3)])
            TT(tmp[:, :, 32:64], x2, sb_, ALU.mult, [SC(j1)] + TABS, [SC(j3)])
            TT(o3[:, :, 0:32], tmp[:, :, 0:32], tmp[:, :, 32:64], ALU.subtract, [SC(j3)], out_res)
            j4 = nscr()
            tmp2 = scr[j4][:, 0:n].rearrange("p (h d) -> p h d", h=nh)
            PTT(tmp2[:, :, 0:32], x2, cb, ALU.mult, [SC(j1)] + TABS, [SC(j4)])
            PTT(tmp2[:, :, 32:64], x1, sb_, ALU.mult, [SC(j1)] + TABS, [SC(j4)])
            PTT(o3[:, :, 32:64], tmp2[:, :, 0:32], tmp2[:, :, 32:64], ALU.add, [SC(j4)], out_res)

        def transpose_to(src_aps, src_res, dst_ap, dst_res, evac_eng):
            b = nbank()
            for i, sap in enumerate(src_aps):
                TR(psb[b][:, i * 128:(i + 1) * 128], sap, ident[:], src_res, [PB(b)])
            CP(evac_eng, dst_ap, psb[b][:, 0:len(src_aps) * 128], [PB(b)], dst_res)

        for l in range(nl):
            for gi, gd in enumerate((qna_d, kna_d, qnb_d, qnb_d, knb_d, knb_d)):
                DMA("sp", gains[:, gi, :], gd[l].partition_broadcast(128), [], [r("gains")], "gains")
            TS(gains[:, 1, :], gains[:, 1, :], 8.0, None, ALU.mult, None, [r("gains")], [r("gains")])
            TS(gains[:, 4:6, :], gains[:, 4:6, :], 8.0, None, ALU.mult, None, [r("gains")], [r("gains")])
            DMA("sp", esink[:], snk_d[l].partition_broadcast(128), [], [r("esink")], "esink")
            ACTF(esink[:], esink[:], AF.Exp, [r("esink")], [r("esink")])

            R.phase('P1')
            norm_phase(l, ang_d)

            if dbg == 'stopP2':
                break
            R.phase('P2')
            VA = AV(V0, 2080).rearrange("p (t k d) -> p t k d", t=NT, k=2)
            MEMSET(VA[:, :, :, 64:65], 1.0, [Ar(V0, 2080)])
            s_q = load_slot([(lambda sl: sl[:].rearrange("p (k n) -> p k n", k=8),
                              win_d[l, :, 0:512].rearrange("(k p) n -> p k n", p=128))])
            s_kv = load_slot([(lambda sl: sl[:].rearrange("p (k n) -> p k n", k=8)[:, :, 0:256],
                               win_d[l, :, 512:768].rearrange("(k p) n -> p k n", p=128))])
            def z_pipe(ntile, produce, consume, depth=3):
                pend = []
                for i in range(ntile + depth):
                    if i - depth >= 0:
                        consume(*pend.pop(0))
                    if i < ntile:
                        pend.append(produce(i))

            def qa_prod(t):
                b = nbank()
                z_matmuls(b, s_q, t * 128, 512)
                jq = npt()
                qn = pts_[jq][:, 0:512]
                qk_norm(b, 0, 8, bc(gains[:, 0, :].unsqueeze(1), [128, 8, 64]), t, True, qn, [PT(jq)])
                return (t, qn, jq)

            def qa_cons(t, qn, jq):
                transpose_to([qn[:, i * 128:(i + 1) * 128] for i in range(4)], [PT(jq)],
                             AV(QT0 + t * 512, 512), [Ar(QT0 + t * 512, 512)], "act")

            z_pipe(NT, qa_prod, qa_cons)

            def ka_prod(t):
                b = nbank()
                z_matmuls(b, s_kv, t * 128, 256)
                jk = npt()
                kn = pts_[jk][:, 0:128]
                qk_norm(b, 0, 2, bc(gains[:, 1, :].unsqueeze(1), [128, 2, 64]), t, True, kn, [PT(jk)])
                CP("act", VA[:, t, :, 0:64], ps[b][:, 128:256].rearrange("p (k d) -> p k d", k=2),
                   [PB(b)], [Ar(V0 + t * 130, 130)])
                return (t, kn, jk)

            def ka_cons(t, kn, jk):
                transpose_to([kn], [PT(jk)], AV(KT0 + t * 128, 128), [Ar(KT0 + t * 128, 128)], "act")

            z_pipe(NT, ka_prod, ka_cons)

            if dbg == 'stopP3':
                break
            R.phase('P3')
            units = []
            for n in range(NT):
                for kv in range(2):
                    kbs = [kb for kb in (n - 1, n, n + 1) if 0 <= kb < NT]
                    for ii, kb in enumerate(kbs):
                        units.append((n, kv, kb, ii == 0, ii == len(kbs) - 1))
            ctxs = {}
            grp_bo = {}
            due = []

            def a_s1(v):
                (n, kv, kb, first, last) = units[v]
                b = nbank()
                MM(ps[b][:, 0:512], arena[kv * 64:(kv + 1) * 64, KT0 + kb * 128: KT0 + (kb + 1) * 128],
                   arena[kv * 64:(kv + 1) * 64, QT0 + n * 512: QT0 + (n + 1) * 512], True, (kb == n),
                   [Ar(KT0 + kb * 128, 128), Ar(QT0 + n * 512, 512)], [PB(b)], sgc=True)
                if kb != n:
                    w = 0 if kb < n else 1
                    MM(ps[b][:, 0:512], ident[:], maskA[:, w * 512:(w + 1) * 512], False, True,
                       [r("ident"), r("maskA")], [PB(b)], sgc=True)
                jp = npt()
                ACTF(pts_[jp][:, 0:512], ps[b][:, 0:512], AF.Exp, [PB(b)], [PT(jp)])
                ctxs[v] = jp

            def a_s3(v, step):
                (n, kv, kb, first, last) = units[v]
                jp = ctxs.pop(v)
                if first:
                    grp_bo[(n, kv)] = nbank()
                bo = grp_bo[(n, kv)]
                for g in range(4):
                    MM(ps[bo][:, g * 65:(g + 1) * 65], pts_[jp][:, g * 128:(g + 1) * 128], VA[:, kb, kv, :],
                       (first and g == 0), last, [PT(jp), Ar(V0 + kb * 130, 130)], [PB(bo)], sgc=True)
                if not last:
                    return
                pv = ps[bo][:, 0:260].rearrange("p (g d) -> p g d", g=4)
                sd = nstat()
                TT(stats[:, sd, 0:4].unsqueeze(2), pv[:, :, 64:65], esink[:, kv * 4:(kv + 1) * 4].unsqueeze(2), ALU.add,
                   [PB(bo), r("esink")], [ST(sd)])
                RECIP(stats[:, sd, 0:4], stats[:, sd, 0:4], [ST(sd)], [ST(sd)])
                ja = nscr()
                at = scrb[ja][:, 0:256]
                TT(at.rearrange("p (g d) -> p g d", g=4), pv[:, :, 0:64], bc(stats[:, sd, 0:4].unsqueeze(2), [128, 4, 64]),
                   ALU.mult, [PB(bo), ST(sd)], [SC(ja)])

                def s5():
                    dst = AV(AA0, 8192).rearrange("p (c s) -> p c s", c=4)[:, kv * 2:kv * 2 + 2, n * 128:(n + 1) * 128]
                    dres = [("A", AA0 + (kv * 2 + c_) * 2048 + n * 128, AA0 + (kv * 2 + c_) * 2048 + (n + 1) * 128) for c_ in range(2)]
                    bt = nbank()
                    for i in range(2):
                        TR(psb[bt][:, i * 128:(i + 1) * 128], at[:, i * 128:(i + 1) * 128], ident[:], [SC(ja)], [PB(bt)])
                    CP("act", dst, psb[bt][:, 0:256].rearrange("p (c s) -> p c s", c=2), [PB(bt)], dres)
                due.append((step + 2, s5))

            nun = len(units)
            for step in range(nun + 4):
                if step < nun:
                    a_s1(step)
                if 1 <= step <= nun:
                    a_s3(step - 1, step)
                for (ds, fn) in [d_ for d_ in due if d_[0] <= step]:
                    fn()
                due[:] = [d_ for d_ in due if d_[0] > step]
            assert not due

            VB = AV(V0, 2080).rearrange("p (t k d) -> p t k d", t=NT, k=2)
            VS = AV(V0 + 2080, 1950).rearrange("p (t k d) -> p t k d", t=15, k=2)
            for hp in range(4):
                MEMSET(VB[:, :, :, 64:65], 1.0, [Ar(V0, 2080)])
                if dbg == 'stopP4%d' % hp:
                    break
                R.phase('P4z%d' % hp)
                s_b = load_slot([
                    (lambda sl: sl[:].rearrange("p (k n) -> p k n", k=8)[:, :, 0:128],
                     win_d[l, :, 768 + hp * 128: 768 + (hp + 1) * 128].rearrange("(k p) n -> p k n", p=128)),
                    (lambda sl: sl[:].rearrange("p (k n) -> p k n", k=8)[:, :, 128:256],
                     win_d[l, :, 1280 + hp * 128: 1280 + (hp + 1) * 128].rearrange("(k p) n -> p k n", p=128)),
                    (lambda sl: sl[:].rearrange("p (k n) -> p k n", k=8)[:, :, 256:384],
                     win_d[l, :, 1792 + hp * 128: 1792 + (hp + 1) * 128].rearrange("(k p) n -> p k n", p=128)),
                ])
                s_bias = load_slot([(lambda sl: sl[:], bias_d[l, hp])])
                def b_prod(t):
                    b = nbank()
                    z_matmuls(b, s_b, t * 128, 384)
                    jq = npt()
                    qkn = pts_[jq][:, 0:256]
                    qk_norm(b, 0, 4, gains[:, 2:6, :], t, False, qkn, [PT(jq)])
                    CP("act", VB[:, t, :, 0:64], ps[b][:, 256:384].rearrange("p (k d) -> p k d", k=2),
                       [PB(b)], [Ar(V0 + t * 130, 130)])
                    return (t, qkn, jq)

                def b_cons(t, qkn, jq):
                    bq = nbank()
                    TR(psb[bq][:, 0:128], qkn[:, 0:128], ident[:], [PT(jq)], [PB(bq)])
                    TR(psb[bq][:, 128:256], qkn[:, 128:256], ident[:], [PT(jq)], [PB(bq)])
                    lo = QT0 + t * 128
                    dst = arena[:, lo: lo + 2 * (KT0 - QT0)].rearrange("p (c s) -> p c s", c=2)[:, :, 0:128]
                    CP("act" if t % 2 == 0 else "dve", dst, psb[bq][:, 0:256].rearrange("p (c s) -> p c s", c=2), [PB(bq)],
                       [Ar(QT0 + t * 128, 128), Ar(KT0 + t * 128, 128)])

                z_pipe(NT, b_prod, b_cons)
                if VS_DMA:
                    DMA("sp", arena[0:64, V0 + 2080: V0 + 2080 + 1950], arena[64:128, V0: V0 + 1950],
                        [Ar(V0, 2080)], [Ar(V0 + 2080, 1950)], "vs")
                    DMA("sp", arena[64:128, V0 + 2080: V0 + 2080 + 1950], arena[0:64, V0 + 130: V0 + 2080],
                        [Ar(V0, 2080)], [Ar(V0 + 2080, 1950)], "vs")
                else:
                    MEMSET(VS[:, :, :, 64:65], 1.0, [Ar(V0 + 2080, 1950)])
                    for t in range(15):
                        b = nbank()
                        for k in range(8):
                            MM(ps[b][:, 0:128], hT_ap(k, 64 + t * 128, 128), slots[s_b][:, k * 512 + 256: k * 512 + 384],
                               (k == 0), (k == 7), hT_res(64 + t * 128, 128) + [SL(s_b)], [PB(b)])
                        CP("act", VS[:, t, :, 0:64], ps[b][:, 0:128].rearrange("p (k d) -> p k d", k=2),
                           [PB(b)], [Ar(V0 + 2080 + t * 130, 130)])
                if dbg == 'stopP5%d' % hp:
                    break
                R.phase('P5a%d' % hp)
                btr = 7
                bctx = {}

                def b_s1(rw):
                    rs = min(max(rw - 4, 0), 24)
                    j0 = rs - rw + 7
                    b = nbank()
                    for hh in range(2):
                        for i in range(4):
                            a_ = rs + 2 * i
                            MM(ps[b][:, hh * 256 + i * 64: hh * 256 + (i + 1) * 64],
                               arena[hh * 64:(hh + 1) * 64, KT0 + a_ * 64: KT0 + a_ * 64 + 128],
                               arena[hh * 64:(hh + 1) * 64, QT0 + rw * 64: QT0 + (rw + 1) * 64], (hh == 0 and i == 0), False,
                               [Ar(KT0 + a_ * 64, 128), Ar(QT0 + rw * 64, 64)], [PB(b)], sgc=True)
                        boff = hh * 2048 + j0 * 256
                        MM(ps[b][:, hh * 256:(hh + 1) * 256], ident[:], slots[s_bias][:, boff:boff + 256], False, True,
                           [r("ident"), SL(s_bias)], [PB(b)], sgc=True)
                    jp = npt()
                    ACTF(pts_[jp][:, 0:512], ps[b][:, 0:512], AF.Exp, [PB(b)], [PT(jp)])
                    bctx[rw] = (rs, jp)

                def b_s3(rw):
                    (rs, jp) = bctx[rw]
                    bo = nbank()
                    for hh in range(2):
                        for i in range(4):
                            a_ = rs + 2 * i
                            if a_ % 2 == 0:
                                vap = VB[:, a_ // 2, hh, :]
                                vres = Ar(V0 + (a_ // 2) * 130, 130)
                            else:
                                vap = VS[:, (a_ - 1) // 2, hh, :]
                                vres = Ar(V0 + 2080 + ((a_ - 1) // 2) * 130, 130)
                            MM(ps[bo][0:64, hh * 65:(hh + 1) * 65], pts_[jp][:, hh * 256 + i * 64: hh * 256 + (i + 1) * 64], vap,
                               (i == 0), (i == 3), [PT(jp), vres], [PB(bo)], sgc=True)
                    pv = ps[bo][0:64, 0:130].rearrange("p (g d) -> p g d", g=2)
                    sd = nstat()
                    RECIP(stats[0:64, sd, 0:2].unsqueeze(2), pv[:, :, 64:65], [PB(bo)], [ST(sd)])
                    ja = nscr()
                    at = scrb[ja][0:64, 0:128]
                    TT(at.rearrange("p (g d) -> p g d", g=2), pv[:, :, 0:64], bc(stats[0:64, sd, 0:2].unsqueeze(2), [64, 2, 64]),
                       ALU.mult, [PB(bo), ST(sd)], [SC(ja)])
                    bctx[rw] = (at, ja)

                def b_s5(rw):
                    (at, ja) = bctx.pop(rw)
                    rr = rw % 8
                    TR(psb[btr][:, rr * 64:(rr + 1) * 64], at, ident[0:64, 0:64], [SC(ja)], [PB(btr)])
                    if rr == 7:
                        dlo = AB0 + hp * 2048 + (rw // 8) * 512
                        CP("act", AV(dlo, 512), psb[btr][:, 0:512], [PB(btr)], [Ar(dlo, 512)])

                for step in range(32 + 2):
                    if step < 32:
                        b_s1(step)
                    if 1 <= step <= 32:
                        b_s3(step - 1)
                    if 2 <= step:
                        b_s5(step - 2)

            if dbg == 'stopP6':
                break
            R.phase('P6')
            for c in range(8):
                s_w = load_slot([
                    (lambda sl: sl[:, 0:1024].rearrange("p (k n) -> p k n", k=8),
                     win_d[l, :, 2304 + c * 128: 2304 + (c + 1) * 128].rearrange("(k p) n -> p k n", p=128)),
                    (lambda sl: sl[:, 1024:2048].rearrange("p (k n) -> p k n", k=8),
                     win_d[l, :, 3328 + c * 128: 3328 + (c + 1) * 128].rearrange("(k p) n -> p k n", p=128)),
                    (lambda sl: sl[:, 2048:2560].rearrange("p (k n) -> p k n", k=4),
                     wpa_d[l, :, c * 128:(c + 1) * 128].rearrange("(k p) n -> p k n", p=128)),
                    (lambda sl: sl[:, 2560:3072].rearrange("p (k n) -> p k n", k=4),
                     wpb_d[l, :, c * 128:(c + 1) * 128].rearrange("(k p) n -> p k n", p=128)),
                ])
                for tg in range(4):
                    bga, bgb, bya, byb = nbank(), nbank(), nbank(), nbank()
                    for (bb, woff) in ((bga, 0), (bgb, 1024)):
                        for k in range(8):
                            MM(ps[bb][:, 0:512], slots[s_w][:, woff + k * 128: woff + (k + 1) * 128], hT_ap(k, tg * 512, 512),
                               (k == 0), (k == 7), hT_res(tg * 512, 512) + [SL(s_w)], [PB(bb)])
                    for (bb, woff, a0) in ((bya, 2048, AA0), (byb, 2560, AB0)):
                        for k in range(4):
                            lo = a0 + k * 2048 + tg * 512
                            MM(ps[bb][:, 0:512], slots[s_w][:, woff + k * 128: woff + (k + 1) * 128], arena[:, lo:lo + 512],
                               (k == 0), (k == 3), [("A", lo, lo + 512), SL(s_w)], [PB(bb)])
                    j1, j2 = nscr(), nscr()
                    ACTF(scr[j1][:, 0:512], ps[bga][:, 0:512], AF.Sigmoid, [PB(bga)], [SC(j1)])
                    ACTF(scr[j2][:, 0:512], ps[bgb][:, 0:512], AF.Sigmoid, [PB(bgb)], [SC(j2)])
                    TT(scr[j1][:, 0:512], scr[j1][:, 0:512], ps[bya][:, 0:512], ALU.mult, [SC(j1), PB(bya)], [SC(j1)])
                    TT(scr[j2][:, 0:512], scr[j2][:, 0:512], ps[byb][:, 0:512], ALU.mult, [SC(j2), PB(byb)], [SC(j2)])
                    mlo = MX0 + c * 2048 + tg * 512
                    TT(AV(mlo, 512), scr[j1][:, 0:512], scr[j2][:, 0:512], ALU.add, [SC(j1), SC(j2)], [Ar(mlo, 512)])

            R.phase('P7')
            for hf in range(2):
                s_o = load_slot([(lambda sl: sl[:].rearrange("p (k n) -> p k n", k=8),
                                  wout_d[l, :, hf * 512:(hf + 1) * 512].rearrange("(k p) n -> p k n", p=128))])
                for t in range(NT):
                    b = nbank()
                    for k in range(8):
                        lo = MX0 + k * 2048 + t * 128
                        MM(ps[b][:, 0:512], arena[:, lo:lo + 128], slots[s_o][:, k * 512:(k + 1) * 512],
                           (k == 0), (k == 7), [("A", lo, lo + 128), SL(s_o)], [PB(b)])
                    xs = x_sb[:, t, hf * 512:(hf + 1) * 512]
                    TT(xs, xs, ps[b][:, 0:512], ALU.add, [X(t), PB(b)], [X(t)])

            if dbg == 'attn':
                dbg_d = dt('dbg_ab', [128, 16384], BF16, kind='ExternalOutput').ap()
                DMA('sp', dbg_d, AV(AA0, 16384), [Ar(AA0, 16384)], [], 'dbgout')
                continue
            R.phase('P8')
            norm_phase(l, fng_d)

            R.phase('P9')
            f0 = 0
            for fgn in (8, 8, 6):
                for fp in range(fgn // 2):
                    fa = f0 + fp * 2
                    s_gu = load_slot([
                        (lambda sl: sl[:, 0:2048].rearrange("p (k n) -> p k n", k=8),
                         wg_d[l, :, fa * 128:(fa + 2) * 128].rearrange("(k p) n -> p k n", p=128)),
                        (lambda sl: sl[:, 2048:4096].rearrange("p (k n) -> p k n", k=8),
                         wu_d[l, :, fa * 128:(fa + 2) * 128].rearrange("(k p) n -> p k n", p=128)),
                    ])
                    for q2 in range(2):
                        fl = fp * 2 + q2
                        for tg in range(4):
                            bg, bu = nbank(), nbank()
                            for (bb, woff) in ((bg, 0), (bu, 2048)):
                                for k in range(8):
                                    wlo = woff + k * 256 + q2 * 128
                                    MM(ps[bb][:, 0:512], slots[s_gu][:, wlo:wlo + 128], hT_ap(k, tg * 512, 512),
                                       (k == 0), (k == 7), hT_res(tg * 512, 512) + [SL(s_gu)], [PB(bb)])
                            j1 = nscr()
                            ACTF(scr[j1][:, 0:512], ps[bg][:, 0:512], AF.Silu, [PB(bg)], [SC(j1)])
                            ulo = UT0 + fl * 2048 + tg * 512
                            TT(AV(ulo, 512), scr[j1][:, 0:512], ps[bu][:, 0:512], ALU.mult, [SC(j1), PB(bu)], [Ar(ulo, 512)])
                dsl = []
                for j in range(0, fgn, 4):
                    nch = min(4, fgn - j)
                    s_d = load_slot([(lambda sl, nch=nch: sl[:, 0:nch * 1024].rearrange("p (k n) -> p k n", k=nch),
                                      wd_d[l, (f0 + j) * 128:(f0 + j + nch) * 128, :].rearrange("(k p) n -> p k n", p=128))])
                    dsl.append(s_d)
                for t in range(NT):
                    for hf in range(2):
                        b = nbank()
                        for j in range(fgn):
                            s_d = dsl[j // 4]
                            lo = UT0 + j * 2048 + t * 128
                            wlo = (j % 4) * 1024 + hf * 512
                            MM(ps[b][:, 0:512], arena[:, lo:lo + 128], slots[s_d][:, wlo:wlo + 512],
                               (j == 0), (j == fgn - 1), [("A", lo, lo + 128), SL(s_d)], [PB(b)])
                        xs = x_sb[:, t, hf * 512:(hf + 1) * 512]
                        TT(xs, xs, ps[b][:, 0:512], ALU.add, [X(t), PB(b)], [X(t)])
                f0 += fgn

        for t in range(NT):
            DMA("sp", out_d[t * 128:(t + 1) * 128, :], x_sb[:, t, :], [X(t)], [], "out")
        R.phase('end')
        global PHASES
        PHASES = list(R.phases)
        R.emit(final_waits=[(R.dmasems["out"], R.dmacnt["out"])])
    return nc


def _consts():
    ident = np.eye(128, dtype=np.float32)
    j = np.arange(128)[:, None]
    rq = np.arange(128)[None, :]
    m_prev = np.where(j >= rq, 0.0, NEG).astype(np.float32)
    m_next = np.where(j <= rq, 0.0, NEG).astype(np.float32)
    mask = np.concatenate([np.tile(m_prev, (1, 4)), np.tile(m_next, (1, 4))], axis=1)
    half = 32
    inv = (10000.0 ** (-np.arange(half, dtype=np.float32) / np.float32(half))).astype(np.float32)
    invf = np.tile(inv[None, :], (128, 1)).astype(np.float32)
    return ident, np.ascontiguousarray(mask), invf


def _bias_tiles(rel_bias):
    L = rel_bias.shape[0]
    o = np.arange(2)
    j0 = np.arange(8)
    i = np.arange(4)
    DR = np.minimum(j0[None, :, None] + 2 * i[None, None, :] + o[:, None, None], 14)
    kc = np.arange(64)[:, None]
    qc = np.arange(64)[None, :]
    DC = np.clip(kc - qc + 15, 0, 30)
    cs = np.clip(qc - 8, 0, 48)
    valid = (kc >= cs) & (kc < cs + 16)
    g = rel_bias[:, :, DR[:, None, :, :, None], DC[None, :, None, None, :]]
    g = np.where(valid[None, None, None, :, None, None, :], g, np.float32(NEG)).astype(np.float32)
    g = g.reshape(L, 4, 2, 128, 8, 4, 64).transpose(0, 1, 3, 2, 4, 5, 6)
    return np.ascontiguousarray(g.reshape(L, 4, 128, 4096))


def _perm_w_in(w_in):
    L = w_in.shape[0]
    qa = w_in[:, :, 0:512].reshape(L, D, 2, 4, 64).transpose(0, 1, 3, 2, 4).reshape(L, D, 512)
    return np.ascontiguousarray(np.concatenate([qa, w_in[:, :, 512:]], axis=2))


_NC_CACHE = {}


def _get_nc(nl):
    if nl not in _NC_CACHE:
        _NC_CACHE[nl] = build(nl)
    return _NC_CACHE[nl]


def kernel(x, positions, attn_norm_g, w_in, q_norm_a, k_norm_a, sink_a, q_norm_b, k_norm_b,
           rel_bias_b, w_proj_a, w_proj_b, w_out, ffn_norm_g, w_gate, w_up, w_down):
    f = lambda a: np.ascontiguousarray(np.asarray(a, dtype=np.float32))
    x = f(x)
    pos = np.ascontiguousarray(np.asarray(positions, dtype=np.int32).reshape(NT, 128).T)
    ident, mask, invf = _consts()
    w_in_p = _perm_w_in(f(w_in))
    bias_t = _bias_tiles(f(rel_bias_b))
    per_layer = {
        "attn_norm_g": f(attn_norm_g), "w_in": w_in_p, "q_norm_a": f(q_norm_a), "k_norm_a": f(k_norm_a),
        "sink_a": f(sink_a), "q_norm_b": f(q_norm_b), "k_norm_b": f(k_norm_b), "bias_t": bias_t,
        "w_proj_a": f(w_proj_a), "w_proj_b": f(w_proj_b), "w_out": f(w_out), "ffn_norm_g": f(ffn_norm_g),
        "w_gate": f(w_gate), "w_up": f(w_up), "w_down": f(w_down),
    }
    consts = {"pos": pos, "ident": ident, "mask_a": mask, "invf": invf}
    B = x.shape[0]
    if FUSED:
        nc = _get_nc(DEPTH)
        in_maps = []
        for b in range(B):
            m = {"x": x[b]}
            m.update(consts)
            m.update(per_layer)
            in_maps.append(m)
        res = run_bass_kernel_spmd(nc, in_maps, core_ids=list(range(B)))
        return np.stack([res.results[b]["out"] for b in range(B)], axis=0)
    nc = _get_nc(1)
    cur = [x[b] for b in range(B)]
    for l in range(DEPTH):
        in_maps = []
        for b in range(B):
            m = {"x": cur[b]}
            m.update(consts)
            m.update({k: np.ascontiguousarray(v[l:l + 1]) for k, v in per_layer.items()})
            in_maps.append(m)
        res = run_bass_kernel_spmd(nc, in_maps, core_ids=list(range(B)))
        cur = [np.ascontiguousarray(res.results[b]["out"]) for b in range(B)]
    return np.stack(cur, axis=0)
```

```python
import numpy as np
from contextlib import ExitStack
import concourse.bass as bass
import concourse.mybir as mybir
from concourse.bass_utils import run_bass_kernel_spmd

F32 = mybir.dt.float32
BF16 = mybir.dt.bfloat16
I32 = mybir.dt.int32
ALU = mybir.AluOpType
AF = mybir.ActivationFunctionType
AX = mybir.AxisListType

S = 2048
D = 1024
NT = 16
FF = 2816
NFC = 22
INC = 4352
NEG = -30000.0
EPS = 1e-6
DEPTH = 4
FUSED = True
ROPE2_ENG = "pool"
VS_DMA = True
NPT = 5
NSCR = 7
ZDEPTH = 4
B_MULT_ENG = "dve"
B_BIAS_MULT = True
DBG_NOEXP = False
DBG_NOMULT = False

ENGS = ["pe", "act", "dve", "pool", "sp"]


class Op:
    __slots__ = ("eng", "fn", "deps", "marked", "count", "sem", "is_dma", "idx")

    def __init__(self, eng, fn):
        self.eng = eng
        self.fn = fn
        self.deps = []
        self.marked = False
        self.count = 0
        self.sem = None
        self.is_dma = False


class Rec:
    def __init__(self, nc, stack):
        self.nc = nc
        self.stack = stack
        self.ops = {e: [] for e in ENGS}
        self.hist = {}
        self.names = {}
        self.engsem = {e: stack.enter_context(nc.semaphore("s_" + e)) for e in ENGS}
        self.dmasems = {}
        self.dmacnt = {}
        self.nops = 0
        self.phases = []

    def phase(self, name):
        self.phases.append((name, len(self.ops['pe'])))

    def res(self, name):
        if name not in self.names:
            self.names[name] = len(self.names)
        i = self.names[name]
        return ("N", i, i + 1)

    def _sem_for(self, key):
        if key not in self.dmasems:
            self.dmasems[key] = self.stack.enter_context(self.nc.semaphore("d_%d" % len(self.dmasems)))
            self.dmacnt[key] = 0
        return self.dmasems[key]

    def add(self, eng, fn, reads=(), writes=(), dma=None):
        op = Op(eng, fn)
        op.idx = self.nops
        self.nops += 1
        deps = {}
        for (sp, lo, hi) in reads:
            for rec in self.hist.get(sp, ()):
                if rec[0] < hi and lo < rec[1] and (rec[3] or (sp == "P" and rec[2].eng != eng)):
                    deps[id(rec[2])] = rec[2]
        for (sp, lo, hi) in writes:
            for rec in self.hist.get(sp, ()):
                if rec[0] < hi and lo < rec[1]:
                    deps[id(rec[2])] = rec[2]
        for d in deps.values():
            if d.eng == "pe" and eng == "pe":
                continue
            d.marked = True
            op.deps.append(d)
        if dma is not None:
            op.is_dma = True
            op.sem = self._sem_for(dma)
            self.dmacnt[dma] += 16
            op.count = self.dmacnt[dma]
        for (sp, lo, hi) in writes:
            h = self.hist.setdefault(sp, [])
            h[:] = [r for r in h if not (lo <= r[0] and r[1] <= hi)]
            h.append([lo, hi, op, True])
        for (sp, lo, hi) in reads:
            h = self.hist.setdefault(sp, [])
            if not op.is_dma:
                h[:] = [r for r in h if not ((not r[3]) and r[0] == lo and r[1] == hi
                                             and r[2].eng == eng and not r[2].is_dma)]
            h.append([lo, hi, op, False])
        self.ops[eng].append(op)
        return op

    def emit(self, final_waits=()):
        nc = self.nc
        for e in ENGS:
            c = 0
            for op in self.ops[e]:
                if op.is_dma:
                    continue
                if op.marked:
                    c += 1
                    op.count = c
                    op.sem = self.engsem[e]
        ops = self.ops
        engsem = self.engsem

        def make_body(e):
            def body(eng):
                waited = {}
                for op in ops[e]:
                    need = {}
                    for d in op.deps:
                        k = id(d.sem)
                        if k not in need or need[k][1] < d.count:
                            need[k] = (d.sem, d.count)
                    for k, (sem, val) in need.items():
                        if waited.get(k, 0) >= val:
                            continue
                        eng.wait_ge(sem, val)
                        waited[k] = val
                    ins = op.fn(eng)
                    if op.is_dma:
                        ins.then_inc(op.sem, 16)
                    elif op.marked:
                        ins.then_inc(engsem[e], 1)
                if e == "sp":
                    for (sem, val) in final_waits:
                        eng.wait_ge(sem, val)
            return body

        with nc.Block() as block:
            block.tensor(make_body("pe"))
            block.scalar(make_body("act"))
            block.vector(make_body("dve"))
            block.gpsimd(make_body("pool"))
            block.sync(make_body("sp"))


def bc(ap, shape):
    return ap.to_broadcast(list(shape))


def build(nl, dbg=None):
    nc = bass.Bass("TRN2", target_bir_lowering=False)
    dt = nc.dram_tensor
    x_d = dt("x", [S, D], F32, kind="ExternalInput").ap()
    pos_d = dt("pos", [128, NT], I32, kind="ExternalInput").ap()
    ang_d = dt("attn_norm_g", [nl, 128, 8], F32, kind="ExternalInput").ap()
    win_d = dt("w_in", [nl, D, INC], F32, kind="ExternalInput").ap()
    qna_d = dt("q_norm_a", [nl, 64], F32, kind="ExternalInput").ap()
    kna_d = dt("k_norm_a", [nl, 64], F32, kind="ExternalInput").ap()
    snk_d = dt("sink_a", [nl, 8], F32, kind="ExternalInput").ap()
    qnb_d = dt("q_norm_b", [nl, 64], F32, kind="ExternalInput").ap()
    knb_d = dt("k_norm_b", [nl, 64], F32, kind="ExternalInput").ap()
    bias_d = dt("bias_t", [nl, 4, 128, 4096], F32, kind="ExternalInput").ap()
    wpa_d = dt("w_proj_a", [nl, 512, D], F32, kind="ExternalInput").ap()
    wpb_d = dt("w_proj_b", [nl, 512, D], F32, kind="ExternalInput").ap()
    wout_d = dt("w_out", [nl, D, D], F32, kind="ExternalInput").ap()
    fng_d = dt("ffn_norm_g", [nl, 128, 8], F32, kind="ExternalInput").ap()
    wg_d = dt("w_gate", [nl, D, FF], F32, kind="ExternalInput").ap()
    wu_d = dt("w_up", [nl, D, FF], F32, kind="ExternalInput").ap()
    wd_d = dt("w_down", [nl, FF, D], F32, kind="ExternalInput").ap()
    ident_d = dt("ident", [128, 128], F32, kind="ExternalInput").ap()
    mask_d = dt("mask_a", [128, 1024], F32, kind="ExternalInput").ap()
    invf_d = dt("invf", [128, 32], F32, kind="ExternalInput").ap()
    out_d = dt("out", [S, D], F32, kind="ExternalOutput").ap()

    with ExitStack() as st:
        R = Rec(nc, st)
        sbt = lambda n, s, d: st.enter_context(nc.sbuf_tensor(n, s, d))
        x_sb = sbt("x_sb", [128, NT, D], F32)
        AR_N = 47104
        arena = sbt("arena", [128, AR_N], BF16)
        slots = [sbt("slot%d" % i, [128, 4096], BF16) for i in range(3)]
        scr = [sbt("scr%d" % i, [128, 512], F32) for i in range(NSCR)]
        scrb = [s_.bitcast(BF16) for s_ in scr]
        pts_ = [sbt("pt%d" % i, [128, 512], BF16) for i in range(NPT)]
        gT = sbt("gT", [128, 2, 8], F32)
        ident = sbt("ident_sb", [128, 128], BF16)
        maskA = sbt("maskA_sb", [128, 1024], BF16)
        cos_t = sbt("cos_t", [128, NT, 32], F32)
        sin_t = sbt("sin_t", [128, NT, 32], F32)
        gains = sbt("gains", [128, 6, 64], F32)
        epsb = sbt("epsb", [128, 2], F32)
        esink = sbt("esink", [128, 8], F32)
        stats = sbt("stats", [128, 15, 16], F32)
        ps = [st.enter_context(nc.psum_tensor("ps%d" % i, [128, 512], F32)) for i in range(8)]
        psb = [p_.bitcast(BF16) for p_ in ps]

        HT0 = 0
        QT0 = 16384
        KT0 = QT0 + 8192
        V0 = KT0 + 2048
        AA0 = V0 + 4096
        AB0 = AA0 + 8192
        MX0 = QT0
        UT0 = QT0
        assert AB0 + 8192 == AR_N

        def AV(lo, n):
            return arena[:, lo:lo + n]

        def Ar(lo, n):
            return ("A", lo, lo + n)

        def X(t):
            return ("X", t, t + 1)

        def PB(b):
            return ("P", b, b + 1)

        def SL(s_):
            return ("W", s_, s_ + 1)

        def SC(i):
            return ("S", i, i + 1)

        r = R.res
        state = {"bank": 0, "scr": 0, "stat": 0, "slot": 0, "pt": 0}

        def nbank():
            b = state["bank"]
            state["bank"] = (b + 1) % 7
            return b

        def nscr():
            i = state["scr"]
            state["scr"] = (i + 1) % NSCR
            return i

        def nstat():
            i = state["stat"]
            state["stat"] = (i + 1) % 15
            return i

        def nslot():
            i = state["slot"]
            state["slot"] = (i + 1) % 3
            return i

        def ST(i):
            return ("T", i, i + 1)

        def PT(i):
            return ("Q", i, i + 1)

        def npt():
            i = state["pt"]
            state["pt"] = (i + 1) % NPT
            return i

        def MM(out, lhsT, rhs, start, stop, reads, writes, sgc=False):
            if sgc:
                R.add("pe", lambda e: e.matmul(out, lhsT=lhsT, rhs=rhs, start=start, stop=stop, skip_group_check=True),
                      reads, writes)
            else:
                R.add("pe", lambda e: e.matmul(out, lhsT=lhsT, rhs=rhs, start=start, stop=stop), reads, writes)

        def TR(out, in_, idn, reads, writes):
            R.add("pe", lambda e: e.transpose(out=out, in_=in_, identity=idn), reads + [r("ident")], writes)

        def ACTF(out, in_, func, reads, writes, scale=None, accum=None, bias=None):
            kw = {}
            if bias is not None:
                kw["bias"] = bias
            if scale is not None:
                kw["scale"] = scale
            if accum is not None:
                kw["accum_out"] = accum
            R.add("act", lambda e: e.activation(out=out, in_=in_, func=func, **kw), reads, writes)

        def TT(out, in0, in1, op, reads, writes):
            R.add("dve", lambda e: e.tensor_tensor(out=out, in0=in0, in1=in1, op=op), reads, writes)

        def PTT(out, in0, in1, op, reads, writes):
            R.add(ROPE2_ENG, lambda e: e.tensor_tensor(out=out, in0=in0, in1=in1, op=op), reads, writes)

        def GTT(out, in0, in1, op, reads, writes):
            R.add("pool", lambda e: e.tensor_tensor(out=out, in0=in0, in1=in1, op=op), reads, writes)

        def TS(out, in0, s1, s2, op0, op1, reads, writes):
            if s2 is None:
                R.add("dve", lambda e: e.tensor_scalar(out=out, in0=in0, scalar1=s1, scalar2=None, op0=op0), reads, writes)
            else:
                R.add("dve", lambda e: e.tensor_scalar(out=out, in0=in0, scalar1=s1, scalar2=s2, op0=op0, op1=op1), reads, writes)

        def STT(out, in0, scalar, in1, op0, op1, reads, writes):
            R.add("dve", lambda e: e.scalar_tensor_tensor(out=out, in0=in0, scalar=scalar, in1=in1, op0=op0, op1=op1), reads, writes)

        def CP(eng, out, in_, reads, writes):
            if eng == "act":
                R.add("act", lambda e: e.copy(out=out, in_=in_), reads, writes)
            else:
                R.add("dve", lambda e: e.tensor_copy(out=out, in_=in_), reads, writes)

        def DMA(eng, out, in_, reads, writes, key):
            R.add(eng, lambda e: e.dma_start(out=out, in_=in_), reads, writes, dma=key)

        def MEMSET(ap, val, writes):
            R.add("dve", lambda e: e.memset(ap, val), (), writes)

        def RECIP(out, in_, reads, writes):
            R.add("dve", lambda e: e.reciprocal(out=out, in_=in_), reads, writes)

        def RSUM(out, in_, reads, writes):
            R.add("dve", lambda e: e.reduce_sum(out=out, in_=in_, axis=AX.X), reads, writes)

        for t in range(NT):
            DMA("sp", x_sb[:, t, :], x_d[t * 128:(t + 1) * 128, :], [], [X(t)], "x%d" % t)
        DMA("pool", ident[:], ident_d, [], [r("ident")], "c_ident")
        DMA("pool", maskA[:], mask_d, [], [r("maskA")], "c_mask")
        pos_i = scr[2].bitcast(I32)[:, 0:NT]
        DMA("sp", pos_i, pos_d, [], [SC(2)], "c_pos")
        invf = scr[5][:, 0:32]
        DMA("sp", invf, invf_d, [], [SC(5)], "c_invf")
        MEMSET(epsb[:, 0:1], EPS, [r("epsb")])
        MEMSET(epsb[:, 1:2], 64.0 * EPS, [r("epsb")])
        posf = stats[:, 14, :]
        CP("dve", posf, pos_i, [SC(2)], [ST(14)])
        TWO_PI = 2.0 * np.pi
        kf = scr[4][:, 0:512].rearrange("p (a b) -> p a b", a=NT)
        ki = scr[3].bitcast(I32)[:, 0:512].rearrange("p (a b) -> p a b", a=NT)
        for (tab, shift, nm) in ((sin_t, 0.0, "tab_s"), (cos_t, 0.5 * np.pi, "tab_c")):
            TT(tab[:], bc(posf.unsqueeze(2), [128, NT, 32]), bc(invf.unsqueeze(1), [128, NT, 32]), ALU.mult,
               [ST(14), SC(5)], [r(nm)])
            if shift != 0.0:
                TS(tab[:], tab[:], float(shift), None, ALU.add, None, [r(nm)], [r(nm)])
            TS(kf, tab[:], float(1.0 / TWO_PI), None, ALU.mult, None, [r(nm)], [SC(4)])
            CP("dve", ki, kf, [SC(4)], [SC(3)])
            CP("dve", kf, ki, [SC(3)], [SC(4)])
            STT(tab[:], kf, float(-TWO_PI), tab[:], ALU.mult, ALU.add, [SC(4), r(nm)], [r(nm)])
            TS(tab[:], tab[:], float(np.pi), float(-np.pi), ALU.min, ALU.max, [r(nm)], [r(nm)])
            ACTF(tab[:], tab[:], AF.Sin, [r(nm)], [r(nm)])
        TABS = [r("tab_s"), r("tab_c")]

        def load_slot(parts):
            s_ = nslot()
            for (dst_fn, src) in parts:
                DMA("pool", dst_fn(slots[s_]), src, [], [SL(s_)], "slot%d" % s_)
            return s_

        def rms_stats(ss_ap, out_ap, res_in, res_out, mult, which):
            ACTF(out_ap, ss_ap, AF.Ln, [res_in, r("epsb")], [res_out], scale=float(mult), bias=epsb[:, which:which + 1])
            ACTF(out_ap, out_ap, AF.Exp, [res_out], [res_out], scale=-0.5)

        def hT_ap(k, lo, n):
            return arena[:, HT0 + k * 2048 + lo: HT0 + k * 2048 + lo + n]

        def hT_res(lo, n):
            return [("A", HT0 + k * 2048 + lo, HT0 + k * 2048 + lo + n) for k in range(8)]

        def norm_phase(l, g_d, gi):
            DMA("sp", gT[:, gi, :], g_d[l], [], [r("gT%d" % gi)], "gT%d" % gi)
            si = nstat()
            so = nstat()
            for t in range(NT):
                j = nscr()
                ACTF(scrb[j][:, 0:1024], x_sb[:, t, :], AF.Square, [X(t)], [SC(j), ST(si)], accum=stats[:, si, t:t + 1])
            rms_stats(stats[:, si, :], stats[:, so, :], ST(si), ST(so), 1.0 / D, 0)
            for t in range(NT):
                j = nscr()
                ACTF(scrb[j][:, 0:1024], x_sb[:, t, :], AF.Copy, [X(t), ST(so)], [SC(j)], scale=stats[:, so, t:t + 1])
                b = nbank()
                for c in range(8):
                    TR(psb[b][:, c * 128:(c + 1) * 128], scrb[j][:, c * 128:(c + 1) * 128], ident[:], [SC(j)], [PB(b)])
                dst = AV(HT0, 16384).rearrange("p (c s) -> p c s", c=8)[:, :, t * 128:(t + 1) * 128]
                src = psb[b][:, 0:1024].rearrange("p (c s) -> p c s", c=8)
                TT(dst, src, bc(gT[:, gi, :].unsqueeze(2), [128, 8, 128]), ALU.mult, [PB(b), r("gT%d" % gi)], hT_res(t * 128, 128))

        def z_matmuls(b, s_, tok_lo, ncols):
            for k in range(8):
                MM(ps[b][:, 0:ncols], hT_ap(k, tok_lo, 128), slots[s_][:, k * 512: k * 512 + ncols],
                   (k == 0), (k == 7), hT_res(tok_lo, 128) + [SL(s_)], [PB(b)])

        def qk_norm(b, col_lo, nh, gain_ap, t, rope, out_ap, out_res, extra=None):
            n = nh * 64
            zin = ps[b][:, col_lo:col_lo + n]
            j1 = nscr()
            si = nstat()
            so = nstat()
            ACTF(scr[j1][:, 0:n], zin, AF.Square, [PB(b)], [SC(j1)])
            if extra is not None:
                extra()
            RSUM(stats[:, si, 0:nh], scr[j1][:, 0:n].rearrange("p (h d) -> p h d", h=nh), [SC(j1)], [ST(si)])

            def partb():
                rms_stats(stats[:, si, 0:nh], stats[:, so, 0:nh], ST(si), ST(so), 1.0, 1)
                t1 = scr[j1][:, 0:n].rearrange("p (h d) -> p h d", h=nh)
                TT(t1, zin.rearrange("p (h d) -> p h d", h=nh), bc(stats[:, so, 0:nh].unsqueeze(2), [128, nh, 64]), ALU.mult,
                   [PB(b), ST(so)], [SC(j1)])
                o3 = out_ap.rearrange("p (h d) -> p h d", h=nh)
                if not rope:
                    TT(o3, t1, gain_ap, ALU.mult, [SC(j1), r("gains")], out_res)
                    return
                GTT(t1, t1, gain_ap, ALU.mult, [SC(j1), r("gains")], [SC(j1)])
                j3 = nscr()
                tmp = scr[j3][:, 0:n].rearrange("p (h d) -> p h d", h=nh)
                x1 = t1[:, :, 0:32]
                x2 = t1[:, :, 32:64]
                cb = bc(cos_t[:, t, :].unsqueeze(1), [128, nh, 32])
                sb_ = bc(sin_t[:, t, :].unsqueeze(1), [128, nh, 32])
                TT(tmp[:, :, 0:32], x1, cb, ALU.mult, [SC(j1)] + TABS, [SC(j3)])
                TT(tmp[:, :, 32:64], x2, sb_, ALU.mult, [SC(j1)] + TABS, [SC(j3)])
                TT(o3[:, :, 0:32], tmp[:, :, 0:32], tmp[:, :, 32:64], ALU.subtract, [SC(j3)], out_res)
                j4 = nscr()
                tmp2 = scr[j4][:, 0:n].rearrange("p (h d) -> p h d", h=nh)
                PTT(tmp2[:, :, 0:32], x2, cb, ALU.mult, [SC(j1)] + TABS, [SC(j4)])
                PTT(tmp2[:, :, 32:64], x1, sb_, ALU.mult, [SC(j1)] + TABS, [SC(j4)])
                PTT(o3[:, :, 32:64], tmp2[:, :, 0:32], tmp2[:, :, 32:64], ALU.add, [SC(j4)], out_res)
            return partb

        def transpose_to(src_aps, src_res, dst_ap, dst_res, evac_eng):
            b = nbank()
            for i, sap in enumerate(src_aps):
                TR(psb[b][:, i * 128:(i + 1) * 128], sap, ident[:], src_res, [PB(b)])
            CP(evac_eng, dst_ap, psb[b][:, 0:len(src_aps) * 128], [PB(b)], dst_res)

        for l in range(nl):
            for gi, gd in enumerate((qna_d, kna_d, qnb_d, qnb_d, knb_d, knb_d)):
                DMA("sp", gains[:, gi, :], gd[l].partition_broadcast(128), [], [r("gains")], "gains")
            TS(gains[:, 1, :], gains[:, 1, :], 8.0, None, ALU.mult, None, [r("gains")], [r("gains")])
            TS(gains[:, 4:6, :], gains[:, 4:6, :], 8.0, None, ALU.mult, None, [r("gains")], [r("gains")])
            DMA("sp", esink[:], snk_d[l].partition_broadcast(128), [], [r("esink")], "esink")
            ACTF(esink[:], esink[:], AF.Exp, [r("esink")], [r("esink")])

            R.phase('P1')
            norm_phase(l, ang_d, 0)

            if dbg == 'stopP2':
                break
            R.phase('P2')
            VA = AV(V0, 2080).rearrange("p (t k d) -> p t k d", t=NT, k=2)
            MEMSET(VA[:, :, :, 64:65], 1.0, [Ar(V0, 2080)])
            s_q = load_slot([(lambda sl: sl[:].rearrange("p (k n) -> p k n", k=8),
                              win_d[l, :, 0:512].rearrange("(k p) n -> p k n", p=128))])
            s_kv = load_slot([(lambda sl: sl[:].rearrange("p (k n) -> p k n", k=8)[:, :, 0:256],
                               win_d[l, :, 512:768].rearrange("(k p) n -> p k n", p=128))])
            def z_pipe(ntile, produce, consume, depth=ZDEPTH):
                pend = []
                pb = {}
                for i in range(ntile + depth):
                    if i - depth >= 0:
                        consume(*pend.pop(0))
                    if i < ntile:
                        (ctx, partb) = produce(i)
                        pend.append(ctx)
                        pb[i] = partb
                    if 0 <= i - 1 < ntile:
                        pb.pop(i - 1)()

            def qa_prod(t):
                b = nbank()
                z_matmuls(b, s_q, t * 128, 512)
                jq = npt()
                qn = pts_[jq][:, 0:512]
                pb_ = qk_norm(b, 0, 8, bc(gains[:, 0, :].unsqueeze(1), [128, 8, 64]), t, True, qn, [PT(jq)])
                return ((t, qn, jq), pb_)

            def qa_cons(t, qn, jq):
                transpose_to([qn[:, i * 128:(i + 1) * 128] for i in range(4)], [PT(jq)],
                             AV(QT0 + t * 512, 512), [Ar(QT0 + t * 512, 512)], "act")

            z_pipe(NT, qa_prod, qa_cons)

            def ka_prod(t):
                b = nbank()
                z_matmuls(b, s_kv, t * 128, 256)
                jk = npt()
                kn = pts_[jk][:, 0:128]
                pb_ = qk_norm(b, 0, 2, bc(gains[:, 1, :].unsqueeze(1), [128, 2, 64]), t, True, kn, [PT(jk)],
                              extra=lambda: CP("act", VA[:, t, :, 0:64], ps[b][:, 128:256].rearrange("p (k d) -> p k d", k=2),
                                               [PB(b)], [Ar(V0 + t * 130, 130)]))
                return ((t, kn, jk), pb_)

            def ka_cons(t, kn, jk):
                transpose_to([kn], [PT(jk)], AV(KT0 + t * 128, 128), [Ar(KT0 + t * 128, 128)], "act")

            z_pipe(NT, ka_prod, ka_cons)

            if dbg == 'stopP3':
                break
            R.phase('P3')
            units = []
            for n in range(NT):
                for kv in range(2):
                    kbs = [kb for kb in (n - 1, n, n + 1) if 0 <= kb < NT]
                    for ii, kb in enumerate(kbs):
                        units.append((n, kv, kb, ii == 0, ii == len(kbs) - 1))
            ctxs = {}
            grp_bo = {}
            due = []

            def a_s1(v):
                (n, kv, kb, first, last) = units[v]
                b = nbank()
                MM(ps[b][:, 0:512], arena[kv * 64:(kv + 1) * 64, KT0 + kb * 128: KT0 + (kb + 1) * 128],
                   arena[kv * 64:(kv + 1) * 64, QT0 + n * 512: QT0 + (n + 1) * 512], True, True,
                   [Ar(KT0 + kb * 128, 128), Ar(QT0 + n * 512, 512)], [PB(b)], sgc=True)
                jp = npt()
                ACTF(pts_[jp][:, 0:512], ps[b][:, 0:512], AF.Exp, [PB(b)], [PT(jp)])
                if kb != n:
                    w = 0 if kb < n else 1
                    GTT(pts_[jp][:, 0:512], pts_[jp][:, 0:512], maskA[:, w * 512:(w + 1) * 512], ALU.mult,
                        [PT(jp), r("maskA")], [PT(jp)])
                ctxs[v] = jp

            def a_s3(v, step):
                (n, kv, kb, first, last) = units[v]
                jp = ctxs.pop(v)
                if first:
                    grp_bo[(n, kv)] = nbank()
                bo = grp_bo[(n, kv)]
                for g in range(4):
                    MM(ps[bo][:, g * 65:(g + 1) * 65], pts_[jp][:, g * 128:(g + 1) * 128], VA[:, kb, kv, :],
                       (first and g == 0), last, [PT(jp), Ar(V0 + kb * 130, 130)], [PB(bo)], sgc=True)
                if not last:
                    return
                pv = ps[bo][:, 0:260].rearrange("p (g d) -> p g d", g=4)
                sd = nstat()
                TT(stats[:, sd, 0:4].unsqueeze(2), pv[:, :, 64:65], esink[:, kv * 4:(kv + 1) * 4].unsqueeze(2), ALU.add,
                   [PB(bo), r("esink")], [ST(sd)])
                RECIP(stats[:, sd, 0:4], stats[:, sd, 0:4], [ST(sd)], [ST(sd)])
                ja = nscr()
                at = scrb[ja][:, 0:256]
                TT(at.rearrange("p (g d) -> p g d", g=4), pv[:, :, 0:64], bc(stats[:, sd, 0:4].unsqueeze(2), [128, 4, 64]),
                   ALU.mult, [PB(bo), ST(sd)], [SC(ja)])

                def s5():
                    dst = AV(AA0, 8192).rearrange("p (c s) -> p c s", c=4)[:, kv * 2:kv * 2 + 2, n * 128:(n + 1) * 128]
                    dres = [("A", AA0 + (kv * 2 + c_) * 2048 + n * 128, AA0 + (kv * 2 + c_) * 2048 + (n + 1) * 128) for c_ in range(2)]
                    bt = nbank()
                    for i in range(2):
                        TR(psb[bt][:, i * 128:(i + 1) * 128], at[:, i * 128:(i + 1) * 128], ident[:], [SC(ja)], [PB(bt)])
                    CP("act", dst, psb[bt][:, 0:256].rearrange("p (c s) -> p c s", c=2), [PB(bt)], dres)
                due.append((step + 2, s5))

            nun = len(units)
            for step in range(nun + 5):
                if step < nun:
                    a_s1(step)
                if 2 <= step <= nun + 1:
                    a_s3(step - 2, step)
                for (ds, fn) in [d_ for d_ in due if d_[0] <= step]:
                    fn()
                due[:] = [d_ for d_ in due if d_[0] > step]
            assert not due

            VB = AV(V0, 2080).rearrange("p (t k d) -> p t k d", t=NT, k=2)
            VS = AV(V0 + 2080, 1950).rearrange("p (t k d) -> p t k d", t=15, k=2)
            for hp in range(4):
                MEMSET(VB[:, :, :, 64:65], 1.0, [Ar(V0, 2080)])
                if dbg == 'stopP4%d' % hp:
                    break
                R.phase('P4z%d' % hp)
                s_b = load_slot([
                    (lambda sl: sl[:].rearrange("p (k n) -> p k n", k=8)[:, :, 0:128],
                     win_d[l, :, 768 + hp * 128: 768 + (hp + 1) * 128].rearrange("(k p) n -> p k n", p=128)),
                    (lambda sl: sl[:].rearrange("p (k n) -> p k n", k=8)[:, :, 128:256],
                     win_d[l, :, 1280 + hp * 128: 1280 + (hp + 1) * 128].rearrange("(k p) n -> p k n", p=128)),
                    (lambda sl: sl[:].rearrange("p (k n) -> p k n", k=8)[:, :, 256:384],
                     win_d[l, :, 1792 + hp * 128: 1792 + (hp + 1) * 128].rearrange("(k p) n -> p k n", p=128)),
                ])
                s_bias = load_slot([(lambda sl: sl[:], bias_d[l, hp])])
                for q4 in range(4 if (B_BIAS_MULT and not DBG_NOEXP) else 0):
                    ACTF(slots[s_bias][:, q4 * 1024:(q4 + 1) * 1024], slots[s_bias][:, q4 * 1024:(q4 + 1) * 1024], AF.Exp, [SL(s_bias)], [SL(s_bias)])
                def b_prod(t):
                    b = nbank()
                    z_matmuls(b, s_b, t * 128, 384)
                    jq = npt()
                    qkn = pts_[jq][:, 0:256]
                    pb_ = qk_norm(b, 0, 4, gains[:, 2:6, :], t, False, qkn, [PT(jq)],
                                  extra=lambda: CP("act", VB[:, t, :, 0:64], ps[b][:, 256:384].rearrange("p (k d) -> p k d", k=2),
                                                   [PB(b)], [Ar(V0 + t * 130, 130)]))
                    return ((t, qkn, jq), pb_)

                def b_cons(t, qkn, jq):
                    bq = nbank()
                    TR(psb[bq][:, 0:128], qkn[:, 0:128], ident[:], [PT(jq)], [PB(bq)])
                    TR(psb[bq][:, 128:256], qkn[:, 128:256], ident[:], [PT(jq)], [PB(bq)])
                    lo = QT0 + t * 128
                    dst = arena[:, lo: lo + 2 * (KT0 - QT0)].rearrange("p (c s) -> p c s", c=2)[:, :, 0:128]
                    CP("act" if t % 2 == 0 else "dve", dst, psb[bq][:, 0:256].rearrange("p (c s) -> p c s", c=2), [PB(bq)],
                       [Ar(QT0 + t * 128, 128), Ar(KT0 + t * 128, 128)])

                z_pipe(NT, b_prod, b_cons)
                if VS_DMA:
                    DMA("sp", arena[0:64, V0 + 2080: V0 + 2080 + 1950], arena[64:128, V0: V0 + 1950],
                        [Ar(V0, 2080)], [Ar(V0 + 2080, 1950)], "vs")
                    DMA("sp", arena[64:128, V0 + 2080: V0 + 2080 + 1950], arena[0:64, V0 + 130: V0 + 2080],
                        [Ar(V0, 2080)], [Ar(V0 + 2080, 1950)], "vs")
                else:
                    MEMSET(VS[:, :, :, 64:65], 1.0, [Ar(V0 + 2080, 1950)])
                    for t in range(15):
                        b = nbank()
                        for k in range(8):
                            MM(ps[b][:, 0:128], hT_ap(k, 64 + t * 128, 128), slots[s_b][:, k * 512 + 256: k * 512 + 384],
                               (k == 0), (k == 7), hT_res(64 + t * 128, 128) + [SL(s_b)], [PB(b)])
                        CP("act", VS[:, t, :, 0:64], ps[b][:, 0:128].rearrange("p (k d) -> p k d", k=2),
                           [PB(b)], [Ar(V0 + 2080 + t * 130, 130)])
                if dbg == 'stopP5%d' % hp:
                    break
                R.phase('P5a%d' % hp)
                btr = 7
                bctx = {}

                bk = {}

                def b_qk(rw, hh):
                    rs = min(max(rw - 4, 0), 24)
                    j0 = rs - rw + 7
                    if hh == 0:
                        bk[rw] = nbank()
                    b = bk[rw]
                    for i in range(4):
                        a_ = rs + 2 * i
                        MM(ps[b][:, hh * 256 + i * 64: hh * 256 + (i + 1) * 64],
                           arena[hh * 64:(hh + 1) * 64, KT0 + a_ * 64: KT0 + a_ * 64 + 128],
                           arena[hh * 64:(hh + 1) * 64, QT0 + rw * 64: QT0 + (rw + 1) * 64], (hh == 0 and i == 0), B_BIAS_MULT,
                           [Ar(KT0 + a_ * 64, 128), Ar(QT0 + rw * 64, 64)], [PB(b)], sgc=True)
                    if not B_BIAS_MULT:
                        boff = hh * 2048 + j0 * 256
                        MM(ps[b][:, hh * 256:(hh + 1) * 256], ident[:], slots[s_bias][:, boff:boff + 256], False, True,
                           [r("ident"), SL(s_bias)], [PB(b)], sgc=True)

                def b_s1a(rw):
                    b_qk(rw, 0)

                def b_sep():
                    bd = nbank()
                    MM(ps[bd][:, 0:128], ident[:], ident[:], True, True, [r("ident")], [PB(bd)], sgc=True)

                def b_s1b(rw):
                    b_qk(rw, 1)
                    rs = min(max(rw - 4, 0), 24)
                    j0 = rs - rw + 7
                    b = bk.pop(rw)
                    jp = npt()
                    ACTF(pts_[jp][:, 0:512], ps[b][:, 0:512], AF.Exp, [PB(b)], [PT(jp)])
                    if B_BIAS_MULT and not DBG_NOMULT:
                        ebv = slots[s_bias][:, :].rearrange("p (h x) -> p h x", h=2)[:, :, j0 * 256:(j0 + 1) * 256]
                        (TT if B_MULT_ENG == "dve" else GTT)(
                            pts_[jp][:, 0:512].rearrange("p (h x) -> p h x", h=2), pts_[jp][:, 0:512].rearrange("p (h x) -> p h x", h=2),
                            ebv, ALU.mult, [PT(jp), SL(s_bias)], [PT(jp)])
                    bctx[rw] = (rs, jp)

                def b_s3(rw):
                    (rs, jp) = bctx[rw]
                    bo = nbank()
                    for hh in range(2):
                        for i in range(4):
                            a_ = rs + 2 * i
                            if a_ % 2 == 0:
                                vap = VB[:, a_ // 2, hh, :]
                                vres = Ar(V0 + (a_ // 2) * 130, 130)
                            else:
                                vap = VS[:, (a_ - 1) // 2, hh, :]
                                vres = Ar(V0 + 2080 + ((a_ - 1) // 2) * 130, 130)
                            MM(ps[bo][0:64, hh * 65:(hh + 1) * 65], pts_[jp][:, hh * 256 + i * 64: hh * 256 + (i + 1) * 64], vap,
                               (i == 0), (i == 3), [PT(jp), vres], [PB(bo)], sgc=True)
                    pv = ps[bo][0:64, 0:130].rearrange("p (g d) -> p g d", g=2)
                    sd = nstat()
                    RECIP(stats[0:64, sd, 0:2].unsqueeze(2), pv[:, :, 64:65], [PB(bo)], [ST(sd)])
                    ja = nscr()
                    at = scrb[ja][0:64, 0:128]
                    TT(at.rearrange("p (g d) -> p g d", g=2), pv[:, :, 0:64], bc(stats[0:64, sd, 0:2].unsqueeze(2), [64, 2, 64]),
                       ALU.mult, [PB(bo), ST(sd)], [SC(ja)])
                    bctx[rw] = (at, ja)

                def b_s5(rw):
                    (at, ja) = bctx.pop(rw)
                    rr = rw % 8
                    TR(psb[btr][:, rr * 64:(rr + 1) * 64], at, ident[0:64, 0:64], [SC(ja)], [PB(btr)])
                    if rr == 7:
                        dlo = AB0 + hp * 2048 + (rw // 8) * 512
                        CP("act", AV(dlo, 512), psb[btr][:, 0:512], [PB(btr)], [Ar(dlo, 512)])

                for step in range(32 + 3):
                    if step < 32:
                        b_s1a(step)
                    if 2 <= step <= 33:
                        b_s3(step - 2)
                    elif step < 32 and B_BIAS_MULT:
                        b_sep()
                    if step < 32:
                        b_s1b(step)
                    if 3 <= step:
                        b_s5(step - 3)

            if dbg == 'stopP6':
                break
            R.phase('P6')
            for c in range(8):
                s_w = load_slot([
                    (lambda sl: sl[:, 0:1024].rearrange("p (k n) -> p k n", k=8),
                     win_d[l, :, 2304 + c * 128: 2304 + (c + 1) * 128].rearrange("(k p) n -> p k n", p=128)),
                    (lambda sl: sl[:, 1024:2048].rearrange("p (k n) -> p k n", k=8),
                     win_d[l, :, 3328 + c * 128: 3328 + (c + 1) * 128].rearrange("(k p) n -> p k n", p=128)),
                    (lambda sl: sl[:, 2048:2560].rearrange("p (k n) -> p k n", k=4),
                     wpa_d[l, :, c * 128:(c + 1) * 128].rearrange("(k p) n -> p k n", p=128)),
                    (lambda sl: sl[:, 2560:3072].rearrange("p (k n) -> p k n", k=4),
                     wpb_d[l, :, c * 128:(c + 1) * 128].rearrange("(k p) n -> p k n", p=128)),
                ])
                for tg in range(4):
                    bga, bgb, bya, byb = nbank(), nbank(), nbank(), nbank()
                    for (bb, woff) in ((bga, 0), (bgb, 1024)):
                        for k in range(8):
                            MM(ps[bb][:, 0:512], slots[s_w][:, woff + k * 128: woff + (k + 1) * 128], hT_ap(k, tg * 512, 512),
                               (k == 0), (k == 7), hT_res(tg * 512, 512) + [SL(s_w)], [PB(bb)])
                    for (bb, woff, a0) in ((bya, 2048, AA0), (byb, 2560, AB0)):
                        for k in range(4):
                            lo = a0 + k * 2048 + tg * 512
                            MM(ps[bb][:, 0:512], slots[s_w][:, woff + k * 128: woff + (k + 1) * 128], arena[:, lo:lo + 512],
                               (k == 0), (k == 3), [("A", lo, lo + 512), SL(s_w)], [PB(bb)])
                    j1, j2 = nscr(), nscr()
                    ACTF(scr[j1][:, 0:512], ps[bga][:, 0:512], AF.Sigmoid, [PB(bga)], [SC(j1)])
                    ACTF(scr[j2][:, 0:512], ps[bgb][:, 0:512], AF.Sigmoid, [PB(bgb)], [SC(j2)])
                    TT(scr[j1][:, 0:512], scr[j1][:, 0:512], ps[bya][:, 0:512], ALU.mult, [SC(j1), PB(bya)], [SC(j1)])
                    TT(scr[j2][:, 0:512], scr[j2][:, 0:512], ps[byb][:, 0:512], ALU.mult, [SC(j2), PB(byb)], [SC(j2)])
                    mlo = MX0 + c * 2048 + tg * 512
                    TT(AV(mlo, 512), scr[j1][:, 0:512], scr[j2][:, 0:512], ALU.add, [SC(j1), SC(j2)], [Ar(mlo, 512)])

            R.phase('P7')
            for hf in range(2):
                s_o = load_slot([(lambda sl: sl[:].rearrange("p (k n) -> p k n", k=8),
                                  wout_d[l, :, hf * 512:(hf + 1) * 512].rearrange("(k p) n -> p k n", p=128))])
                for t in range(NT):
                    b = nbank()
                    for k in range(8):
                        lo = MX0 + k * 2048 + t * 128
                        MM(ps[b][:, 0:512], arena[:, lo:lo + 128], slots[s_o][:, k * 512:(k + 1) * 512],
                           (k == 0), (k == 7), [("A", lo, lo + 128), SL(s_o)], [PB(b)])
                    xs = x_sb[:, t, hf * 512:(hf + 1) * 512]
                    TT(xs, xs, ps[b][:, 0:512], ALU.add, [X(t), PB(b)], [X(t)])

            if dbg == 'attn':
                dbg_d = dt('dbg_ab', [128, 16384], BF16, kind='ExternalOutput').ap()
                DMA('sp', dbg_d, AV(AA0, 16384), [Ar(AA0, 16384)], [], 'dbgout')
                continue
            R.phase('P8')
            norm_phase(l, fng_d, 1)

            R.phase('P9')
            f0 = 0
            for fgn in (8, 8, 6):
                for fp in range(fgn // 2):
                    fa = f0 + fp * 2
                    s_gu = load_slot([
                        (lambda sl: sl[:, 0:2048].rearrange("p (k n) -> p k n", k=8),
                         wg_d[l, :, fa * 128:(fa + 2) * 128].rearrange("(k p) n -> p k n", p=128)),
                        (lambda sl: sl[:, 2048:4096].rearrange("p (k n) -> p k n", k=8),
                         wu_d[l, :, fa * 128:(fa + 2) * 128].rearrange("(k p) n -> p k n", p=128)),
                    ])
                    for q2 in range(2):
                        fl = fp * 2 + q2
                        for tg in range(4):
                            bg, bu = nbank(), nbank()
                            for (bb, woff) in ((bg, 0), (bu, 2048)):
                                for k in range(8):
                                    wlo = woff + k * 256 + q2 * 128
                                    MM(ps[bb][:, 0:512], slots[s_gu][:, wlo:wlo + 128], hT_ap(k, tg * 512, 512),
                                       (k == 0), (k == 7), hT_res(tg * 512, 512) + [SL(s_gu)], [PB(bb)])
                            j1 = nscr()
                            ACTF(scr[j1][:, 0:512], ps[bg][:, 0:512], AF.Silu, [PB(bg)], [SC(j1)])
                            ulo = UT0 + fl * 2048 + tg * 512
                            TT(AV(ulo, 512), scr[j1][:, 0:512], ps[bu][:, 0:512], ALU.mult, [SC(j1), PB(bu)], [Ar(ulo, 512)])
                dsl = []
                for j in range(0, fgn, 4):
                    nch = min(4, fgn - j)
                    s_d = load_slot([(lambda sl, nch=nch: sl[:, 0:nch * 1024].rearrange("p (k n) -> p k n", k=nch),
                                      wd_d[l, (f0 + j) * 128:(f0 + j + nch) * 128, :].rearrange("(k p) n -> p k n", p=128))])
                    dsl.append(s_d)
                for t in range(NT):
                    for hf in range(2):
                        b = nbank()
                        for j in range(fgn):
                            s_d = dsl[j // 4]
                            lo = UT0 + j * 2048 + t * 128
                            wlo = (j % 4) * 1024 + hf * 512
                            MM(ps[b][:, 0:512], arena[:, lo:lo + 128], slots[s_d][:, wlo:wlo + 512],
                               (j == 0), (j == fgn - 1), [("A", lo, lo + 128), SL(s_d)], [PB(b)])
                        xs = x_sb[:, t, hf * 512:(hf + 1) * 512]
                        TT(xs, xs, ps[b][:, 0:512], ALU.add, [X(t), PB(b)], [X(t)])
                f0 += fgn

        for t in range(NT):
            DMA("sp", out_d[t * 128:(t + 1) * 128, :], x_sb[:, t, :], [X(t)], [], "out")
        R.phase('end')
        global PHASES
        PHASES = list(R.phases)
        R.emit(final_waits=[(R.dmasems["out"], R.dmacnt["out"])])
    return nc


def _consts():
    ident = np.eye(128, dtype=np.float32)
    j = np.arange(128)[:, None]
    rq = np.arange(128)[None, :]
    m_prev = np.where(j >= rq, 1.0, 0.0).astype(np.float32)
    m_next = np.where(j <= rq, 1.0, 0.0).astype(np.float32)
    mask = np.concatenate([np.tile(m_prev, (1, 4)), np.tile(m_next, (1, 4))], axis=1)
    half = 32
    inv = (10000.0 ** (-np.arange(half, dtype=np.float32) / np.float32(half))).astype(np.float32)
    invf = np.tile(inv[None, :], (128, 1)).astype(np.float32)
    return ident, np.ascontiguousarray(mask), invf


def _bias_tiles(rel_bias):
    L = rel_bias.shape[0]
    o = np.arange(2)
    j0 = np.arange(8)
    i = np.arange(4)
    DR = np.minimum(j0[None, :, None] + 2 * i[None, None, :] + o[:, None, None], 14)
    kc = np.arange(64)[:, None]
    qc = np.arange(64)[None, :]
    DC = np.clip(kc - qc + 15, 0, 30)
    cs = np.clip(qc - 8, 0, 48)
    valid = (kc >= cs) & (kc < cs + 16)
    g = rel_bias[:, :, DR[:, None, :, :, None], DC[None, :, None, None, :]]
    g = np.where(valid[None, None, None, :, None, None, :], g, np.float32(NEG)).astype(np.float32)
    g = g.reshape(L, 4, 2, 128, 8, 4, 64).transpose(0, 1, 3, 2, 4, 5, 6)
    return np.ascontiguousarray(g.reshape(L, 4, 128, 4096))


def _perm_w_in(w_in):
    L = w_in.shape[0]
    qa = w_in[:, :, 0:512].reshape(L, D, 2, 4, 64).transpose(0, 1, 3, 2, 4).reshape(L, D, 512)
    return np.ascontiguousarray(np.concatenate([qa, w_in[:, :, 512:]], axis=2))


def _gT(g):
    return np.ascontiguousarray(g.reshape(g.shape[0], 8, 128).transpose(0, 2, 1))


_NC_CACHE = {}


def _get_nc(nl):
    if nl not in _NC_CACHE:
        _NC_CACHE[nl] = build(nl)
    return _NC_CACHE[nl]


def kernel(x, positions, attn_norm_g, w_in, q_norm_a, k_norm_a, sink_a, q_norm_b, k_norm_b,
           rel_bias_b, w_proj_a, w_proj_b, w_out, ffn_norm_g, w_gate, w_up, w_down):
    f = lambda a: np.ascontiguousarray(np.asarray(a, dtype=np.float32))
    x = f(x)
    pos = np.ascontiguousarray(np.asarray(positions, dtype=np.int32).reshape(NT, 128).T)
    ident, mask, invf = _consts()
    w_in_p = _perm_w_in(f(w_in))
    bias_t = _bias_tiles(f(rel_bias_b))
    per_layer = {
        "attn_norm_g": _gT(f(attn_norm_g)), "w_in": w_in_p, "q_norm_a": f(q_norm_a), "k_norm_a": f(k_norm_a),
        "sink_a": f(sink_a), "q_norm_b": f(q_norm_b), "k_norm_b": f(k_norm_b), "bias_t": bias_t,
        "w_proj_a": f(w_proj_a), "w_proj_b": f(w_proj_b), "w_out": f(w_out), "ffn_norm_g": _gT(f(ffn_norm_g)),
        "w_gate": f(w_gate), "w_up": f(w_up), "w_down": f(w_down),
    }
    consts = {"pos": pos, "ident": ident, "mask_a": mask, "invf": invf}
    B = x.shape[0]
    if FUSED:
        nc = _get_nc(DEPTH)
        in_maps = []
        for b in range(B):
            m = {"x": x[b]}
            m.update(consts)
            m.update(per_layer)
            in_maps.append(m)
        res = run_bass_kernel_spmd(nc, in_maps, core_ids=list(range(B)))
        return np.stack([res.results[b]["out"] for b in range(B)], axis=0)
    nc = _get_nc(1)
    cur = [x[b] for b in range(B)]
    for l in range(DEPTH):
        in_maps = []
        for b in range(B):
            m = {"x": cur[b]}
            m.update(consts)
            m.update({k: np.ascontiguousarray(v[l:l + 1]) for k, v in per_layer.items()})
            in_maps.append(m)
        res = run_bass_kernel_spmd(nc, in_maps, core_ids=list(range(B)))
        cur = [np.ascontiguousarray(res.results[b]["out"]) for b in range(B)]
    return np.stack(cur, axis=0)
```

```python
import numpy as np
from contextlib import ExitStack
import concourse.bass as bass
import concourse.mybir as mybir
from concourse.bass_utils import run_bass_kernel_spmd

F32 = mybir.dt.float32
BF16 = mybir.dt.bfloat16
I32 = mybir.dt.int32
ALU = mybir.AluOpType
AF = mybir.ActivationFunctionType
AX = mybir.AxisListType

S = 2048
D = 1024
NT = 16
FF = 2816
NFC = 22
INC = 4352
NEG = -30000.0
EPS = 1e-6
DEPTH = 4
FUSED = True
ROPE2_ENG = "pool"
VS_DMA = True
NPT = 7
NSCR = 6
ZDEPTH = 4
QDEPTH = 2
B_MULT_ENG = "dve"
A_MASK_ENG = "dve"
B_BIAS_MULT = True
DBG_NOEXP = False
DBG_NOMULT = False

ENGS = ["pe", "act", "dve", "pool", "sp"]


class Op:
    __slots__ = ("eng", "fn", "deps", "marked", "count", "sem", "is_dma", "idx")

    def __init__(self, eng, fn):
        self.eng = eng
        self.fn = fn
        self.deps = []
        self.marked = False
        self.count = 0
        self.sem = None
        self.is_dma = False


class Rec:
    def __init__(self, nc, stack):
        self.nc = nc
        self.stack = stack
        self.ops = {e: [] for e in ENGS}
        self.hist = {}
        self.names = {}
        self.engsem = {e: stack.enter_context(nc.semaphore("s_" + e)) for e in ENGS}
        self.dmasems = {}
        self.dmacnt = {}
        self.nops = 0
        self.phases = []

    def phase(self, name):
        self.phases.append((name, len(self.ops['pe'])))

    def res(self, name):
        if name not in self.names:
            self.names[name] = len(self.names)
        i = self.names[name]
        return ("N", i, i + 1)

    def _sem_for(self, key):
        if key not in self.dmasems:
            self.dmasems[key] = self.stack.enter_context(self.nc.semaphore("d_%d" % len(self.dmasems)))
            self.dmacnt[key] = 0
        return self.dmasems[key]

    def add(self, eng, fn, reads=(), writes=(), dma=None):
        op = Op(eng, fn)
        op.idx = self.nops
        self.nops += 1
        deps = {}
        for (sp, lo, hi) in reads:
            for rec in self.hist.get(sp, ()):
                if rec[0] < hi and lo < rec[1] and (rec[3] or (sp == "P" and rec[2].eng != eng)):
                    deps[id(rec[2])] = rec[2]
        for (sp, lo, hi) in writes:
            for rec in self.hist.get(sp, ()):
                if rec[0] < hi and lo < rec[1]:
                    deps[id(rec[2])] = rec[2]
        for d in deps.values():
            if d.eng == "pe" and eng == "pe":
                continue
            d.marked = True
            op.deps.append(d)
        if dma is not None:
            op.is_dma = True
            op.sem = self._sem_for(dma)
            self.dmacnt[dma] += 16
            op.count = self.dmacnt[dma]
        for (sp, lo, hi) in writes:
            h = self.hist.setdefault(sp, [])
            h[:] = [r for r in h if not (lo <= r[0] and r[1] <= hi)]
            h.append([lo, hi, op, True])
        for (sp, lo, hi) in reads:
            h = self.hist.setdefault(sp, [])
            if not op.is_dma:
                h[:] = [r for r in h if not ((not r[3]) and r[0] == lo and r[1] == hi
                                             and r[2].eng == eng and not r[2].is_dma)]
            h.append([lo, hi, op, False])
        self.ops[eng].append(op)
        return op

    def emit(self, final_waits=()):
        nc = self.nc
        for e in ENGS:
            c = 0
            for op in self.ops[e]:
                if op.is_dma:
                    continue
                if op.marked:
                    c += 1
                    op.count = c
                    op.sem = self.engsem[e]
        ops = self.ops
        engsem = self.engsem

        def make_body(e):
            def body(eng):
                waited = {}
                for op in ops[e]:
                    need = {}
                    for d in op.deps:
                        k = id(d.sem)
                        if k not in need or need[k][1] < d.count:
                            need[k] = (d.sem, d.count)
                    for k, (sem, val) in need.items():
                        if waited.get(k, 0) >= val:
                            continue
                        eng.wait_ge(sem, val)
                        waited[k] = val
                    ins = op.fn(eng)
                    if op.is_dma:
                        ins.then_inc(op.sem, 16)
                    elif op.marked:
                        ins.then_inc(engsem[e], 1)
                if e == "sp":
                    for (sem, val) in final_waits:
                        eng.wait_ge(sem, val)
            return body

        with nc.Block() as block:
            block.tensor(make_body("pe"))
            block.scalar(make_body("act"))
            block.vector(make_body("dve"))
            block.gpsimd(make_body("pool"))
            block.sync(make_body("sp"))


def bc(ap, shape):
    return ap.to_broadcast(list(shape))


def build(nl, dbg=None):
    nc = bass.Bass("TRN2", target_bir_lowering=False)
    dt = nc.dram_tensor
    x_d = dt("x", [S, D], F32, kind="ExternalInput").ap()
    pos_d = dt("pos", [128, NT], I32, kind="ExternalInput").ap()
    ang_d = dt("attn_norm_g", [nl, 128, 8], F32, kind="ExternalInput").ap()
    win_d = dt("w_in", [nl, D, INC], F32, kind="ExternalInput").ap()
    qna_d = dt("q_norm_a", [nl, 64], F32, kind="ExternalInput").ap()
    kna_d = dt("k_norm_a", [nl, 64], F32, kind="ExternalInput").ap()
    snk_d = dt("sink_a", [nl, 8], F32, kind="ExternalInput").ap()
    qnb_d = dt("q_norm_b", [nl, 64], F32, kind="ExternalInput").ap()
    knb_d = dt("k_norm_b", [nl, 64], F32, kind="ExternalInput").ap()
    bias_d = dt("bias_t", [nl, 4, 128, 4096], F32, kind="ExternalInput").ap()
    wpa_d = dt("w_proj_a", [nl, 512, D], F32, kind="ExternalInput").ap()
    wpb_d = dt("w_proj_b", [nl, 512, D], F32, kind="ExternalInput").ap()
    wout_d = dt("w_out", [nl, D, D], F32, kind="ExternalInput").ap()
    fng_d = dt("ffn_norm_g", [nl, 128, 8], F32, kind="ExternalInput").ap()
    wg_d = dt("w_gate", [nl, D, FF], F32, kind="ExternalInput").ap()
    wu_d = dt("w_up", [nl, D, FF], F32, kind="ExternalInput").ap()
    wd_d = dt("w_down", [nl, FF, D], F32, kind="ExternalInput").ap()
    ident_d = dt("ident", [128, 128], F32, kind="ExternalInput").ap()
    mask_d = dt("mask_a", [128, 1024], F32, kind="ExternalInput").ap()
    invf_d = dt("invf", [128, 32], F32, kind="ExternalInput").ap()
    out_d = dt("out", [S, D], F32, kind="ExternalOutput").ap()

    with ExitStack() as st:
        R = Rec(nc, st)
        sbt = lambda n, s, d: st.enter_context(nc.sbuf_tensor(n, s, d))
        x_sb = sbt("x_sb", [128, NT, D], F32)
        AR_N = 47104
        arena = sbt("arena", [128, AR_N], BF16)
        slots = [sbt("slot%d" % i, [128, 4096], BF16) for i in range(3)]
        scr = [sbt("scr%d" % i, [128, 512], F32) for i in range(NSCR)]
        scrb = [s_.bitcast(BF16) for s_ in scr]
        pts_ = [sbt("pt%d" % i, [128, 512], BF16) for i in range(NPT)]
        gT = sbt("gT", [128, 2, 8], F32)
        ident = sbt("ident_sb", [128, 128], BF16)
        maskA = sbt("maskA_sb", [128, 1024], BF16)
        cos_t = sbt("cos_t", [128, NT, 32], F32)
        sin_t = sbt("sin_t", [128, NT, 32], F32)
        gains = sbt("gains", [128, 6, 64], F32)
        epsb = sbt("epsb", [128, 2], F32)
        esink = sbt("esink", [128, 8], F32)
        stats = sbt("stats", [128, 15, 16], F32)
        ps = [st.enter_context(nc.psum_tensor("ps%d" % i, [128, 512], F32)) for i in range(8)]
        psb = [p_.bitcast(BF16) for p_ in ps]

        HT0 = 0
        QT0 = 16384
        KT0 = QT0 + 8192
        V0 = KT0 + 2048
        AA0 = V0 + 4096
        AB0 = AA0 + 8192
        MX0 = QT0
        UT0 = QT0
        assert AB0 + 8192 == AR_N

        def AV(lo, n):
            return arena[:, lo:lo + n]

        def Ar(lo, n):
            return ("A", lo, lo + n)

        def X(t):
            return ("X", t, t + 1)

        def PB(b):
            return ("P", b, b + 1)

        def SL(s_):
            return ("W", s_, s_ + 1)

        def SC(i):
            return ("S", i, i + 1)

        r = R.res
        state = {"bank": 0, "scr": 0, "stat": 0, "slot": 0, "pt": 0}

        def nbank():
            b = state["bank"]
            state["bank"] = (b + 1) % 7
            return b

        def nscr():
            i = state["scr"]
            state["scr"] = (i + 1) % NSCR
            return i

        def nstat():
            i = state["stat"]
            state["stat"] = (i + 1) % 15
            return i

        def nslot():
            i = state["slot"]
            state["slot"] = (i + 1) % 3
            return i

        def ST(i):
            return ("T", i, i + 1)

        def PT(i):
            return ("Q", i, i + 1)

        def npt():
            i = state["pt"]
            state["pt"] = (i + 1) % NPT
            return i

        def MM(out, lhsT, rhs, start, stop, reads, writes, sgc=False):
            if sgc:
                R.add("pe", lambda e: e.matmul(out, lhsT=lhsT, rhs=rhs, start=start, stop=stop, skip_group_check=True),
                      reads, writes)
            else:
                R.add("pe", lambda e: e.matmul(out, lhsT=lhsT, rhs=rhs, start=start, stop=stop), reads, writes)

        def TR(out, in_, idn, reads, writes):
            R.add("pe", lambda e: e.transpose(out=out, in_=in_, identity=idn), reads + [r("ident")], writes)

        def ACTF(out, in_, func, reads, writes, scale=None, accum=None, bias=None):
            kw = {}
            if bias is not None:
                kw["bias"] = bias
            if scale is not None:
                kw["scale"] = scale
            if accum is not None:
                kw["accum_out"] = accum
            R.add("act", lambda e: e.activation(out=out, in_=in_, func=func, **kw), reads, writes)

        def TT(out, in0, in1, op, reads, writes):
            R.add("dve", lambda e: e.tensor_tensor(out=out, in0=in0, in1=in1, op=op), reads, writes)

        def PTT(out, in0, in1, op, reads, writes):
            R.add(ROPE2_ENG, lambda e: e.tensor_tensor(out=out, in0=in0, in1=in1, op=op), reads, writes)

        def GTT(out, in0, in1, op, reads, writes):
            R.add("pool", lambda e: e.tensor_tensor(out=out, in0=in0, in1=in1, op=op), reads, writes)

        def TS(out, in0, s1, s2, op0, op1, reads, writes):
            if s2 is None:
                R.add("dve", lambda e: e.tensor_scalar(out=out, in0=in0, scalar1=s1, scalar2=None, op0=op0), reads, writes)
            else:
                R.add("dve", lambda e: e.tensor_scalar(out=out, in0=in0, scalar1=s1, scalar2=s2, op0=op0, op1=op1), reads, writes)

        def STT(out, in0, scalar, in1, op0, op1, reads, writes):
            R.add("dve", lambda e: e.scalar_tensor_tensor(out=out, in0=in0, scalar=scalar, in1=in1, op0=op0, op1=op1), reads, writes)

        def CP(eng, out, in_, reads, writes):
            if eng == "act":
                R.add("act", lambda e: e.copy(out=out, in_=in_), reads, writes)
            else:
                R.add("dve", lambda e: e.tensor_copy(out=out, in_=in_), reads, writes)

        def DMA(eng, out, in_, reads, writes, key):
            R.add(eng, lambda e: e.dma_start(out=out, in_=in_), reads, writes, dma=key)

        def MEMSET(ap, val, writes):
            R.add("dve", lambda e: e.memset(ap, val), (), writes)

        def RECIP(out, in_, reads, writes):
            R.add("dve", lambda e: e.reciprocal(out=out, in_=in_), reads, writes)

        def RSUM(out, in_, reads, writes):
            R.add("dve", lambda e: e.reduce_sum(out=out, in_=in_, axis=AX.X), reads, writes)

        for t in range(NT):
            DMA("sp", x_sb[:, t, :], x_d[t * 128:(t + 1) * 128, :], [], [X(t)], "x%d" % t)
        DMA("pool", ident[:], ident_d, [], [r("ident")], "c_ident")
        DMA("pool", maskA[:], mask_d, [], [r("maskA")], "c_mask")
        pos_i = scr[2].bitcast(I32)[:, 0:NT]
        DMA("sp", pos_i, pos_d, [], [SC(2)], "c_pos")
        invf = scr[5][:, 0:32]
        DMA("sp", invf, invf_d, [], [SC(5)], "c_invf")
        MEMSET(epsb[:, 0:1], EPS, [r("epsb")])
        MEMSET(epsb[:, 1:2], 64.0 * EPS, [r("epsb")])
        posf = stats[:, 14, :]
        CP("dve", posf, pos_i, [SC(2)], [ST(14)])
        TWO_PI = 2.0 * np.pi
        kf = scr[4][:, 0:512].rearrange("p (a b) -> p a b", a=NT)
        ki = scr[3].bitcast(I32)[:, 0:512].rearrange("p (a b) -> p a b", a=NT)
        for (tab, shift, nm) in ((sin_t, 0.0, "tab_s"), (cos_t, 0.5 * np.pi, "tab_c")):
            TT(tab[:], bc(posf.unsqueeze(2), [128, NT, 32]), bc(invf.unsqueeze(1), [128, NT, 32]), ALU.mult,
               [ST(14), SC(5)], [r(nm)])
            if shift != 0.0:
                TS(tab[:], tab[:], float(shift), None, ALU.add, None, [r(nm)], [r(nm)])
            TS(kf, tab[:], float(1.0 / TWO_PI), None, ALU.mult, None, [r(nm)], [SC(4)])
            CP("dve", ki, kf, [SC(4)], [SC(3)])
            CP("dve", kf, ki, [SC(3)], [SC(4)])
            STT(tab[:], kf, float(-TWO_PI), tab[:], ALU.mult, ALU.add, [SC(4), r(nm)], [r(nm)])
            TS(tab[:], tab[:], float(np.pi), float(-np.pi), ALU.min, ALU.max, [r(nm)], [r(nm)])
            ACTF(tab[:], tab[:], AF.Sin, [r(nm)], [r(nm)])
        TABS = [r("tab_s"), r("tab_c")]

        def load_slot(parts):
            s_ = nslot()
            for (dst_fn, src) in parts:
                DMA("pool", dst_fn(slots[s_]), src, [], [SL(s_)], "slot%d" % s_)
            return s_

        def rms_stats(ss_ap, out_ap, res_in, res_out, mult, which):
            ACTF(out_ap, ss_ap, AF.Ln, [res_in, r("epsb")], [res_out], scale=float(mult), bias=epsb[:, which:which + 1])
            ACTF(out_ap, out_ap, AF.Exp, [res_out], [res_out], scale=-0.5)

        def hT_ap(k, lo, n):
            return arena[:, HT0 + k * 2048 + lo: HT0 + k * 2048 + lo + n]

        def hT_res(lo, n):
            return [("A", HT0 + k * 2048 + lo, HT0 + k * 2048 + lo + n) for k in range(8)]

        def norm_phase(l, g_d, gi):
            DMA("sp", gT[:, gi, :], g_d[l], [], [r("gT%d" % gi)], "gT%d" % gi)
            si = nstat()
            so = nstat()
            for t in range(NT):
                j = nscr()
                ACTF(scrb[j][:, 0:1024], x_sb[:, t, :], AF.Square, [X(t)], [SC(j), ST(si)], accum=stats[:, si, t:t + 1])
            rms_stats(stats[:, si, :], stats[:, so, :], ST(si), ST(so), 1.0 / D, 0)
            for t in range(NT):
                j = nscr()
                ACTF(scrb[j][:, 0:1024], x_sb[:, t, :], AF.Copy, [X(t), ST(so)], [SC(j)], scale=stats[:, so, t:t + 1])
                b = nbank()
                for c in range(8):
                    TR(psb[b][:, c * 128:(c + 1) * 128], scrb[j][:, c * 128:(c + 1) * 128], ident[:], [SC(j)], [PB(b)])
                dst = AV(HT0, 16384).rearrange("p (c s) -> p c s", c=8)[:, :, t * 128:(t + 1) * 128]
                src = psb[b][:, 0:1024].rearrange("p (c s) -> p c s", c=8)
                TT(dst, src, bc(gT[:, gi, :].unsqueeze(2), [128, 8, 128]), ALU.mult, [PB(b), r("gT%d" % gi)], hT_res(t * 128, 128))

        def z_matmuls(b, s_, tok_lo, ncols):
            for k in range(8):
                MM(ps[b][:, 0:ncols], hT_ap(k, tok_lo, 128), slots[s_][:, k * 512: k * 512 + ncols],
                   (k == 0), (k == 7), hT_res(tok_lo, 128) + [SL(s_)], [PB(b)])

        def qk_norm(b, col_lo, nh, gain_ap, t, rope, out_ap, out_res, extra=None, fixed=None):
            n = nh * 64
            zin = ps[b][:, col_lo:col_lo + n]
            j1 = nscr() if fixed is None else fixed[0]
            si = nstat()
            so = nstat()
            ACTF(scr[j1][:, 0:n], zin, AF.Square, [PB(b)], [SC(j1)])
            if extra is not None:
                extra()
            RSUM(stats[:, si, 0:nh], scr[j1][:, 0:n].rearrange("p (h d) -> p h d", h=nh), [SC(j1)], [ST(si)])

            def partb():
                rms_stats(stats[:, si, 0:nh], stats[:, so, 0:nh], ST(si), ST(so), 1.0, 1)
                t1 = scr[j1][:, 0:n].rearrange("p (h d) -> p h d", h=nh)
                TT(t1, zin.rearrange("p (h d) -> p h d", h=nh), bc(stats[:, so, 0:nh].unsqueeze(2), [128, nh, 64]), ALU.mult,
                   [PB(b), ST(so)], [SC(j1)])
                o3 = out_ap.rearrange("p (h d) -> p h d", h=nh)
                if not rope:
                    TT(o3, t1, gain_ap, ALU.mult, [SC(j1), r("gains")], out_res)
                    return
                GTT(t1, t1, gain_ap, ALU.mult, [SC(j1), r("gains")], [SC(j1)])
                j3 = nscr() if fixed is None else fixed[1]
                tmp = scr[j3][:, 0:n].rearrange("p (h d) -> p h d", h=nh)
                x1 = t1[:, :, 0:32]
                x2 = t1[:, :, 32:64]
                cb = bc(cos_t[:, t, :].unsqueeze(1), [128, nh, 32])
                sb_ = bc(sin_t[:, t, :].unsqueeze(1), [128, nh, 32])
                TT(tmp[:, :, 0:32], x1, cb, ALU.mult, [SC(j1)] + TABS, [SC(j3)])
                TT(tmp[:, :, 32:64], x2, sb_, ALU.mult, [SC(j1)] + TABS, [SC(j3)])
                TT(o3[:, :, 0:32], tmp[:, :, 0:32], tmp[:, :, 32:64], ALU.subtract, [SC(j3)], out_res)
                j4 = nscr() if fixed is None else fixed[2]
                tmp2 = scr[j4][:, 0:n].rearrange("p (h d) -> p h d", h=nh)
                PTT(tmp2[:, :, 0:32], x2, cb, ALU.mult, [SC(j1)] + TABS, [SC(j4)])
                PTT(tmp2[:, :, 32:64], x1, sb_, ALU.mult, [SC(j1)] + TABS, [SC(j4)])
                PTT(o3[:, :, 32:64], tmp2[:, :, 0:32], tmp2[:, :, 32:64], ALU.add, [SC(j4)], out_res)
            return partb

        def transpose_to(src_aps, src_res, dst_ap, dst_res, evac_eng):
            b = nbank()
            for i, sap in enumerate(src_aps):
                TR(psb[b][:, i * 128:(i + 1) * 128], sap, ident[:], src_res, [PB(b)])
            CP(evac_eng, dst_ap, psb[b][:, 0:len(src_aps) * 128], [PB(b)], dst_res)

        for l in range(nl):
            for gi, gd in enumerate((qna_d, kna_d, qnb_d, qnb_d, knb_d, knb_d)):
                DMA("sp", gains[:, gi, :], gd[l].partition_broadcast(128), [], [r("gains")], "gains")
            TS(gains[:, 1, :], gains[:, 1, :], 8.0, None, ALU.mult, None, [r("gains")], [r("gains")])
            TS(gains[:, 4:6, :], gains[:, 4:6, :], 8.0, None, ALU.mult, None, [r("gains")], [r("gains")])
            DMA("sp", esink[:], snk_d[l].partition_broadcast(128), [], [r("esink")], "esink")
            ACTF(esink[:], esink[:], AF.Exp, [r("esink")], [r("esink")])

            R.phase('P1')
            norm_phase(l, ang_d, 0)

            if dbg == 'stopP2':
                break
            R.phase('P2')
            VA = AV(V0, 2080).rearrange("p (t k d) -> p t k d", t=NT, k=2)
            MEMSET(VA[:, :, :, 64:65], 1.0, [Ar(V0, 2080)])
            s_q = load_slot([(lambda sl: sl[:].rearrange("p (k n) -> p k n", k=8),
                              win_d[l, :, 0:512].rearrange("(k p) n -> p k n", p=128))])
            s_kv = load_slot([(lambda sl: sl[:].rearrange("p (k n) -> p k n", k=8)[:, :, 0:256],
                               win_d[l, :, 512:768].rearrange("(k p) n -> p k n", p=128))])
            def z_pipe(ntile, produce, consume, depth=ZDEPTH):
                pend = []
                pb = {}
                for i in range(ntile + depth):
                    if i - depth >= 0:
                        consume(*pend.pop(0))
                    if i < ntile:
                        (ctx, partb) = produce(i)
                        pend.append(ctx)
                        pb[i] = partb
                    if 0 <= i - 1 < ntile:
                        pb.pop(i - 1)()

            def ka_prod(t):
                b = nbank()
                z_matmuls(b, s_kv, t * 128, 256)
                jk = npt()
                kn = pts_[jk][:, 0:128]
                pb_ = qk_norm(b, 0, 2, bc(gains[:, 1, :].unsqueeze(1), [128, 2, 64]), t, True, kn, [PT(jk)],
                              extra=lambda: CP("act", VA[:, t, :, 0:64], ps[b][:, 128:256].rearrange("p (k d) -> p k d", k=2),
                                               [PB(b)], [Ar(V0 + t * 130, 130)]))
                return ((t, kn, jk), pb_)

            def ka_cons(t, kn, jk):
                transpose_to([kn], [PT(jk)], AV(KT0 + t * 128, 128), [Ar(KT0 + t * 128, 128)], "act")

            z_pipe(NT, ka_prod, ka_cons)

            def qa_prod(t):
                b = t % 2
                z_matmuls(b, s_q, t * 128, 512)
                jq = t % 3
                qn = pts_[jq][:, 0:512]
                pb_ = qk_norm(b, 0, 8, bc(gains[:, 0, :].unsqueeze(1), [128, 8, 64]), t, True, qn, [PT(jq)],
                              fixed=(t % 2, 2, 3))
                return ((t, qn, jq), pb_)

            def qa_cons(t, qn, jq):
                bq = 7
                for i in range(4):
                    TR(psb[bq][:, i * 128:(i + 1) * 128], qn[:, i * 128:(i + 1) * 128], ident[:], [PT(jq)], [PB(bq)])
                CP("act", AV(QT0 + t * 512, 512), psb[bq][:, 0:512], [PB(bq)], [Ar(QT0 + t * 512, 512)])

            if dbg == 'stopP3':
                break
            R.phase('P3')
            units = []
            for n in range(NT):
                for kv in range(2):
                    kbs = [kb for kb in (n - 1, n, n + 1) if 0 <= kb < NT]
                    for ii, kb in enumerate(kbs):
                        units.append((n, kv, kb, ii == 0, ii == len(kbs) - 1))
            ctxs = {}
            grp_bo = {}
            due = []

            def a_s1(v):
                (n, kv, kb, first, last) = units[v]
                b = 2 + (v % 3)
                MM(ps[b][:, 0:512], arena[kv * 64:(kv + 1) * 64, KT0 + kb * 128: KT0 + (kb + 1) * 128],
                   arena[kv * 64:(kv + 1) * 64, QT0 + n * 512: QT0 + (n + 1) * 512], True, True,
                   [Ar(KT0 + kb * 128, 128), Ar(QT0 + n * 512, 512)], [PB(b)], sgc=True)
                jp = 3 + (v % 4)
                ACTF(pts_[jp][:, 0:512], ps[b][:, 0:512], AF.Exp, [PB(b)], [PT(jp)])
                if kb != n:
                    w = 0 if kb < n else 1
                    (TT if A_MASK_ENG == "dve" else GTT)(pts_[jp][:, 0:512], pts_[jp][:, 0:512], maskA[:, w * 512:(w + 1) * 512], ALU.mult,
                        [PT(jp), r("maskA")], [PT(jp)])
                ctxs[v] = jp

            def a_s3(v, step):
                (n, kv, kb, first, last) = units[v]
                jp = ctxs.pop(v)
                if first:
                    grp_bo[(n, kv)] = 5 + ((n * 2 + kv) % 2)
                bo = grp_bo[(n, kv)]
                for g in range(4):
                    MM(ps[bo][:, g * 65:(g + 1) * 65], pts_[jp][:, g * 128:(g + 1) * 128], VA[:, kb, kv, :],
                       (first and g == 0), last, [PT(jp), Ar(V0 + kb * 130, 130)], [PB(bo)], sgc=True)
                if not last:
                    return
                pv = ps[bo][:, 0:260].rearrange("p (g d) -> p g d", g=4)
                sd = nstat()
                TT(stats[:, sd, 0:4].unsqueeze(2), pv[:, :, 64:65], esink[:, kv * 4:(kv + 1) * 4].unsqueeze(2), ALU.add,
                   [PB(bo), r("esink")], [ST(sd)])
                RECIP(stats[:, sd, 0:4], stats[:, sd, 0:4], [ST(sd)], [ST(sd)])
                ja = 4 + ((n * 2 + kv) % 2)
                at = scrb[ja][:, 0:256]
                TT(at.rearrange("p (g d) -> p g d", g=4), pv[:, :, 0:64], bc(stats[:, sd, 0:4].unsqueeze(2), [128, 4, 64]),
                   ALU.mult, [PB(bo), ST(sd)], [SC(ja)])

                def s5():
                    dst = AV(AA0, 8192).rearrange("p (c s) -> p c s", c=4)[:, kv * 2:kv * 2 + 2, n * 128:(n + 1) * 128]
                    dres = [("A", AA0 + (kv * 2 + c_) * 2048 + n * 128, AA0 + (kv * 2 + c_) * 2048 + (n + 1) * 128) for c_ in range(2)]
                    bt = 7
                    for i in range(2):
                        TR(psb[bt][:, i * 128:(i + 1) * 128], at[:, i * 128:(i + 1) * 128], ident[:], [SC(ja)], [PB(bt)])
                    CP("act", dst, psb[bt][:, 0:256].rearrange("p (c s) -> p c s", c=2), [PB(bt)], dres)
                due.append((step + 2, s5))

            nun = len(units)
            ast = [0]

            def att_step():
                step = ast[0]
                ast[0] += 1
                if step < nun:
                    a_s1(step)
                if 2 <= step <= nun + 1:
                    a_s3(step - 2, step)
                for (ds, fn) in [d_ for d_ in due if d_[0] <= step]:
                    fn()
                due[:] = [d_ for d_ in due if d_[0] > step]

            first_unit = {}
            for v, u in enumerate(units):
                first_unit.setdefault(u[0], v)
            first_unit[NT] = nun
            qpend = []
            qpb = {}
            for it in range(NT + QDEPTH + 1):
                if it < NT:
                    (ctx, pb_) = qa_prod(it)
                    qpend.append(ctx)
                    qpb[it] = pb_
                nready = it - QDEPTH - 1
                if 0 <= nready < NT:
                    while ast[0] < first_unit[nready + 1]:
                        att_step()
                if 0 <= it - 1 < NT:
                    qpb.pop(it - 1)()
                if it - QDEPTH >= 0 and qpend:
                    qa_cons(*qpend.pop(0))
            while ast[0] < nun + 5:
                att_step()
            assert not due and not qpend and not qpb

            VB = AV(V0, 2080).rearrange("p (t k d) -> p t k d", t=NT, k=2)
            VS = AV(V0 + 2080, 1950).rearrange("p (t k d) -> p t k d", t=15, k=2)
            for hp in range(4):
                MEMSET(VB[:, :, :, 64:65], 1.0, [Ar(V0, 2080)])
                if dbg == 'stopP4%d' % hp:
                    break
                R.phase('P4z%d' % hp)
                s_b = load_slot([
                    (lambda sl: sl[:].rearrange("p (k n) -> p k n", k=8)[:, :, 0:128],
                     win_d[l, :, 768 + hp * 128: 768 + (hp + 1) * 128].rearrange("(k p) n -> p k n", p=128)),
                    (lambda sl: sl[:].rearrange("p (k n) -> p k n", k=8)[:, :, 128:256],
                     win_d[l, :, 1280 + hp * 128: 1280 + (hp + 1) * 128].rearrange("(k p) n -> p k n", p=128)),
                    (lambda sl: sl[:].rearrange("p (k n) -> p k n", k=8)[:, :, 256:384],
                     win_d[l, :, 1792 + hp * 128: 1792 + (hp + 1) * 128].rearrange("(k p) n -> p k n", p=128)),
                ])
                s_bias = load_slot([(lambda sl: sl[:], bias_d[l, hp])])
                for q4 in range(4 if (B_BIAS_MULT and not DBG_NOEXP) else 0):
                    ACTF(slots[s_bias][:, q4 * 1024:(q4 + 1) * 1024], slots[s_bias][:, q4 * 1024:(q4 + 1) * 1024], AF.Exp, [SL(s_bias)], [SL(s_bias)])
                def b_prod(t):
                    b = nbank()
                    z_matmuls(b, s_b, t * 128, 384)
                    jq = npt()
                    qkn = pts_[jq][:, 0:256]
                    pb_ = qk_norm(b, 0, 4, gains[:, 2:6, :], t, False, qkn, [PT(jq)],
                                  extra=lambda: CP("act", VB[:, t, :, 0:64], ps[b][:, 256:384].rearrange("p (k d) -> p k d", k=2),
                                                   [PB(b)], [Ar(V0 + t * 130, 130)]))
                    return ((t, qkn, jq), pb_)

                def b_cons(t, qkn, jq):
                    bq = nbank()
                    TR(psb[bq][:, 0:128], qkn[:, 0:128], ident[:], [PT(jq)], [PB(bq)])
                    TR(psb[bq][:, 128:256], qkn[:, 128:256], ident[:], [PT(jq)], [PB(bq)])
                    lo = QT0 + t * 128
                    dst = arena[:, lo: lo + 2 * (KT0 - QT0)].rearrange("p (c s) -> p c s", c=2)[:, :, 0:128]
                    CP("act" if t % 2 == 0 else "dve", dst, psb[bq][:, 0:256].rearrange("p (c s) -> p c s", c=2), [PB(bq)],
                       [Ar(QT0 + t * 128, 128), Ar(KT0 + t * 128, 128)])

                z_pipe(NT, b_prod, b_cons)
                if VS_DMA:
                    DMA("sp", arena[0:64, V0 + 2080: V0 + 2080 + 1950], arena[64:128, V0: V0 + 1950],
                        [Ar(V0, 2080)], [Ar(V0 + 2080, 1950)], "vs")
                    DMA("sp", arena[64:128, V0 + 2080: V0 + 2080 + 1950], arena[0:64, V0 + 130: V0 + 2080],
                        [Ar(V0, 2080)], [Ar(V0 + 2080, 1950)], "vs")
                else:
                    MEMSET(VS[:, :, :, 64:65], 1.0, [Ar(V0 + 2080, 1950)])
                    for t in range(15):
                        b = nbank()
                        for k in range(8):
                            MM(ps[b][:, 0:128], hT_ap(k, 64 + t * 128, 128), slots[s_b][:, k * 512 + 256: k * 512 + 384],
                               (k == 0), (k == 7), hT_res(64 + t * 128, 128) + [SL(s_b)], [PB(b)])
                        CP("act", VS[:, t, :, 0:64], ps[b][:, 0:128].rearrange("p (k d) -> p k d", k=2),
                           [PB(b)], [Ar(V0 + 2080 + t * 130, 130)])
                if dbg == 'stopP5%d' % hp:
                    break
                R.phase('P5a%d' % hp)
                btr = 7
                bctx = {}

                bk = {}

                def b_qk(rw, hh):
                    rs = min(max(rw - 4, 0), 24)
                    j0 = rs - rw + 7
                    if hh == 0:
                        bk[rw] = nbank()
                    b = bk[rw]
                    for i in range(4):
                        a_ = rs + 2 * i
                        MM(ps[b][:, hh * 256 + i * 64: hh * 256 + (i + 1) * 64],
                           arena[hh * 64:(hh + 1) * 64, KT0 + a_ * 64: KT0 + a_ * 64 + 128],
                           arena[hh * 64:(hh + 1) * 64, QT0 + rw * 64: QT0 + (rw + 1) * 64], (hh == 0 and i == 0), B_BIAS_MULT,
                           [Ar(KT0 + a_ * 64, 128), Ar(QT0 + rw * 64, 64)], [PB(b)], sgc=True)
                    if not B_BIAS_MULT:
                        boff = hh * 2048 + j0 * 256
                        MM(ps[b][:, hh * 256:(hh + 1) * 256], ident[:], slots[s_bias][:, boff:boff + 256], False, True,
                           [r("ident"), SL(s_bias)], [PB(b)], sgc=True)

                def b_s1a(rw):
                    b_qk(rw, 0)

                def b_sep():
                    bd = nbank()
                    MM(ps[bd][:, 0:128], ident[:], ident[:], True, True, [r("ident")], [PB(bd)], sgc=True)

                def b_s1b(rw):
                    b_qk(rw, 1)
                    rs = min(max(rw - 4, 0), 24)
                    j0 = rs - rw + 7
                    b = bk.pop(rw)
                    jp = npt()
                    ACTF(pts_[jp][:, 0:512], ps[b][:, 0:512], AF.Exp, [PB(b)], [PT(jp)])
                    if B_BIAS_MULT and not DBG_NOMULT:
                        ebv = slots[s_bias][:, :].rearrange("p (h x) -> p h x", h=2)[:, :, j0 * 256:(j0 + 1) * 256]
                        (TT if B_MULT_ENG == "dve" else GTT)(
                            pts_[jp][:, 0:512].rearrange("p (h x) -> p h x", h=2), pts_[jp][:, 0:512].rearrange("p (h x) -> p h x", h=2),
                            ebv, ALU.mult, [PT(jp), SL(s_bias)], [PT(jp)])
                    bctx[rw] = (rs, jp)

                def b_s3(rw):
                    (rs, jp) = bctx[rw]
                    bo = nbank()
                    for hh in range(2):
                        for i in range(4):
                            a_ = rs + 2 * i
                            if a_ % 2 == 0:
                                vap = VB[:, a_ // 2, hh, :]
                                vres = Ar(V0 + (a_ // 2) * 130, 130)
                            else:
                                vap = VS[:, (a_ - 1) // 2, hh, :]
                                vres = Ar(V0 + 2080 + ((a_ - 1) // 2) * 130, 130)
                            MM(ps[bo][0:64, hh * 65:(hh + 1) * 65], pts_[jp][:, hh * 256 + i * 64: hh * 256 + (i + 1) * 64], vap,
                               (i == 0), (i == 3), [PT(jp), vres], [PB(bo)], sgc=True)
                    pv = ps[bo][0:64, 0:130].rearrange("p (g d) -> p g d", g=2)
                    sd = nstat()
                    RECIP(stats[0:64, sd, 0:2].unsqueeze(2), pv[:, :, 64:65], [PB(bo)], [ST(sd)])
                    ja = nscr()
                    at = scrb[ja][0:64, 0:128]
                    TT(at.rearrange("p (g d) -> p g d", g=2), pv[:, :, 0:64], bc(stats[0:64, sd, 0:2].unsqueeze(2), [64, 2, 64]),
                       ALU.mult, [PB(bo), ST(sd)], [SC(ja)])
                    bctx[rw] = (at, ja)

                def b_s5(rw):
                    (at, ja) = bctx.pop(rw)
                    rr = rw % 8
                    TR(psb[btr][:, rr * 64:(rr + 1) * 64], at, ident[0:64, 0:64], [SC(ja)], [PB(btr)])
                    if rr == 7:
                        dlo = AB0 + hp * 2048 + (rw // 8) * 512
                        CP("act", AV(dlo, 512), psb[btr][:, 0:512], [PB(btr)], [Ar(dlo, 512)])

                for step in range(32 + 3):
                    if step < 32:
                        b_s1a(step)
                    if 2 <= step <= 33:
                        b_s3(step - 2)
                    elif step < 32 and B_BIAS_MULT:
                        b_sep()
                    if step < 32:
                        b_s1b(step)
                    if 3 <= step:
                        b_s5(step - 3)

            if dbg == 'stopP6':
                break
            R.phase('P6')
            for c in range(8):
                s_w = load_slot([
                    (lambda sl: sl[:, 0:1024].rearrange("p (k n) -> p k n", k=8),
                     win_d[l, :, 2304 + c * 128: 2304 + (c + 1) * 128].rearrange("(k p) n -> p k n", p=128)),
                    (lambda sl: sl[:, 1024:2048].rearrange("p (k n) -> p k n", k=8),
                     win_d[l, :, 3328 + c * 128: 3328 + (c + 1) * 128].rearrange("(k p) n -> p k n", p=128)),
                    (lambda sl: sl[:, 2048:2560].rearrange("p (k n) -> p k n", k=4),
                     wpa_d[l, :, c * 128:(c + 1) * 128].rearrange("(k p) n -> p k n", p=128)),
                    (lambda sl: sl[:, 2560:3072].rearrange("p (k n) -> p k n", k=4),
                     wpb_d[l, :, c * 128:(c + 1) * 128].rearrange("(k p) n -> p k n", p=128)),
                ])
                for tg in range(4):
                    bga, bgb, bya, byb = nbank(), nbank(), nbank(), nbank()
                    for (bb, woff) in ((bga, 0), (bgb, 1024)):
                        for k in range(8):
                            MM(ps[bb][:, 0:512], slots[s_w][:, woff + k * 128: woff + (k + 1) * 128], hT_ap(k, tg * 512, 512),
                               (k == 0), (k == 7), hT_res(tg * 512, 512) + [SL(s_w)], [PB(bb)])
                    for (bb, woff, a0) in ((bya, 2048, AA0), (byb, 2560, AB0)):
                        for k in range(4):
                            lo = a0 + k * 2048 + tg * 512
                            MM(ps[bb][:, 0:512], slots[s_w][:, woff + k * 128: woff + (k + 1) * 128], arena[:, lo:lo + 512],
                               (k == 0), (k == 3), [("A", lo, lo + 512), SL(s_w)], [PB(bb)])
                    j1, j2 = nscr(), nscr()
                    ACTF(scr[j1][:, 0:512], ps[bga][:, 0:512], AF.Sigmoid, [PB(bga)], [SC(j1)])
                    ACTF(scr[j2][:, 0:512], ps[bgb][:, 0:512], AF.Sigmoid, [PB(bgb)], [SC(j2)])
                    TT(scr[j1][:, 0:512], scr[j1][:, 0:512], ps[bya][:, 0:512], ALU.mult, [SC(j1), PB(bya)], [SC(j1)])
                    TT(scr[j2][:, 0:512], scr[j2][:, 0:512], ps[byb][:, 0:512], ALU.mult, [SC(j2), PB(byb)], [SC(j2)])
                    mlo = MX0 + c * 2048 + tg * 512
                    TT(AV(mlo, 512), scr[j1][:, 0:512], scr[j2][:, 0:512], ALU.add, [SC(j1), SC(j2)], [Ar(mlo, 512)])

            R.phase('P7')
            for hf in range(2):
                s_o = load_slot([(lambda sl: sl[:].rearrange("p (k n) -> p k n", k=8),
                                  wout_d[l, :, hf * 512:(hf + 1) * 512].rearrange("(k p) n -> p k n", p=128))])
                for t in range(NT):
                    b = nbank()
                    for k in range(8):
                        lo = MX0 + k * 2048 + t * 128
                        MM(ps[b][:, 0:512], arena[:, lo:lo + 128], slots[s_o][:, k * 512:(k + 1) * 512],
                           (k == 0), (k == 7), [("A", lo, lo + 128), SL(s_o)], [PB(b)])
                    xs = x_sb[:, t, hf * 512:(hf + 1) * 512]
                    TT(xs, xs, ps[b][:, 0:512], ALU.add, [X(t), PB(b)], [X(t)])

            if dbg == 'attn':
                dbg_d = dt('dbg_ab', [128, 16384], BF16, kind='ExternalOutput').ap()
                DMA('sp', dbg_d, AV(AA0, 16384), [Ar(AA0, 16384)], [], 'dbgout')
                continue
            R.phase('P8')
            norm_phase(l, fng_d, 1)

            R.phase('P9')
            f0 = 0
            for fgn in (8, 8, 6):
                for fp in range(fgn // 2):
                    fa = f0 + fp * 2
                    s_gu = load_slot([
                        (lambda sl: sl[:, 0:2048].rearrange("p (k n) -> p k n", k=8),
                         wg_d[l, :, fa * 128:(fa + 2) * 128].rearrange("(k p) n -> p k n", p=128)),
                        (lambda sl: sl[:, 2048:4096].rearrange("p (k n) -> p k n", k=8),
                         wu_d[l, :, fa * 128:(fa + 2) * 128].rearrange("(k p) n -> p k n", p=128)),
                    ])
                    for q2 in range(2):
                        fl = fp * 2 + q2
                        for tg in range(4):
                            bg, bu = nbank(), nbank()
                            for (bb, woff) in ((bg, 0), (bu, 2048)):
                                for k in range(8):
                                    wlo = woff + k * 256 + q2 * 128
                                    MM(ps[bb][:, 0:512], slots[s_gu][:, wlo:wlo + 128], hT_ap(k, tg * 512, 512),
                                       (k == 0), (k == 7), hT_res(tg * 512, 512) + [SL(s_gu)], [PB(bb)])
                            j1 = nscr()
                            ACTF(scr[j1][:, 0:512], ps[bg][:, 0:512], AF.Silu, [PB(bg)], [SC(j1)])
                            ulo = UT0 + fl * 2048 + tg * 512
                            TT(AV(ulo, 512), scr[j1][:, 0:512], ps[bu][:, 0:512], ALU.mult, [SC(j1), PB(bu)], [Ar(ulo, 512)])
                dsl = []
                for j in range(0, fgn, 4):
                    nch = min(4, fgn - j)
                    s_d = load_slot([(lambda sl, nch=nch: sl[:, 0:nch * 1024].rearrange("p (k n) -> p k n", k=nch),
                                      wd_d[l, (f0 + j) * 128:(f0 + j + nch) * 128, :].rearrange("(k p) n -> p k n", p=128))])
                    dsl.append(s_d)
                for t in range(NT):
                    for hf in range(2):
                        b = nbank()
                        for j in range(fgn):
                            s_d = dsl[j // 4]
                            lo = UT0 + j * 2048 + t * 128
                            wlo = (j % 4) * 1024 + hf * 512
                            MM(ps[b][:, 0:512], arena[:, lo:lo + 128], slots[s_d][:, wlo:wlo + 512],
                               (j == 0), (j == fgn - 1), [("A", lo, lo + 128), SL(s_d)], [PB(b)])
                        xs = x_sb[:, t, hf * 512:(hf + 1) * 512]
                        TT(xs, xs, ps[b][:, 0:512], ALU.add, [X(t), PB(b)], [X(t)])
                f0 += fgn

        for t in range(NT):
            DMA("sp", out_d[t * 128:(t + 1) * 128, :], x_sb[:, t, :], [X(t)], [], "out")
        R.phase('end')
        global PHASES
        PHASES = list(R.phases)
        R.emit(final_waits=[(R.dmasems["out"], R.dmacnt["out"])])
    return nc


def _consts():
    ident = np.eye(128, dtype=np.float32)
    j = np.arange(128)[:, None]
    rq = np.arange(128)[None, :]
    m_prev = np.where(j >= rq, 1.0, 0.0).astype(np.float32)
    m_next = np.where(j <= rq, 1.0, 0.0).astype(np.float32)
    mask = np.concatenate([np.tile(m_prev, (1, 4)), np.tile(m_next, (1, 4))], axis=1)
    half = 32
    inv = (10000.0 ** (-np.arange(half, dtype=np.float32) / np.float32(half))).astype(np.float32)
    invf = np.tile(inv[None, :], (128, 1)).astype(np.float32)
    return ident, np.ascontiguousarray(mask), invf


def _bias_tiles(rel_bias):
    L = rel_bias.shape[0]
    o = np.arange(2)
    j0 = np.arange(8)
    i = np.arange(4)
    DR = np.minimum(j0[None, :, None] + 2 * i[None, None, :] + o[:, None, None], 14)
    kc = np.arange(64)[:, None]
    qc = np.arange(64)[None, :]
    DC = np.clip(kc - qc + 15, 0, 30)
    cs = np.clip(qc - 8, 0, 48)
    valid = (kc >= cs) & (kc < cs + 16)
    g = rel_bias[:, :, DR[:, None, :, :, None], DC[None, :, None, None, :]]
    g = np.where(valid[None, None, None, :, None, None, :], g, np.float32(NEG)).astype(np.float32)
    g = g.reshape(L, 4, 2, 128, 8, 4, 64).transpose(0, 1, 3, 2, 4, 5, 6)
    return np.ascontiguousarray(g.reshape(L, 4, 128, 4096))


def _perm_w_in(w_in):
    L = w_in.shape[0]
    qa = w_in[:, :, 0:512].reshape(L, D, 2, 4, 64).transpose(0, 1, 3, 2, 4).reshape(L, D, 512)
    return np.ascontiguousarray(np.concatenate([qa, w_in[:, :, 512:]], axis=2))


def _gT(g):
    return np.ascontiguousarray(g.reshape(g.shape[0], 8, 128).transpose(0, 2, 1))


_NC_CACHE = {}


def _get_nc(nl):
    if nl not in _NC_CACHE:
        _NC_CACHE[nl] = build(nl)
    return _NC_CACHE[nl]


def kernel(x, positions, attn_norm_g, w_in, q_norm_a, k_norm_a, sink_a, q_norm_b, k_norm_b,
           rel_bias_b, w_proj_a, w_proj_b, w_out, ffn_norm_g, w_gate, w_up, w_down):
    f = lambda a: np.ascontiguousarray(np.asarray(a, dtype=np.float32))
    x = f(x)
    pos = np.ascontiguousarray(np.asarray(positions, dtype=np.int32).reshape(NT, 128).T)
    ident, mask, invf = _consts()
    w_in_p = _perm_w_in(f(w_in))
    bias_t = _bias_tiles(f(rel_bias_b))
    per_layer = {
        "attn_norm_g": _gT(f(attn_norm_g)), "w_in": w_in_p, "q_norm_a": f(q_norm_a), "k_norm_a": f(k_norm_a),
        "sink_a": f(sink_a), "q_norm_b": f(q_norm_b), "k_norm_b": f(k_norm_b), "bias_t": bias_t,
        "w_proj_a": f(w_proj_a), "w_proj_b": f(w_proj_b), "w_out": f(w_out), "ffn_norm_g": _gT(f(ffn_norm_g)),
        "w_gate": f(w_gate), "w_up": f(w_up), "w_down": f(w_down),
    }
    consts = {"pos": pos, "ident": ident, "mask_a": mask, "invf": invf}
    B = x.shape[0]
    if FUSED:
        nc = _get_nc(DEPTH)
        in_maps = []
        for b in range(B):
            m = {"x": x[b]}
            m.update(consts)
            m.update(per_layer)
            in_maps.append(m)
        res = run_bass_kernel_spmd(nc, in_maps, core_ids=list(range(B)))
        return np.stack([res.results[b]["out"] for b in range(B)], axis=0)
    nc = _get_nc(1)
    cur = [x[b] for b in range(B)]
    for l in range(DEPTH):
        in_maps = []
        for b in range(B):
            m = {"x": cur[b]}
            m.update(consts)
            m.update({k: np.ascontiguousarray(v[l:l + 1]) for k, v in per_layer.items()})
            in_maps.append(m)
        res = run_bass_kernel_spmd(nc, in_maps, core_ids=list(range(B)))
        cur = [np.ascontiguousarray(res.results[b]["out"]) for b in range(B)]
    return np.stack(cur, axis=0)
```

```python
import numpy as np
from contextlib import ExitStack
import concourse.bass as bass
import concourse.mybir as mybir
from concourse.bass_utils import run_bass_kernel_spmd

F32 = mybir.dt.float32
BF16 = mybir.dt.bfloat16
I32 = mybir.dt.int32
ALU = mybir.AluOpType
AF = mybir.ActivationFunctionType
AX = mybir.AxisListType

S = 2048
D = 1024
NT = 16
FF = 2816
NFC = 22
INC = 4352
NEG = -30000.0
EPS = 1e-6
DEPTH = 4
FUSED = True
ROPE2_ENG = "pool"
VS_DMA = True
NPT = 7
NSCR = 6
ZDEPTH = 4
QDEPTH = 3
B_MULT_ENG = "dve"
A_MASK_ENG = "dve"
B_BIAS_MULT = True
DBG_NOEXP = False
DBG_NOMULT = False

ENGS = ["pe", "act", "dve", "pool", "sp"]


class Op:
    __slots__ = ("eng", "fn", "deps", "marked", "count", "sem", "is_dma", "idx")

    def __init__(self, eng, fn):
        self.eng = eng
        self.fn = fn
        self.deps = []
        self.marked = False
        self.count = 0
        self.sem = None
        self.is_dma = False


class Rec:
    def __init__(self, nc, stack):
        self.nc = nc
        self.stack = stack
        self.ops = {e: [] for e in ENGS}
        self.hist = {}
        self.names = {}
        self.engsem = {e: stack.enter_context(nc.semaphore("s_" + e)) for e in ENGS}
        self.dmasems = {}
        self.dmacnt = {}
        self.nops = 0
        self.phases = []

    def phase(self, name):
        self.phases.append((name, len(self.ops['pe'])))

    def res(self, name):
        if name not in self.names:
            self.names[name] = len(self.names)
        i = self.names[name]
        return ("N", i, i + 1)

    def _sem_for(self, key):
        if key not in self.dmasems:
            self.dmasems[key] = self.stack.enter_context(self.nc.semaphore("d_%d" % len(self.dmasems)))
            self.dmacnt[key] = 0
        return self.dmasems[key]

    def add(self, eng, fn, reads=(), writes=(), dma=None):
        op = Op(eng, fn)
        op.idx = self.nops
        self.nops += 1
        deps = {}
        for (sp, lo, hi) in reads:
            for rec in self.hist.get(sp, ()):
                if rec[0] < hi and lo < rec[1] and (rec[3] or (sp == "P" and rec[2].eng != eng)):
                    deps[id(rec[2])] = rec[2]
        for (sp, lo, hi) in writes:
            for rec in self.hist.get(sp, ()):
                if rec[0] < hi and lo < rec[1]:
                    deps[id(rec[2])] = rec[2]
        for d in deps.values():
            if d.eng == "pe" and eng == "pe":
                continue
            d.marked = True
            op.deps.append(d)
        if dma is not None:
            op.is_dma = True
            op.sem = self._sem_for(dma)
            self.dmacnt[dma] += 16
            op.count = self.dmacnt[dma]
        for (sp, lo, hi) in writes:
            h = self.hist.setdefault(sp, [])
            h[:] = [r for r in h if not (lo <= r[0] and r[1] <= hi)]
            h.append([lo, hi, op, True])
        for (sp, lo, hi) in reads:
            h = self.hist.setdefault(sp, [])
            if not op.is_dma:
                h[:] = [r for r in h if not ((not r[3]) and r[0] == lo and r[1] == hi
                                             and r[2].eng == eng and not r[2].is_dma)]
            h.append([lo, hi, op, False])
        self.ops[eng].append(op)
        return op

    def emit(self, final_waits=()):
        nc = self.nc
        for e in ENGS:
            c = 0
            for op in self.ops[e]:
                if op.is_dma:
                    continue
                if op.marked:
                    c += 1
                    op.count = c
                    op.sem = self.engsem[e]
        ops = self.ops
        engsem = self.engsem

        def make_body(e):
            def body(eng):
                waited = {}
                for op in ops[e]:
                    need = {}
                    for d in op.deps:
                        k = id(d.sem)
                        if k not in need or need[k][1] < d.count:
                            need[k] = (d.sem, d.count)
                    for k, (sem, val) in need.items():
                        if waited.get(k, 0) >= val:
                            continue
                        eng.wait_ge(sem, val)
                        waited[k] = val
                    ins = op.fn(eng)
                    if op.is_dma:
                        ins.then_inc(op.sem, 16)
                    elif op.marked:
                        ins.then_inc(engsem[e], 1)
                if e == "sp":
                    for (sem, val) in final_waits:
                        eng.wait_ge(sem, val)
            return body

        with nc.Block() as block:
            block.tensor(make_body("pe"))
            block.scalar(make_body("act"))
            block.vector(make_body("dve"))
            block.gpsimd(make_body("pool"))
            block.sync(make_body("sp"))


def bc(ap, shape):
    return ap.to_broadcast(list(shape))


def build(nl, dbg=None):
    nc = bass.Bass("TRN2", target_bir_lowering=False)
    dt = nc.dram_tensor
    x_d = dt("x", [S, D], F32, kind="ExternalInput").ap()
    pos_d = dt("pos", [128, NT], I32, kind="ExternalInput").ap()
    ang_d = dt("attn_norm_g", [nl, 128, 8], F32, kind="ExternalInput").ap()
    win_d = dt("w_in", [nl, D, INC], F32, kind="ExternalInput").ap()
    qna_d = dt("q_norm_a", [nl, 64], F32, kind="ExternalInput").ap()
    kna_d = dt("k_norm_a", [nl, 64], F32, kind="ExternalInput").ap()
    snk_d = dt("sink_a", [nl, 8], F32, kind="ExternalInput").ap()
    qnb_d = dt("q_norm_b", [nl, 64], F32, kind="ExternalInput").ap()
    knb_d = dt("k_norm_b", [nl, 64], F32, kind="ExternalInput").ap()
    bias_d = dt("bias_t", [nl, 4, 128, 4096], F32, kind="ExternalInput").ap()
    wpa_d = dt("w_proj_a", [nl, 512, D], F32, kind="ExternalInput").ap()
    wpb_d = dt("w_proj_b", [nl, 512, D], F32, kind="ExternalInput").ap()
    wout_d = dt("w_out", [nl, D, D], F32, kind="ExternalInput").ap()
    fng_d = dt("ffn_norm_g", [nl, 128, 8], F32, kind="ExternalInput").ap()
    wg_d = dt("w_gate", [nl, D, FF], F32, kind="ExternalInput").ap()
    wu_d = dt("w_up", [nl, D, FF], F32, kind="ExternalInput").ap()
    wd_d = dt("w_down", [nl, FF, D], F32, kind="ExternalInput").ap()
    ident_d = dt("ident", [128, 128], F32, kind="ExternalInput").ap()
    mask_d = dt("mask_a", [128, 1024], F32, kind="ExternalInput").ap()
    invf_d = dt("invf", [128, 32], F32, kind="ExternalInput").ap()
    out_d = dt("out", [S, D], F32, kind="ExternalOutput").ap()

    with ExitStack() as st:
        R = Rec(nc, st)
        sbt = lambda n, s, d: st.enter_context(nc.sbuf_tensor(n, s, d))
        x_sb = sbt("x_sb", [128, NT, D], F32)
        AR_N = 47104
        arena = sbt("arena", [128, AR_N], BF16)
        slots = [sbt("slot%d" % i, [128, 4096], BF16) for i in range(3)]
        scr = [sbt("scr%d" % i, [128, 512], F32) for i in range(NSCR)]
        scrb = [s_.bitcast(BF16) for s_ in scr]
        pts_ = [sbt("pt%d" % i, [128, 512], BF16) for i in range(NPT)]
        gT = sbt("gT", [128, 2, 8], F32)
        ident = sbt("ident_sb", [128, 128], BF16)
        maskA = sbt("maskA_sb", [128, 1024], BF16)
        cos_t = sbt("cos_t", [128, NT, 32], F32)
        sin_t = sbt("sin_t", [128, NT, 32], F32)
        gains = sbt("gains", [128, 6, 64], F32)
        epsb = sbt("epsb", [128, 2], F32)
        esink = sbt("esink", [128, 8], F32)
        stats = sbt("stats", [128, 15, 16], F32)
        ps = [st.enter_context(nc.psum_tensor("ps%d" % i, [128, 512], F32)) for i in range(8)]
        psb = [p_.bitcast(BF16) for p_ in ps]

        HT0 = 0
        QT0 = 16384
        KT0 = QT0 + 8192
        V0 = KT0 + 2048
        AA0 = V0 + 4096
        AB0 = AA0 + 8192
        MX0 = QT0
        UT0 = QT0
        assert AB0 + 8192 == AR_N

        def AV(lo, n):
            return arena[:, lo:lo + n]

        def Ar(lo, n):
            return ("A", lo, lo + n)

        def X(t):
            return ("X", t, t + 1)

        def PB(b):
            return ("P", b, b + 1)

        def SL(s_):
            return ("W", s_, s_ + 1)

        def SC(i):
            return ("S", i, i + 1)

        r = R.res
        state = {"bank": 0, "scr": 0, "stat": 0, "slot": 0, "pt": 0}

        def nbank():
            b = state["bank"]
            state["bank"] = (b + 1) % 7
            return b

        def nscr():
            i = state["scr"]
            state["scr"] = (i + 1) % NSCR
            return i

        def nstat():
            i = state["stat"]
            state["stat"] = (i + 1) % 15
            return i

        def nslot():
            i = state["slot"]
            state["slot"] = (i + 1) % 3
            return i

        def ST(i):
            return ("T", i, i + 1)

        def PT(i):
            return ("Q", i, i + 1)

        def npt():
            i = state["pt"]
            state["pt"] = (i + 1) % NPT
            return i

        def MM(out, lhsT, rhs, start, stop, reads, writes, sgc=False):
            if sgc:
                R.add("pe", lambda e: e.matmul(out, lhsT=lhsT, rhs=rhs, start=start, stop=stop, skip_group_check=True),
                      reads, writes)
            else:
                R.add("pe", lambda e: e.matmul(out, lhsT=lhsT, rhs=rhs, start=start, stop=stop), reads, writes)

        def TR(out, in_, idn, reads, writes):
            R.add("pe", lambda e: e.transpose(out=out, in_=in_, identity=idn), reads + [r("ident")], writes)

        def ACTF(out, in_, func, reads, writes, scale=None, accum=None, bias=None):
            kw = {}
            if bias is not None:
                kw["bias"] = bias
            if scale is not None:
                kw["scale"] = scale
            if accum is not None:
                kw["accum_out"] = accum
            R.add("act", lambda e: e.activation(out=out, in_=in_, func=func, **kw), reads, writes)

        def TT(out, in0, in1, op, reads, writes):
            R.add("dve", lambda e: e.tensor_tensor(out=out, in0=in0, in1=in1, op=op), reads, writes)

        def PTT(out, in0, in1, op, reads, writes):
            R.add(ROPE2_ENG, lambda e: e.tensor_tensor(out=out, in0=in0, in1=in1, op=op), reads, writes)

        def GTT(out, in0, in1, op, reads, writes):
            R.add("pool", lambda e: e.tensor_tensor(out=out, in0=in0, in1=in1, op=op), reads, writes)

        def TS(out, in0, s1, s2, op0, op1, reads, writes):
            if s2 is None:
                R.add("dve", lambda e: e.tensor_scalar(out=out, in0=in0, scalar1=s1, scalar2=None, op0=op0), reads, writes)
            else:
                R.add("dve", lambda e: e.tensor_scalar(out=out, in0=in0, scalar1=s1, scalar2=s2, op0=op0, op1=op1), reads, writes)

        def STT(out, in0, scalar, in1, op0, op1, reads, writes):
            R.add("dve", lambda e: e.scalar_tensor_tensor(out=out, in0=in0, scalar=scalar, in1=in1, op0=op0, op1=op1), reads, writes)

        def CP(eng, out, in_, reads, writes):
            if eng == "act":
                R.add("act", lambda e: e.copy(out=out, in_=in_), reads, writes)
            else:
                R.add("dve", lambda e: e.tensor_copy(out=out, in_=in_), reads, writes)

        def DMA(eng, out, in_, reads, writes, key):
            R.add(eng, lambda e: e.dma_start(out=out, in_=in_), reads, writes, dma=key)

        def MEMSET(ap, val, writes):
            R.add("dve", lambda e: e.memset(ap, val), (), writes)

        def RECIP(out, in_, reads, writes):
            R.add("dve", lambda e: e.reciprocal(out=out, in_=in_), reads, writes)

        def RSUM(out, in_, reads, writes):
            R.add("dve", lambda e: e.reduce_sum(out=out, in_=in_, axis=AX.X), reads, writes)

        for t in range(NT):
            DMA("sp", x_sb[:, t, :], x_d[t * 128:(t + 1) * 128, :], [], [X(t)], "x%d" % t)
        DMA("pool", ident[:], ident_d, [], [r("ident")], "c_ident")
        DMA("pool", maskA[:], mask_d, [], [r("maskA")], "c_mask")
        pos_i = scr[2].bitcast(I32)[:, 0:NT]
        DMA("sp", pos_i, pos_d, [], [SC(2)], "c_pos")
        invf = scr[5][:, 0:32]
        DMA("sp", invf, invf_d, [], [SC(5)], "c_invf")
        MEMSET(epsb[:, 0:1], EPS, [r("epsb")])
        MEMSET(epsb[:, 1:2], 64.0 * EPS, [r("epsb")])
        posf = stats[:, 14, :]
        CP("dve", posf, pos_i, [SC(2)], [ST(14)])
        TWO_PI = 2.0 * np.pi
        kf = scr[4][:, 0:512].rearrange("p (a b) -> p a b", a=NT)
        ki = scr[3].bitcast(I32)[:, 0:512].rearrange("p (a b) -> p a b", a=NT)
        for (tab, shift, nm) in ((sin_t, 0.0, "tab_s"), (cos_t, 0.5 * np.pi, "tab_c")):
            TT(tab[:], bc(posf.unsqueeze(2), [128, NT, 32]), bc(invf.unsqueeze(1), [128, NT, 32]), ALU.mult,
               [ST(14), SC(5)], [r(nm)])
            if shift != 0.0:
                TS(tab[:], tab[:], float(shift), None, ALU.add, None, [r(nm)], [r(nm)])
            TS(kf, tab[:], float(1.0 / TWO_PI), None, ALU.mult, None, [r(nm)], [SC(4)])
            CP("dve", ki, kf, [SC(4)], [SC(3)])
            CP("dve", kf, ki, [SC(3)], [SC(4)])
            STT(tab[:], kf, float(-TWO_PI), tab[:], ALU.mult, ALU.add, [SC(4), r(nm)], [r(nm)])
            TS(tab[:], tab[:], float(np.pi), float(-np.pi), ALU.min, ALU.max, [r(nm)], [r(nm)])
            ACTF(tab[:], tab[:], AF.Sin, [r(nm)], [r(nm)])
        TABS = [r("tab_s"), r("tab_c")]

        def load_slot(parts):
            s_ = nslot()
            for (dst_fn, src) in parts:
                DMA("pool", dst_fn(slots[s_]), src, [], [SL(s_)], "slot%d" % s_)
            return s_

        def rms_stats(ss_ap, out_ap, res_in, res_out, mult, which):
            ACTF(out_ap, ss_ap, AF.Ln, [res_in, r("epsb")], [res_out], scale=float(mult), bias=epsb[:, which:which + 1])
            ACTF(out_ap, out_ap, AF.Exp, [res_out], [res_out], scale=-0.5)

        def hT_ap(k, lo, n):
            return arena[:, HT0 + k * 2048 + lo: HT0 + k * 2048 + lo + n]

        def hT_res(lo, n):
            return [("A", HT0 + k * 2048 + lo, HT0 + k * 2048 + lo + n) for k in range(8)]

        def norm_phase(l, g_d, gi):
            DMA("sp", gT[:, gi, :], g_d[l], [], [r("gT%d" % gi)], "gT%d" % gi)
            si = nstat()
            so = nstat()
            for t in range(NT):
                j = nscr()
                ACTF(scrb[j][:, 0:1024], x_sb[:, t, :], AF.Square, [X(t)], [SC(j), ST(si)], accum=stats[:, si, t:t + 1])
            rms_stats(stats[:, si, :], stats[:, so, :], ST(si), ST(so), 1.0 / D, 0)
            for t in range(NT):
                j = nscr()
                ACTF(scrb[j][:, 0:1024], x_sb[:, t, :], AF.Copy, [X(t), ST(so)], [SC(j)], scale=stats[:, so, t:t + 1])
                b = nbank()
                for c in range(8):
                    TR(psb[b][:, c * 128:(c + 1) * 128], scrb[j][:, c * 128:(c + 1) * 128], ident[:], [SC(j)], [PB(b)])
                dst = AV(HT0, 16384).rearrange("p (c s) -> p c s", c=8)[:, :, t * 128:(t + 1) * 128]
                src = psb[b][:, 0:1024].rearrange("p (c s) -> p c s", c=8)
                TT(dst, src, bc(gT[:, gi, :].unsqueeze(2), [128, 8, 128]), ALU.mult, [PB(b), r("gT%d" % gi)], hT_res(t * 128, 128))

        def z_matmuls(b, s_, tok_lo, ncols):
            for k in range(8):
                MM(ps[b][:, 0:ncols], hT_ap(k, tok_lo, 128), slots[s_][:, k * 512: k * 512 + ncols],
                   (k == 0), (k == 7), hT_res(tok_lo, 128) + [SL(s_)], [PB(b)])

        def qk_norm(b, col_lo, nh, gain_ap, t, rope, out_ap, out_res, extra=None, fixed=None):
            n = nh * 64
            zin = ps[b][:, col_lo:col_lo + n]
            j1 = nscr() if fixed is None else fixed[0]
            si = nstat()
            so = nstat()
            ACTF(scr[j1][:, 0:n], zin, AF.Square, [PB(b)], [SC(j1)])
            if extra is not None:
                extra()
            RSUM(stats[:, si, 0:nh], scr[j1][:, 0:n].rearrange("p (h d) -> p h d", h=nh), [SC(j1)], [ST(si)])

            def partb():
                rms_stats(stats[:, si, 0:nh], stats[:, so, 0:nh], ST(si), ST(so), 1.0, 1)
                t1 = scr[j1][:, 0:n].rearrange("p (h d) -> p h d", h=nh)
                TT(t1, zin.rearrange("p (h d) -> p h d", h=nh), bc(stats[:, so, 0:nh].unsqueeze(2), [128, nh, 64]), ALU.mult,
                   [PB(b), ST(so)], [SC(j1)])
                o3 = out_ap.rearrange("p (h d) -> p h d", h=nh)
                if not rope:
                    TT(o3, t1, gain_ap, ALU.mult, [SC(j1), r("gains")], out_res)
                    return
                GTT(t1, t1, gain_ap, ALU.mult, [SC(j1), r("gains")], [SC(j1)])
                j3 = nscr() if fixed is None else fixed[1]
                tmp = scr[j3][:, 0:n].rearrange("p (h d) -> p h d", h=nh)
                x1 = t1[:, :, 0:32]
                x2 = t1[:, :, 32:64]
                cb = bc(cos_t[:, t, :].unsqueeze(1), [128, nh, 32])
                sb_ = bc(sin_t[:, t, :].unsqueeze(1), [128, nh, 32])
                TT(tmp[:, :, 0:32], x1, cb, ALU.mult, [SC(j1)] + TABS, [SC(j3)])
                TT(tmp[:, :, 32:64], x2, sb_, ALU.mult, [SC(j1)] + TABS, [SC(j3)])
                TT(o3[:, :, 0:32], tmp[:, :, 0:32], tmp[:, :, 32:64], ALU.subtract, [SC(j3)], out_res)
                j4 = nscr() if fixed is None else fixed[2]
                tmp2 = scr[j4][:, 0:n].rearrange("p (h d) -> p h d", h=nh)
                PTT(tmp2[:, :, 0:32], x2, cb, ALU.mult, [SC(j1)] + TABS, [SC(j4)])
                PTT(tmp2[:, :, 32:64], x1, sb_, ALU.mult, [SC(j1)] + TABS, [SC(j4)])
                PTT(o3[:, :, 32:64], tmp2[:, :, 0:32], tmp2[:, :, 32:64], ALU.add, [SC(j4)], out_res)
            return partb

        def transpose_to(src_aps, src_res, dst_ap, dst_res, evac_eng):
            b = nbank()
            for i, sap in enumerate(src_aps):
                TR(psb[b][:, i * 128:(i + 1) * 128], sap, ident[:], src_res, [PB(b)])
            CP(evac_eng, dst_ap, psb[b][:, 0:len(src_aps) * 128], [PB(b)], dst_res)

        for l in range(nl):
            for gi, gd in enumerate((qna_d, kna_d, qnb_d, qnb_d, knb_d, knb_d)):
                DMA("sp", gains[:, gi, :], gd[l].partition_broadcast(128), [], [r("gains")], "gains")
            TS(gains[:, 1, :], gains[:, 1, :], 8.0, None, ALU.mult, None, [r("gains")], [r("gains")])
            TS(gains[:, 4:6, :], gains[:, 4:6, :], 8.0, None, ALU.mult, None, [r("gains")], [r("gains")])
            DMA("sp", esink[:], snk_d[l].partition_broadcast(128), [], [r("esink")], "esink")
            ACTF(esink[:], esink[:], AF.Exp, [r("esink")], [r("esink")])

            R.phase('P1')
            norm_phase(l, ang_d, 0)

            if dbg == 'stopP2':
                break
            R.phase('P2')
            VA = AV(V0, 2080).rearrange("p (t k d) -> p t k d", t=NT, k=2)
            MEMSET(VA[:, :, :, 64:65], 1.0, [Ar(V0, 2080)])
            s_q = load_slot([(lambda sl: sl[:].rearrange("p (k n) -> p k n", k=8),
                              win_d[l, :, 0:512].rearrange("(k p) n -> p k n", p=128))])
            s_kv = load_slot([(lambda sl: sl[:].rearrange("p (k n) -> p k n", k=8)[:, :, 0:256],
                               win_d[l, :, 512:768].rearrange("(k p) n -> p k n", p=128))])
            def z_pipe(ntile, produce, consume, depth=ZDEPTH):
                pend = []
                pb = {}
                for i in range(ntile + depth):
                    if i - depth >= 0:
                        consume(*pend.pop(0))
                    if i < ntile:
                        (ctx, partb) = produce(i)
                        pend.append(ctx)
                        pb[i] = partb
                    if 0 <= i - 1 < ntile:
                        pb.pop(i - 1)()

            def ka_prod(t):
                b = nbank()
                z_matmuls(b, s_kv, t * 128, 256)
                jk = npt()
                kn = pts_[jk][:, 0:128]
                pb_ = qk_norm(b, 0, 2, bc(gains[:, 1, :].unsqueeze(1), [128, 2, 64]), t, True, kn, [PT(jk)],
                              extra=lambda: CP("act", VA[:, t, :, 0:64], ps[b][:, 128:256].rearrange("p (k d) -> p k d", k=2),
                                               [PB(b)], [Ar(V0 + t * 130, 130)]))
                return ((t, kn, jk), pb_)

            def ka_cons(t, kn, jk):
                transpose_to([kn], [PT(jk)], AV(KT0 + t * 128, 128), [Ar(KT0 + t * 128, 128)], "act")

            z_pipe(NT, ka_prod, ka_cons)

            def qa_prod(t):
                b = t % 2
                z_matmuls(b, s_q, t * 128, 512)
                jq = t % 4
                qn = pts_[jq][:, 0:512]
                pb_ = qk_norm(b, 0, 8, bc(gains[:, 0, :].unsqueeze(1), [128, 8, 64]), t, True, qn, [PT(jq)],
                              fixed=(t % 2, 2, 3))
                return ((t, qn, jq), pb_)

            def qa_cons(t, qn, jq):
                bq = 7
                for i in range(4):
                    TR(psb[bq][:, i * 128:(i + 1) * 128], qn[:, i * 128:(i + 1) * 128], ident[:], [PT(jq)], [PB(bq)])
                CP("act", AV(QT0 + t * 512, 512), psb[bq][:, 0:512], [PB(bq)], [Ar(QT0 + t * 512, 512)])

            if dbg == 'stopP3':
                break
            R.phase('P3')
            units = []
            for n in range(NT):
                for kv in range(2):
                    kbs = [kb for kb in (n - 1, n, n + 1) if 0 <= kb < NT]
                    for ii, kb in enumerate(kbs):
                        units.append((n, kv, kb, ii == 0, ii == len(kbs) - 1))
            ctxs = {}
            grp_bo = {}
            due = []

            def a_s1(v):
                (n, kv, kb, first, last) = units[v]
                b = 2 + (v % 3)
                MM(ps[b][:, 0:512], arena[kv * 64:(kv + 1) * 64, KT0 + kb * 128: KT0 + (kb + 1) * 128],
                   arena[kv * 64:(kv + 1) * 64, QT0 + n * 512: QT0 + (n + 1) * 512], True, True,
                   [Ar(KT0 + kb * 128, 128), Ar(QT0 + n * 512, 512)], [PB(b)], sgc=True)
                jp = 4 + (v % 3)
                ACTF(pts_[jp][:, 0:512], ps[b][:, 0:512], AF.Exp, [PB(b)], [PT(jp)])
                if kb != n:
                    w = 0 if kb < n else 1
                    (TT if A_MASK_ENG == "dve" else GTT)(pts_[jp][:, 0:512], pts_[jp][:, 0:512], maskA[:, w * 512:(w + 1) * 512], ALU.mult,
                        [PT(jp), r("maskA")], [PT(jp)])
                ctxs[v] = jp

            def a_s3(v, step):
                (n, kv, kb, first, last) = units[v]
                jp = ctxs.pop(v)
                if first:
                    grp_bo[(n, kv)] = 5 + ((n * 2 + kv) % 2)
                bo = grp_bo[(n, kv)]
                for g in range(4):
                    MM(ps[bo][:, g * 65:(g + 1) * 65], pts_[jp][:, g * 128:(g + 1) * 128], VA[:, kb, kv, :],
                       (first and g == 0), last, [PT(jp), Ar(V0 + kb * 130, 130)], [PB(bo)], sgc=True)
                if not last:
                    return
                pv = ps[bo][:, 0:260].rearrange("p (g d) -> p g d", g=4)
                sd = nstat()
                TT(stats[:, sd, 0:4].unsqueeze(2), pv[:, :, 64:65], esink[:, kv * 4:(kv + 1) * 4].unsqueeze(2), ALU.add,
                   [PB(bo), r("esink")], [ST(sd)])
                RECIP(stats[:, sd, 0:4], stats[:, sd, 0:4], [ST(sd)], [ST(sd)])
                ja = 4 + ((n * 2 + kv) % 2)
                at = scrb[ja][:, 0:256]
                TT(at.rearrange("p (g d) -> p g d", g=4), pv[:, :, 0:64], bc(stats[:, sd, 0:4].unsqueeze(2), [128, 4, 64]),
                   ALU.mult, [PB(bo), ST(sd)], [SC(ja)])

                def s5():
                    dst = AV(AA0, 8192).rearrange("p (c s) -> p c s", c=4)[:, kv * 2:kv * 2 + 2, n * 128:(n + 1) * 128]
                    dres = [("A", AA0 + (kv * 2 + c_) * 2048 + n * 128, AA0 + (kv * 2 + c_) * 2048 + (n + 1) * 128) for c_ in range(2)]
                    bt = 7
                    for i in range(2):
                        TR(psb[bt][:, i * 128:(i + 1) * 128], at[:, i * 128:(i + 1) * 128], ident[:], [SC(ja)], [PB(bt)])
                    CP("act", dst, psb[bt][:, 0:256].rearrange("p (c s) -> p c s", c=2), [PB(bt)], dres)
                due.append((step + 2, s5))

            nun = len(units)
            ast = [0]

            def att_step():
                step = ast[0]
                ast[0] += 1
                if step < nun:
                    a_s1(step)
                if 2 <= step <= nun + 1:
                    a_s3(step - 2, step)
                for (ds, fn) in [d_ for d_ in due if d_[0] <= step]:
                    fn()
                due[:] = [d_ for d_ in due if d_[0] > step]

            first_unit = {}
            for v, u in enumerate(units):
                first_unit.setdefault(u[0], v)
            first_unit[NT] = nun
            qpend = []
            qpb = {}
            for it in range(NT + QDEPTH + 1):
                if it < NT:
                    (ctx, pb_) = qa_prod(it)
                    qpend.append(ctx)
                    qpb[it] = pb_
                nready = it - QDEPTH - 1
                if 0 <= nready < NT:
                    while ast[0] < first_unit[nready + 1]:
                        att_step()
                if 0 <= it - 1 < NT:
                    qpb.pop(it - 1)()
                if it - QDEPTH >= 0 and qpend:
                    qa_cons(*qpend.pop(0))
            while ast[0] < nun + 5:
                att_step()
            assert not due and not qpend and not qpb

            VB = AV(V0, 2080).rearrange("p (t k d) -> p t k d", t=NT, k=2)
            VS = AV(V0 + 2080, 1950).rearrange("p (t k d) -> p t k d", t=15, k=2)
            for hp in range(4):
                MEMSET(VB[:, :, :, 64:65], 1.0, [Ar(V0, 2080)])
                if dbg == 'stopP4%d' % hp:
                    break
                R.phase('P4z%d' % hp)
                s_b = load_slot([
                    (lambda sl: sl[:].rearrange("p (k n) -> p k n", k=8)[:, :, 0:128],
                     win_d[l, :, 768 + hp * 128: 768 + (hp + 1) * 128].rearrange("(k p) n -> p k n", p=128)),
                    (lambda sl: sl[:].rearrange("p (k n) -> p k n", k=8)[:, :, 128:256],
                     win_d[l, :, 1280 + hp * 128: 1280 + (hp + 1) * 128].rearrange("(k p) n -> p k n", p=128)),
                    (lambda sl: sl[:].rearrange("p (k n) -> p k n", k=8)[:, :, 256:384],
                     win_d[l, :, 1792 + hp * 128: 1792 + (hp + 1) * 128].rearrange("(k p) n -> p k n", p=128)),
                ])
                s_bias = load_slot([(lambda sl: sl[:], bias_d[l, hp])])
                for q4 in range(4 if (B_BIAS_MULT and not DBG_NOEXP) else 0):
                    ACTF(slots[s_bias][:, q4 * 1024:(q4 + 1) * 1024], slots[s_bias][:, q4 * 1024:(q4 + 1) * 1024], AF.Exp, [SL(s_bias)], [SL(s_bias)])
                def b_prod(t):
                    b = nbank()
                    z_matmuls(b, s_b, t * 128, 384)
                    jq = npt()
                    qkn = pts_[jq][:, 0:256]
                    pb_ = qk_norm(b, 0, 4, gains[:, 2:6, :], t, False, qkn, [PT(jq)],
                                  extra=lambda: CP("act", VB[:, t, :, 0:64], ps[b][:, 256:384].rearrange("p (k d) -> p k d", k=2),
                                                   [PB(b)], [Ar(V0 + t * 130, 130)]))
                    return ((t, qkn, jq), pb_)

                def b_cons(t, qkn, jq):
                    bq = nbank()
                    TR(psb[bq][:, 0:128], qkn[:, 0:128], ident[:], [PT(jq)], [PB(bq)])
                    TR(psb[bq][:, 128:256], qkn[:, 128:256], ident[:], [PT(jq)], [PB(bq)])
                    lo = QT0 + t * 128
                    dst = arena[:, lo: lo + 2 * (KT0 - QT0)].rearrange("p (c s) -> p c s", c=2)[:, :, 0:128]
                    CP("act" if t % 2 == 0 else "dve", dst, psb[bq][:, 0:256].rearrange("p (c s) -> p c s", c=2), [PB(bq)],
                       [Ar(QT0 + t * 128, 128), Ar(KT0 + t * 128, 128)])

                z_pipe(NT, b_prod, b_cons)
                if VS_DMA:
                    DMA("sp", arena[0:64, V0 + 2080: V0 + 2080 + 1950], arena[64:128, V0: V0 + 1950],
                        [Ar(V0, 2080)], [Ar(V0 + 2080, 1950)], "vs")
                    DMA("sp", arena[64:128, V0 + 2080: V0 + 2080 + 1950], arena[0:64, V0 + 130: V0 + 2080],
                        [Ar(V0, 2080)], [Ar(V0 + 2080, 1950)], "vs")
                else:
                    MEMSET(VS[:, :, :, 64:65], 1.0, [Ar(V0 + 2080, 1950)])
                    for t in range(15):
                        b = nbank()
                        for k in range(8):
                            MM(ps[b][:, 0:128], hT_ap(k, 64 + t * 128, 128), slots[s_b][:, k * 512 + 256: k * 512 + 384],
                               (k == 0), (k == 7), hT_res(64 + t * 128, 128) + [SL(s_b)], [PB(b)])
                        CP("act", VS[:, t, :, 0:64], ps[b][:, 0:128].rearrange("p (k d) -> p k d", k=2),
                           [PB(b)], [Ar(V0 + 2080 + t * 130, 130)])
                if dbg == 'stopP5%d' % hp:
                    break
                R.phase('P5a%d' % hp)
                btr = 7
                bctx = {}

                bk = {}

                def b_qk(rw, hh):
                    rs = min(max(rw - 4, 0), 24)
                    j0 = rs - rw + 7
                    if hh == 0:
                        bk[rw] = nbank()
                    b = bk[rw]
                    for i in range(4):
                        a_ = rs + 2 * i
                        MM(ps[b][:, hh * 256 + i * 64: hh * 256 + (i + 1) * 64],
                           arena[hh * 64:(hh + 1) * 64, KT0 + a_ * 64: KT0 + a_ * 64 + 128],
                           arena[hh * 64:(hh + 1) * 64, QT0 + rw * 64: QT0 + (rw + 1) * 64], (hh == 0 and i == 0), B_BIAS_MULT,
                           [Ar(KT0 + a_ * 64, 128), Ar(QT0 + rw * 64, 64)], [PB(b)], sgc=True)
                    if not B_BIAS_MULT:
                        boff = hh * 2048 + j0 * 256
                        MM(ps[b][:, hh * 256:(hh + 1) * 256], ident[:], slots[s_bias][:, boff:boff + 256], False, True,
                           [r("ident"), SL(s_bias)], [PB(b)], sgc=True)

                def b_s1a(rw):
                    b_qk(rw, 0)

                def b_sep():
                    bd = nbank()
                    MM(ps[bd][:, 0:128], ident[:], ident[:], True, True, [r("ident")], [PB(bd)], sgc=True)

                def b_s1b(rw):
                    b_qk(rw, 1)
                    rs = min(max(rw - 4, 0), 24)
                    j0 = rs - rw + 7
                    b = bk.pop(rw)
                    jp = npt()
                    ACTF(pts_[jp][:, 0:512], ps[b][:, 0:512], AF.Exp, [PB(b)], [PT(jp)])
                    if B_BIAS_MULT and not DBG_NOMULT:
                        ebv = slots[s_bias][:, :].rearrange("p (h x) -> p h x", h=2)[:, :, j0 * 256:(j0 + 1) * 256]
                        (TT if B_MULT_ENG == "dve" else GTT)(
                            pts_[jp][:, 0:512].rearrange("p (h x) -> p h x", h=2), pts_[jp][:, 0:512].rearrange("p (h x) -> p h x", h=2),
                            ebv, ALU.mult, [PT(jp), SL(s_bias)], [PT(jp)])
                    bctx[rw] = (rs, jp)

                def b_s3(rw):
                    (rs, jp) = bctx[rw]
                    bo = nbank()
                    for hh in range(2):
                        for i in range(4):
                            a_ = rs + 2 * i
                            if a_ % 2 == 0:
                                vap = VB[:, a_ // 2, hh, :]
                                vres = Ar(V0 + (a_ // 2) * 130, 130)
                            else:
                                vap = VS[:, (a_ - 1) // 2, hh, :]
                                vres = Ar(V0 + 2080 + ((a_ - 1) // 2) * 130, 130)
                            MM(ps[bo][0:64, hh * 65:(hh + 1) * 65], pts_[jp][:, hh * 256 + i * 64: hh * 256 + (i + 1) * 64], vap,
                               (i == 0), (i == 3), [PT(jp), vres], [PB(bo)], sgc=True)
                    pv = ps[bo][0:64, 0:130].rearrange("p (g d) -> p g d", g=2)
                    sd = nstat()
                    RECIP(stats[0:64, sd, 0:2].unsqueeze(2), pv[:, :, 64:65], [PB(bo)], [ST(sd)])
                    ja = nscr()
                    at = scrb[ja][0:64, 0:128]
                    TT(at.rearrange("p (g d) -> p g d", g=2), pv[:, :, 0:64], bc(stats[0:64, sd, 0:2].unsqueeze(2), [64, 2, 64]),
                       ALU.mult, [PB(bo), ST(sd)], [SC(ja)])
                    bctx[rw] = (at, ja)

                def b_s5(rw):
                    (at, ja) = bctx.pop(rw)
                    rr = rw % 8
                    TR(psb[btr][:, rr * 64:(rr + 1) * 64], at, ident[0:64, 0:64], [SC(ja)], [PB(btr)])
                    if rr == 7:
                        dlo = AB0 + hp * 2048 + (rw // 8) * 512
                        CP("act", AV(dlo, 512), psb[btr][:, 0:512], [PB(btr)], [Ar(dlo, 512)])

                for step in range(32 + 3):
                    if step < 32:
                        b_s1a(step)
                    if 2 <= step <= 33:
                        b_s3(step - 2)
                    elif step < 32 and B_BIAS_MULT:
                        b_sep()
                    if step < 32:
                        b_s1b(step)
                    if 3 <= step:
                        b_s5(step - 3)

            if dbg == 'stopP6':
                break
            R.phase('P6')
            for c in range(8):
                s_w = load_slot([
                    (lambda sl: sl[:, 0:1024].rearrange("p (k n) -> p k n", k=8),
                     win_d[l, :, 2304 + c * 128: 2304 + (c + 1) * 128].rearrange("(k p) n -> p k n", p=128)),
                    (lambda sl: sl[:, 1024:2048].rearrange("p (k n) -> p k n", k=8),
                     win_d[l, :, 3328 + c * 128: 3328 + (c + 1) * 128].rearrange("(k p) n -> p k n", p=128)),
                    (lambda sl: sl[:, 2048:2560].rearrange("p (k n) -> p k n", k=4),
                     wpa_d[l, :, c * 128:(c + 1) * 128].rearrange("(k p) n -> p k n", p=128)),
                    (lambda sl: sl[:, 2560:3072].rearrange("p (k n) -> p k n", k=4),
                     wpb_d[l, :, c * 128:(c + 1) * 128].rearrange("(k p) n -> p k n", p=128)),
                ])
                for tg in range(4):
                    bga, bgb, bya, byb = nbank(), nbank(), nbank(), nbank()
                    for (bb, woff) in ((bga, 0), (bgb, 1024)):
                        for k in range(8):
                            MM(ps[bb][:, 0:512], slots[s_w][:, woff + k * 128: woff + (k + 1) * 128], hT_ap(k, tg * 512, 512),
                               (k == 0), (k == 7), hT_res(tg * 512, 512) + [SL(s_w)], [PB(bb)])
                    for (bb, woff, a0) in ((bya, 2048, AA0), (byb, 2560, AB0)):
                        for k in range(4):
                            lo = a0 + k * 2048 + tg * 512
                            MM(ps[bb][:, 0:512], slots[s_w][:, woff + k * 128: woff + (k + 1) * 128], arena[:, lo:lo + 512],
                               (k == 0), (k == 3), [("A", lo, lo + 512), SL(s_w)], [PB(bb)])
                    j1, j2 = nscr(), nscr()
                    ACTF(scr[j1][:, 0:512], ps[bga][:, 0:512], AF.Sigmoid, [PB(bga)], [SC(j1)])
                    ACTF(scr[j2][:, 0:512], ps[bgb][:, 0:512], AF.Sigmoid, [PB(bgb)], [SC(j2)])
                    TT(scr[j1][:, 0:512], scr[j1][:, 0:512], ps[bya][:, 0:512], ALU.mult, [SC(j1), PB(bya)], [SC(j1)])
                    TT(scr[j2][:, 0:512], scr[j2][:, 0:512], ps[byb][:, 0:512], ALU.mult, [SC(j2), PB(byb)], [SC(j2)])
                    mlo = MX0 + c * 2048 + tg * 512
                    TT(AV(mlo, 512), scr[j1][:, 0:512], scr[j2][:, 0:512], ALU.add, [SC(j1), SC(j2)], [Ar(mlo, 512)])

            R.phase('P7')
            for hf in range(2):
                s_o = load_slot([(lambda sl: sl[:].rearrange("p (k n) -> p k n", k=8),
                                  wout_d[l, :, hf * 512:(hf + 1) * 512].rearrange("(k p) n -> p k n", p=128))])
                for t in range(NT):
                    b = nbank()
                    for k in range(8):
                        lo = MX0 + k * 2048 + t * 128
                        MM(ps[b][:, 0:512], arena[:, lo:lo + 128], slots[s_o][:, k * 512:(k + 1) * 512],
                           (k == 0), (k == 7), [("A", lo, lo + 128), SL(s_o)], [PB(b)])
                    xs = x_sb[:, t, hf * 512:(hf + 1) * 512]
                    TT(xs, xs, ps[b][:, 0:512], ALU.add, [X(t), PB(b)], [X(t)])

            if dbg == 'attn':
                dbg_d = dt('dbg_ab', [128, 16384], BF16, kind='ExternalOutput').ap()
                DMA('sp', dbg_d, AV(AA0, 16384), [Ar(AA0, 16384)], [], 'dbgout')
                continue
            R.phase('P8')
            norm_phase(l, fng_d, 1)

            R.phase('P9')
            f0 = 0
            for fgn in (8, 8, 6):
                for fp in range(fgn // 2):
                    fa = f0 + fp * 2
                    s_gu = load_slot([
                        (lambda sl: sl[:, 0:2048].rearrange("p (k n) -> p k n", k=8),
                         wg_d[l, :, fa * 128:(fa + 2) * 128].rearrange("(k p) n -> p k n", p=128)),
                        (lambda sl: sl[:, 2048:4096].rearrange("p (k n) -> p k n", k=8),
                         wu_d[l, :, fa * 128:(fa + 2) * 128].rearrange("(k p) n -> p k n", p=128)),
                    ])
                    for q2 in range(2):
                        fl = fp * 2 + q2
                        for tg in range(4):
                            bg, bu = nbank(), nbank()
                            for (bb, woff) in ((bg, 0), (bu, 2048)):
                                for k in range(8):
                                    wlo = woff + k * 256 + q2 * 128
                                    MM(ps[bb][:, 0:512], slots[s_gu][:, wlo:wlo + 128], hT_ap(k, tg * 512, 512),
                                       (k == 0), (k == 7), hT_res(tg * 512, 512) + [SL(s_gu)], [PB(bb)])
                            j1 = nscr()
                            ACTF(scr[j1][:, 0:512], ps[bg][:, 0:512], AF.Silu, [PB(bg)], [SC(j1)])
                            ulo = UT0 + fl * 2048 + tg * 512
                            TT(AV(ulo, 512), scr[j1][:, 0:512], ps[bu][:, 0:512], ALU.mult, [SC(j1), PB(bu)], [Ar(ulo, 512)])
                dsl = []
                for j in range(0, fgn, 4):
                    nch = min(4, fgn - j)
                    s_d = load_slot([(lambda sl, nch=nch: sl[:, 0:nch * 1024].rearrange("p (k n) -> p k n", k=nch),
                                      wd_d[l, (f0 + j) * 128:(f0 + j + nch) * 128, :].rearrange("(k p) n -> p k n", p=128))])
                    dsl.append(s_d)
                for t in range(NT):
                    for hf in range(2):
                        b = nbank()
                        for j in range(fgn):
                            s_d = dsl[j // 4]
                            lo = UT0 + j * 2048 + t * 128
                            wlo = (j % 4) * 1024 + hf * 512
                            MM(ps[b][:, 0:512], arena[:, lo:lo + 128], slots[s_d][:, wlo:wlo + 512],
                               (j == 0), (j == fgn - 1), [("A", lo, lo + 128), SL(s_d)], [PB(b)])
                        xs = x_sb[:, t, hf * 512:(hf + 1) * 512]
                        TT(xs, xs, ps[b][:, 0:512], ALU.add, [X(t), PB(b)], [X(t)])
                f0 += fgn

        for t in range(NT):
            DMA("sp", out_d[t * 128:(t + 1) * 128, :], x_sb[:, t, :], [X(t)], [], "out")
        R.phase('end')
        global PHASES
        PHASES = list(R.phases)
        R.emit(final_waits=[(R.dmasems["out"], R.dmacnt["out"])])
    return nc


def _consts():
    ident = np.eye(128, dtype=np.float32)
    j = np.arange(128)[:, None]
    rq = np.arange(128)[None, :]
    m_prev = np.where(j >= rq, 1.0, 0.0).astype(np.float32)
    m_next = np.where(j <= rq, 1.0, 0.0).astype(np.float32)
    mask = np.concatenate([np.tile(m_prev, (1, 4)), np.tile(m_next, (1, 4))], axis=1)
    half = 32
    inv = (10000.0 ** (-np.arange(half, dtype=np.float32) / np.float32(half))).astype(np.float32)
    invf = np.tile(inv[None, :], (128, 1)).astype(np.float32)
    return ident, np.ascontiguousarray(mask), invf


def _bias_tiles(rel_bias):
    L = rel_bias.shape[0]
    o = np.arange(2)
    j0 = np.arange(8)
    i = np.arange(4)
    DR = np.minimum(j0[None, :, None] + 2 * i[None, None, :] + o[:, None, None], 14)
    kc = np.arange(64)[:, None]
    qc = np.arange(64)[None, :]
    DC = np.clip(kc - qc + 15, 0, 30)
    cs = np.clip(qc - 8, 0, 48)
    valid = (kc >= cs) & (kc < cs + 16)
    g = rel_bias[:, :, DR[:, None, :, :, None], DC[None, :, None, None, :]]
    g = np.where(valid[None, None, None, :, None, None, :], g, np.float32(NEG)).astype(np.float32)
    g = g.reshape(L, 4, 2, 128, 8, 4, 64).transpose(0, 1, 3, 2, 4, 5, 6)
    return np.ascontiguousarray(g.reshape(L, 4, 128, 4096))


def _perm_w_in(w_in):
    L = w_in.shape[0]
    qa = w_in[:, :, 0:512].reshape(L, D, 2, 4, 64).transpose(0, 1, 3, 2, 4).reshape(L, D, 512)
    return np.ascontiguousarray(np.concatenate([qa, w_in[:, :, 512:]], axis=2))


def _gT(g):
    return np.ascontiguousarray(g.reshape(g.shape[0], 8, 128).transpose(0, 2, 1))


_NC_CACHE = {}


def _get_nc(nl):
    if nl not in _NC_CACHE:
        _NC_CACHE[nl] = build(nl)
    return _NC_CACHE[nl]


def kernel(x, positions, attn_norm_g, w_in, q_norm_a, k_norm_a, sink_a, q_norm_b, k_norm_b,
           rel_bias_b, w_proj_a, w_proj_b, w_out, ffn_norm_g, w_gate, w_up, w_down):
    f = lambda a: np.ascontiguousarray(np.asarray(a, dtype=np.float32))
    x = f(x)
    pos = np.ascontiguousarray(np.asarray(positions, dtype=np.int32).reshape(NT, 128).T)
    ident, mask, invf = _consts()
    w_in_p = _perm_w_in(f(w_in))
    bias_t = _bias_tiles(f(rel_bias_b))
    per_layer = {
        "attn_norm_g": _gT(f(attn_norm_g)), "w_in": w_in_p, "q_norm_a": f(q_norm_a), "k_norm_a": f(k_norm_a),
        "sink_a": f(sink_a), "q_norm_b": f(q_norm_b), "k_norm_b": f(k_norm_b), "bias_t": bias_t,
        "w_proj_a": f(w_proj_a), "w_proj_b": f(w_proj_b), "w_out": f(w_out), "ffn_norm_g": _gT(f(ffn_norm_g)),
        "w_gate": f(w_gate), "w_up": f(w_up), "w_down": f(w_down),
    }
    consts = {"pos": pos, "ident": ident, "mask_a": mask, "invf": invf}
    B = x.shape[0]
    if FUSED:
        nc = _get_nc(DEPTH)
        in_maps = []
        for b in range(B):
            m = {"x": x[b]}
            m.update(consts)
            m.update(per_layer)
            in_maps.append(m)
        res = run_bass_kernel_spmd(nc, in_maps, core_ids=list(range(B)))
        return np.stack([res.results[b]["out"] for b in range(B)], axis=0)
    nc = _get_nc(1)
    cur = [x[b] for b in range(B)]
    for l in range(DEPTH):
        in_maps = []
        for b in range(B):
            m = {"x": cur[b]}
            m.update(consts)
            m.update({k: np.ascontiguousarray(v[l:l + 1]) for k, v in per_layer.items()})
            in_maps.append(m)
        res = run_bass_kernel_spmd(nc, in_maps, core_ids=list(range(B)))
        cur = [np.ascontiguousarray(res.results[b]["out"]) for b in range(B)]
    return np.stack(cur, axis=0)
```

```python
import numpy as np
from contextlib import ExitStack
import concourse.bass as bass
import concourse.mybir as mybir
from concourse.bass_utils import run_bass_kernel_spmd

F32 = mybir.dt.float32
BF16 = mybir.dt.bfloat16
I32 = mybir.dt.int32
ALU = mybir.AluOpType
AF = mybir.ActivationFunctionType
AX = mybir.AxisListType

S = 2048
D = 1024
NT = 16
FF = 2816
NFC = 22
INC = 4352
NEG = -30000.0
EPS = 1e-6
DEPTH = 4
FUSED = True
ROPE2_ENG = "pool"
VS_DMA = True
NPT = 7
NSCR = 6
ZDEPTH = 4
QDEPTH = 3
BLAG = 3
B_MULT_ENG = "dve"
A_MASK_ENG = "dve"
B_BIAS_MULT = True
DBG_NOEXP = False
DBG_NOMULT = False

ENGS = ["pe", "act", "dve", "pool", "sp"]


class Op:
    __slots__ = ("eng", "fn", "deps", "marked", "count", "sem", "is_dma", "idx")

    def __init__(self, eng, fn):
        self.eng = eng
        self.fn = fn
        self.deps = []
        self.marked = False
        self.count = 0
        self.sem = None
        self.is_dma = False


class Rec:
    def __init__(self, nc, stack):
        self.nc = nc
        self.stack = stack
        self.ops = {e: [] for e in ENGS}
        self.hist = {}
        self.names = {}
        self.engsem = {e: stack.enter_context(nc.semaphore("s_" + e)) for e in ENGS}
        self.dmasems = {}
        self.dmacnt = {}
        self.nops = 0
        self.phases = []

    def phase(self, name):
        self.phases.append((name, len(self.ops['pe'])))

    def res(self, name):
        if name not in self.names:
            self.names[name] = len(self.names)
        i = self.names[name]
        return ("N", i, i + 1)

    def _sem_for(self, key):
        if key not in self.dmasems:
            self.dmasems[key] = self.stack.enter_context(self.nc.semaphore("d_%d" % len(self.dmasems)))
            self.dmacnt[key] = 0
        return self.dmasems[key]

    def add(self, eng, fn, reads=(), writes=(), dma=None):
        op = Op(eng, fn)
        op.idx = self.nops
        self.nops += 1
        deps = {}
        for (sp, lo, hi) in reads:
            for rec in self.hist.get(sp, ()):
                if rec[0] < hi and lo < rec[1] and (rec[3] or (sp == "P" and rec[2].eng != eng)):
                    deps[id(rec[2])] = rec[2]
        for (sp, lo, hi) in writes:
            for rec in self.hist.get(sp, ()):
                if rec[0] < hi and lo < rec[1]:
                    deps[id(rec[2])] = rec[2]
        for d in deps.values():
            if d.eng == "pe" and eng == "pe":
                continue
            d.marked = True
            op.deps.append(d)
        if dma is not None:
            op.is_dma = True
            op.sem = self._sem_for(dma)
            self.dmacnt[dma] += 16
            op.count = self.dmacnt[dma]
        for (sp, lo, hi) in writes:
            h = self.hist.setdefault(sp, [])
            h[:] = [r for r in h if not (lo <= r[0] and r[1] <= hi)]
            h.append([lo, hi, op, True])
        for (sp, lo, hi) in reads:
            h = self.hist.setdefault(sp, [])
            if not op.is_dma:
                h[:] = [r for r in h if not ((not r[3]) and r[0] == lo and r[1] == hi
                                             and r[2].eng == eng and not r[2].is_dma)]
            h.append([lo, hi, op, False])
        self.ops[eng].append(op)
        return op

    def emit(self, final_waits=()):
        nc = self.nc
        for e in ENGS:
            c = 0
            for op in self.ops[e]:
                if op.is_dma:
                    continue
                if op.marked:
                    c += 1
                    op.count = c
                    op.sem = self.engsem[e]
        ops = self.ops
        engsem = self.engsem

        def make_body(e):
            def body(eng):
                waited = {}
                for op in ops[e]:
                    need = {}
                    for d in op.deps:
                        k = id(d.sem)
                        if k not in need or need[k][1] < d.count:
                            need[k] = (d.sem, d.count)
                    for k, (sem, val) in need.items():
                        if waited.get(k, 0) >= val:
                            continue
                        eng.wait_ge(sem, val)
                        waited[k] = val
                    ins = op.fn(eng)
                    if op.is_dma:
                        ins.then_inc(op.sem, 16)
                    elif op.marked:
                        ins.then_inc(engsem[e], 1)
                if e == "sp":
                    for (sem, val) in final_waits:
                        eng.wait_ge(sem, val)
            return body

        with nc.Block() as block:
            block.tensor(make_body("pe"))
            block.scalar(make_body("act"))
            block.vector(make_body("dve"))
            block.gpsimd(make_body("pool"))
            block.sync(make_body("sp"))


def bc(ap, shape):
    return ap.to_broadcast(list(shape))


def build(nl, dbg=None):
    nc = bass.Bass("TRN2", target_bir_lowering=False)
    dt = nc.dram_tensor
    x_d = dt("x", [S, D], F32, kind="ExternalInput").ap()
    pos_d = dt("pos", [128, NT], I32, kind="ExternalInput").ap()
    ang_d = dt("attn_norm_g", [nl, 128, 8], F32, kind="ExternalInput").ap()
    win_d = dt("w_in", [nl, D, INC], F32, kind="ExternalInput").ap()
    qna_d = dt("q_norm_a", [nl, 64], F32, kind="ExternalInput").ap()
    kna_d = dt("k_norm_a", [nl, 64], F32, kind="ExternalInput").ap()
    snk_d = dt("sink_a", [nl, 8], F32, kind="ExternalInput").ap()
    qnb_d = dt("q_norm_b", [nl, 64], F32, kind="ExternalInput").ap()
    knb_d = dt("k_norm_b", [nl, 64], F32, kind="ExternalInput").ap()
    bias_d = dt("bias_t", [nl, 4, 128, 4096], F32, kind="ExternalInput").ap()
    wpa_d = dt("w_proj_a", [nl, 512, D], F32, kind="ExternalInput").ap()
    wpb_d = dt("w_proj_b", [nl, 512, D], F32, kind="ExternalInput").ap()
    wout_d = dt("w_out", [nl, D, D], F32, kind="ExternalInput").ap()
    fng_d = dt("ffn_norm_g", [nl, 128, 8], F32, kind="ExternalInput").ap()
    wg_d = dt("w_gate", [nl, D, FF], F32, kind="ExternalInput").ap()
    wu_d = dt("w_up", [nl, D, FF], F32, kind="ExternalInput").ap()
    wd_d = dt("w_down", [nl, FF, D], F32, kind="ExternalInput").ap()
    ident_d = dt("ident", [128, 128], F32, kind="ExternalInput").ap()
    mask_d = dt("mask_a", [128, 1024], F32, kind="ExternalInput").ap()
    invf_d = dt("invf", [128, 32], F32, kind="ExternalInput").ap()
    out_d = dt("out", [S, D], F32, kind="ExternalOutput").ap()

    with ExitStack() as st:
        R = Rec(nc, st)
        sbt = lambda n, s, d: st.enter_context(nc.sbuf_tensor(n, s, d))
        x_sb = sbt("x_sb", [128, NT, D], F32)
        AR_N = 47104
        arena = sbt("arena", [128, AR_N], BF16)
        slots = [sbt("slot%d" % i, [128, 4096], BF16) for i in range(3)]
        scr = [sbt("scr%d" % i, [128, 512], F32) for i in range(NSCR)]
        scrb = [s_.bitcast(BF16) for s_ in scr]
        pts_ = [sbt("pt%d" % i, [128, 512], BF16) for i in range(NPT)]
        gT = sbt("gT", [128, 2, 8], F32)
        ident = sbt("ident_sb", [128, 128], BF16)
        maskA = sbt("maskA_sb", [128, 1024], BF16)
        cos_t = sbt("cos_t", [128, NT, 32], F32)
        sin_t = sbt("sin_t", [128, NT, 32], F32)
        gains = sbt("gains", [128, 6, 64], F32)
        epsb = sbt("epsb", [128, 2], F32)
        esink = sbt("esink", [128, 8], F32)
        stats = sbt("stats", [128, 15, 16], F32)
        ps = [st.enter_context(nc.psum_tensor("ps%d" % i, [128, 512], F32)) for i in range(8)]
        psb = [p_.bitcast(BF16) for p_ in ps]

        HT0 = 0
        QT0 = 16384
        KT0 = QT0 + 8192
        V0 = KT0 + 2048
        AA0 = V0 + 4096
        AB0 = AA0 + 8192
        MX0 = QT0
        UT0 = QT0
        assert AB0 + 8192 == AR_N

        def AV(lo, n):
            return arena[:, lo:lo + n]

        def Ar(lo, n):
            return ("A", lo, lo + n)

        def X(t):
            return ("X", t, t + 1)

        def PB(b):
            return ("P", b, b + 1)

        def SL(s_):
            return ("W", s_, s_ + 1)

        def SC(i):
            return ("S", i, i + 1)

        r = R.res
        state = {"bank": 0, "scr": 0, "stat": 0, "slot": 0, "pt": 0}

        def nbank():
            b = state["bank"]
            state["bank"] = (b + 1) % 7
            return b

        def nscr():
            i = state["scr"]
            state["scr"] = (i + 1) % NSCR
            return i

        def nstat():
            i = state["stat"]
            state["stat"] = (i + 1) % 15
            return i

        def nslot():
            i = state["slot"]
            state["slot"] = (i + 1) % 3
            return i

        def ST(i):
            return ("T", i, i + 1)

        def PT(i):
            return ("Q", i, i + 1)

        def npt():
            i = state["pt"]
            state["pt"] = (i + 1) % NPT
            return i

        def MM(out, lhsT, rhs, start, stop, reads, writes, sgc=False):
            if sgc:
                R.add("pe", lambda e: e.matmul(out, lhsT=lhsT, rhs=rhs, start=start, stop=stop, skip_group_check=True),
                      reads, writes)
            else:
                R.add("pe", lambda e: e.matmul(out, lhsT=lhsT, rhs=rhs, start=start, stop=stop), reads, writes)

        def TR(out, in_, idn, reads, writes):
            R.add("pe", lambda e: e.transpose(out=out, in_=in_, identity=idn), reads + [r("ident")], writes)

        def ACTF(out, in_, func, reads, writes, scale=None, accum=None, bias=None):
            kw = {}
            if bias is not None:
                kw["bias"] = bias
            if scale is not None:
                kw["scale"] = scale
            if accum is not None:
                kw["accum_out"] = accum
            R.add("act", lambda e: e.activation(out=out, in_=in_, func=func, **kw), reads, writes)

        def TT(out, in0, in1, op, reads, writes):
            R.add("dve", lambda e: e.tensor_tensor(out=out, in0=in0, in1=in1, op=op), reads, writes)

        def PTT(out, in0, in1, op, reads, writes):
            R.add(ROPE2_ENG, lambda e: e.tensor_tensor(out=out, in0=in0, in1=in1, op=op), reads, writes)

        def GTT(out, in0, in1, op, reads, writes):
            R.add("pool", lambda e: e.tensor_tensor(out=out, in0=in0, in1=in1, op=op), reads, writes)

        def TS(out, in0, s1, s2, op0, op1, reads, writes):
            if s2 is None:
                R.add("dve", lambda e: e.tensor_scalar(out=out, in0=in0, scalar1=s1, scalar2=None, op0=op0), reads, writes)
            else:
                R.add("dve", lambda e: e.tensor_scalar(out=out, in0=in0, scalar1=s1, scalar2=s2, op0=op0, op1=op1), reads, writes)

        def STT(out, in0, scalar, in1, op0, op1, reads, writes):
            R.add("dve", lambda e: e.scalar_tensor_tensor(out=out, in0=in0, scalar=scalar, in1=in1, op0=op0, op1=op1), reads, writes)

        def CP(eng, out, in_, reads, writes):
            if eng == "act":
                R.add("act", lambda e: e.copy(out=out, in_=in_), reads, writes)
            else:
                R.add("dve", lambda e: e.tensor_copy(out=out, in_=in_), reads, writes)

        def DMA(eng, out, in_, reads, writes, key):
            R.add(eng, lambda e: e.dma_start(out=out, in_=in_), reads, writes, dma=key)

        def MEMSET(ap, val, writes):
            R.add("dve", lambda e: e.memset(ap, val), (), writes)

        def RECIP(out, in_, reads, writes):
            R.add("dve", lambda e: e.reciprocal(out=out, in_=in_), reads, writes)

        def RSUM(out, in_, reads, writes):
            R.add("dve", lambda e: e.reduce_sum(out=out, in_=in_, axis=AX.X), reads, writes)

        for t in range(NT):
            DMA("sp", x_sb[:, t, :], x_d[t * 128:(t + 1) * 128, :], [], [X(t)], "x%d" % t)
        DMA("pool", ident[:], ident_d, [], [r("ident")], "c_ident")
        DMA("pool", maskA[:], mask_d, [], [r("maskA")], "c_mask")
        pos_i = scr[2].bitcast(I32)[:, 0:NT]
        DMA("sp", pos_i, pos_d, [], [SC(2)], "c_pos")
        invf = scr[5][:, 0:32]
        DMA("sp", invf, invf_d, [], [SC(5)], "c_invf")
        MEMSET(epsb[:, 0:1], EPS, [r("epsb")])
        MEMSET(epsb[:, 1:2], 64.0 * EPS, [r("epsb")])
        posf = stats[:, 14, :]
        CP("dve", posf, pos_i, [SC(2)], [ST(14)])
        TWO_PI = 2.0 * np.pi
        kf = scr[4][:, 0:512].rearrange("p (a b) -> p a b", a=NT)
        ki = scr[3].bitcast(I32)[:, 0:512].rearrange("p (a b) -> p a b", a=NT)
        for (tab, shift, nm) in ((sin_t, 0.0, "tab_s"), (cos_t, 0.5 * np.pi, "tab_c")):
            TT(tab[:], bc(posf.unsqueeze(2), [128, NT, 32]), bc(invf.unsqueeze(1), [128, NT, 32]), ALU.mult,
               [ST(14), SC(5)], [r(nm)])
            if shift != 0.0:
                TS(tab[:], tab[:], float(shift), None, ALU.add, None, [r(nm)], [r(nm)])
            TS(kf, tab[:], float(1.0 / TWO_PI), None, ALU.mult, None, [r(nm)], [SC(4)])
            CP("dve", ki, kf, [SC(4)], [SC(3)])
            CP("dve", kf, ki, [SC(3)], [SC(4)])
            STT(tab[:], kf, float(-TWO_PI), tab[:], ALU.mult, ALU.add, [SC(4), r(nm)], [r(nm)])
            TS(tab[:], tab[:], float(np.pi), float(-np.pi), ALU.min, ALU.max, [r(nm)], [r(nm)])
            ACTF(tab[:], tab[:], AF.Sin, [r(nm)], [r(nm)])
        TABS = [r("tab_s"), r("tab_c")]

        def load_slot(parts):
            s_ = nslot()
            for (dst_fn, src) in parts:
                DMA("pool", dst_fn(slots[s_]), src, [], [SL(s_)], "slot%d" % s_)
            return s_

        def rms_stats(ss_ap, out_ap, res_in, res_out, mult, which):
            ACTF(out_ap, ss_ap, AF.Ln, [res_in, r("epsb")], [res_out], scale=float(mult), bias=epsb[:, which:which + 1])
            ACTF(out_ap, out_ap, AF.Exp, [res_out], [res_out], scale=-0.5)

        def hT_ap(k, lo, n):
            return arena[:, HT0 + k * 2048 + lo: HT0 + k * 2048 + lo + n]

        def hT_res(lo, n):
            return [("A", HT0 + k * 2048 + lo, HT0 + k * 2048 + lo + n) for k in range(8)]

        def norm_phase(l, g_d, gi, per_tile=False):
            DMA("sp", gT[:, gi, :], g_d[l], [], [r("gT%d" % gi)], "gT%d" % gi)
            si = nstat()
            so = nstat()
            if not per_tile:
                for t in range(NT):
                    j = nscr()
                    ACTF(scrb[j][:, 0:1024], x_sb[:, t, :], AF.Square, [X(t)], [SC(j), ST(si)], accum=stats[:, si, t:t + 1])
                rms_stats(stats[:, si, :], stats[:, so, :], ST(si), ST(so), 1.0 / D, 0)
            for t in range(NT):
                if per_tile:
                    j = nscr()
                    ACTF(scrb[j][:, 0:1024], x_sb[:, t, :], AF.Square, [X(t)], [SC(j), ST(si)], accum=stats[:, si, t:t + 1])
                    rms_stats(stats[:, si, t:t + 1], stats[:, so, t:t + 1], ST(si), ST(so), 1.0 / D, 0)
                j = nscr()
                ACTF(scrb[j][:, 0:1024], x_sb[:, t, :], AF.Copy, [X(t), ST(so)], [SC(j)], scale=stats[:, so, t:t + 1])
                b = nbank()
                for c in range(8):
                    TR(psb[b][:, c * 128:(c + 1) * 128], scrb[j][:, c * 128:(c + 1) * 128], ident[:], [SC(j)], [PB(b)])
                dst = AV(HT0, 16384).rearrange("p (c s) -> p c s", c=8)[:, :, t * 128:(t + 1) * 128]
                src = psb[b][:, 0:1024].rearrange("p (c s) -> p c s", c=8)
                TT(dst, src, bc(gT[:, gi, :].unsqueeze(2), [128, 8, 128]), ALU.mult, [PB(b), r("gT%d" % gi)], hT_res(t * 128, 128))

        def z_matmuls(b, s_, tok_lo, ncols):
            for k in range(8):
                MM(ps[b][:, 0:ncols], hT_ap(k, tok_lo, 128), slots[s_][:, k * 512: k * 512 + ncols],
                   (k == 0), (k == 7), hT_res(tok_lo, 128) + [SL(s_)], [PB(b)])

        def qk_norm(b, col_lo, nh, gain_ap, t, rope, out_ap, out_res, extra=None, fixed=None):
            n = nh * 64
            zin = ps[b][:, col_lo:col_lo + n]
            j1 = nscr() if fixed is None else fixed[0]
            si = nstat()
            so = nstat()
            ACTF(scr[j1][:, 0:n], zin, AF.Square, [PB(b)], [SC(j1)])
            if extra is not None:
                extra()
            RSUM(stats[:, si, 0:nh], scr[j1][:, 0:n].rearrange("p (h d) -> p h d", h=nh), [SC(j1)], [ST(si)])

            def partb():
                rms_stats(stats[:, si, 0:nh], stats[:, so, 0:nh], ST(si), ST(so), 1.0, 1)
                t1 = scr[j1][:, 0:n].rearrange("p (h d) -> p h d", h=nh)
                TT(t1, zin.rearrange("p (h d) -> p h d", h=nh), bc(stats[:, so, 0:nh].unsqueeze(2), [128, nh, 64]), ALU.mult,
                   [PB(b), ST(so)], [SC(j1)])
                o3 = out_ap.rearrange("p (h d) -> p h d", h=nh)
                if not rope:
                    TT(o3, t1, gain_ap, ALU.mult, [SC(j1), r("gains")], out_res)
                    return
                GTT(t1, t1, gain_ap, ALU.mult, [SC(j1), r("gains")], [SC(j1)])
                j3 = nscr() if fixed is None else fixed[1]
                tmp = scr[j3][:, 0:n].rearrange("p (h d) -> p h d", h=nh)
                x1 = t1[:, :, 0:32]
                x2 = t1[:, :, 32:64]
                cb = bc(cos_t[:, t, :].unsqueeze(1), [128, nh, 32])
                sb_ = bc(sin_t[:, t, :].unsqueeze(1), [128, nh, 32])
                TT(tmp[:, :, 0:32], x1, cb, ALU.mult, [SC(j1)] + TABS, [SC(j3)])
                TT(tmp[:, :, 32:64], x2, sb_, ALU.mult, [SC(j1)] + TABS, [SC(j3)])
                TT(o3[:, :, 0:32], tmp[:, :, 0:32], tmp[:, :, 32:64], ALU.subtract, [SC(j3)], out_res)
                j4 = nscr() if fixed is None else fixed[2]
                tmp2 = scr[j4][:, 0:n].rearrange("p (h d) -> p h d", h=nh)
                PTT(tmp2[:, :, 0:32], x2, cb, ALU.mult, [SC(j1)] + TABS, [SC(j4)])
                PTT(tmp2[:, :, 32:64], x1, sb_, ALU.mult, [SC(j1)] + TABS, [SC(j4)])
                PTT(o3[:, :, 32:64], tmp2[:, :, 0:32], tmp2[:, :, 32:64], ALU.add, [SC(j4)], out_res)
            return partb

        def transpose_to(src_aps, src_res, dst_ap, dst_res, evac_eng):
            b = nbank()
            for i, sap in enumerate(src_aps):
                TR(psb[b][:, i * 128:(i + 1) * 128], sap, ident[:], src_res, [PB(b)])
            CP(evac_eng, dst_ap, psb[b][:, 0:len(src_aps) * 128], [PB(b)], dst_res)

        for l in range(nl):
            for gi, gd in enumerate((qna_d, kna_d, qnb_d, qnb_d, knb_d, knb_d)):
                DMA("sp", gains[:, gi, :], gd[l].partition_broadcast(128), [], [r("gains")], "gains")
            TS(gains[:, 1, :], gains[:, 1, :], 8.0, None, ALU.mult, None, [r("gains")], [r("gains")])
            TS(gains[:, 4:6, :], gains[:, 4:6, :], 8.0, None, ALU.mult, None, [r("gains")], [r("gains")])
            DMA("sp", esink[:], snk_d[l].partition_broadcast(128), [], [r("esink")], "esink")
            ACTF(esink[:], esink[:], AF.Exp, [r("esink")], [r("esink")])

            R.phase('P1')
            norm_phase(l, ang_d, 0, per_tile=False)

            if dbg == 'stopP2':
                break
            R.phase('P2')
            VA = AV(V0, 2080).rearrange("p (t k d) -> p t k d", t=NT, k=2)
            MEMSET(VA[:, :, :, 64:65], 1.0, [Ar(V0, 2080)])
            s_q = load_slot([(lambda sl: sl[:].rearrange("p (k n) -> p k n", k=8),
                              win_d[l, :, 0:512].rearrange("(k p) n -> p k n", p=128))])
            s_kv = load_slot([(lambda sl: sl[:].rearrange("p (k n) -> p k n", k=8)[:, :, 0:256],
                               win_d[l, :, 512:768].rearrange("(k p) n -> p k n", p=128))])
            def z_pipe(ntile, produce, consume, depth=ZDEPTH):
                pend = []
                pb = {}
                for i in range(ntile + depth):
                    if i - depth >= 0:
                        consume(*pend.pop(0))
                    if i < ntile:
                        (ctx, partb) = produce(i)
                        pend.append(ctx)
                        pb[i] = partb
                    if 0 <= i - 1 < ntile:
                        pb.pop(i - 1)()

            def ka_prod(t):
                b = nbank()
                z_matmuls(b, s_kv, t * 128, 256)
                jk = npt()
                kn = pts_[jk][:, 0:128]
                pb_ = qk_norm(b, 0, 2, bc(gains[:, 1, :].unsqueeze(1), [128, 2, 64]), t, True, kn, [PT(jk)],
                              extra=lambda: CP("act", VA[:, t, :, 0:64], ps[b][:, 128:256].rearrange("p (k d) -> p k d", k=2),
                                               [PB(b)], [Ar(V0 + t * 130, 130)]))
                return ((t, kn, jk), pb_)

            def ka_cons(t, kn, jk):
                transpose_to([kn], [PT(jk)], AV(KT0 + t * 128, 128), [Ar(KT0 + t * 128, 128)], "act")

            z_pipe(NT, ka_prod, ka_cons)

            def qa_prod(t):
                b = t % 2
                z_matmuls(b, s_q, t * 128, 512)
                jq = t % 4
                qn = pts_[jq][:, 0:512]
                pb_ = qk_norm(b, 0, 8, bc(gains[:, 0, :].unsqueeze(1), [128, 8, 64]), t, True, qn, [PT(jq)],
                              fixed=(t % 2, 2, 3))
                return ((t, qn, jq), pb_)

            def qa_cons(t, qn, jq):
                bq = 7
                for i in range(4):
                    TR(psb[bq][:, i * 128:(i + 1) * 128], qn[:, i * 128:(i + 1) * 128], ident[:], [PT(jq)], [PB(bq)])
                CP("act", AV(QT0 + t * 512, 512), psb[bq][:, 0:512], [PB(bq)], [Ar(QT0 + t * 512, 512)])

            if dbg == 'stopP3':
                break
            R.phase('P3')
            units = []
            for n in range(NT):
                for kv in range(2):
                    kbs = [kb for kb in (n - 1, n, n + 1) if 0 <= kb < NT]
                    for ii, kb in enumerate(kbs):
                        units.append((n, kv, kb, ii == 0, ii == len(kbs) - 1))
            ctxs = {}
            grp_bo = {}
            due = []

            def a_s1(v):
                (n, kv, kb, first, last) = units[v]
                b = 2 + (v % 3)
                MM(ps[b][:, 0:512], arena[kv * 64:(kv + 1) * 64, KT0 + kb * 128: KT0 + (kb + 1) * 128],
                   arena[kv * 64:(kv + 1) * 64, QT0 + n * 512: QT0 + (n + 1) * 512], True, True,
                   [Ar(KT0 + kb * 128, 128), Ar(QT0 + n * 512, 512)], [PB(b)], sgc=True)
                jp = 4 + (v % 3)
                ACTF(pts_[jp][:, 0:512], ps[b][:, 0:512], AF.Exp, [PB(b)], [PT(jp)])
                if kb != n:
                    w = 0 if kb < n else 1
                    (TT if A_MASK_ENG == "dve" else GTT)(pts_[jp][:, 0:512], pts_[jp][:, 0:512], maskA[:, w * 512:(w + 1) * 512], ALU.mult,
                        [PT(jp), r("maskA")], [PT(jp)])
                ctxs[v] = jp

            def a_s3(v, step):
                (n, kv, kb, first, last) = units[v]
                jp = ctxs.pop(v)
                if first:
                    grp_bo[(n, kv)] = 5 + ((n * 2 + kv) % 2)
                bo = grp_bo[(n, kv)]
                for g in range(4):
                    MM(ps[bo][:, g * 65:(g + 1) * 65], pts_[jp][:, g * 128:(g + 1) * 128], VA[:, kb, kv, :],
                       (first and g == 0), last, [PT(jp), Ar(V0 + kb * 130, 130)], [PB(bo)], sgc=True)
                if not last:
                    return
                pv = ps[bo][:, 0:260].rearrange("p (g d) -> p g d", g=4)
                sd = nstat()
                TT(stats[:, sd, 0:4].unsqueeze(2), pv[:, :, 64:65], esink[:, kv * 4:(kv + 1) * 4].unsqueeze(2), ALU.add,
                   [PB(bo), r("esink")], [ST(sd)])
                RECIP(stats[:, sd, 0:4], stats[:, sd, 0:4], [ST(sd)], [ST(sd)])
                ja = 4 + ((n * 2 + kv) % 2)
                at = scrb[ja][:, 0:256]
                TT(at.rearrange("p (g d) -> p g d", g=4), pv[:, :, 0:64], bc(stats[:, sd, 0:4].unsqueeze(2), [128, 4, 64]),
                   ALU.mult, [PB(bo), ST(sd)], [SC(ja)])

                def s5():
                    dst = AV(AA0, 8192).rearrange("p (c s) -> p c s", c=4)[:, kv * 2:kv * 2 + 2, n * 128:(n + 1) * 128]
                    dres = [("A", AA0 + (kv * 2 + c_) * 2048 + n * 128, AA0 + (kv * 2 + c_) * 2048 + (n + 1) * 128) for c_ in range(2)]
                    bt = 7
                    for i in range(2):
                        TR(psb[bt][:, i * 128:(i + 1) * 128], at[:, i * 128:(i + 1) * 128], ident[:], [SC(ja)], [PB(bt)])
                    CP("act", dst, psb[bt][:, 0:256].rearrange("p (c s) -> p c s", c=2), [PB(bt)], dres)
                due.append((step + 2, s5))

            nun = len(units)
            ast = [0]

            def att_step():
                step = ast[0]
                ast[0] += 1
                if step < nun:
                    a_s1(step)
                if 2 <= step <= nun + 1:
                    a_s3(step - 2, step)
                for (ds, fn) in [d_ for d_ in due if d_[0] <= step]:
                    fn()
                due[:] = [d_ for d_ in due if d_[0] > step]

            first_unit = {}
            for v, u in enumerate(units):
                first_unit.setdefault(u[0], v)
            first_unit[NT] = nun
            qpend = []
            qpb = {}
            for it in range(NT + QDEPTH + 1):
                if it < NT:
                    (ctx, pb_) = qa_prod(it)
                    qpend.append(ctx)
                    qpb[it] = pb_
                nready = it - QDEPTH - 1
                if 0 <= nready < NT:
                    while ast[0] < first_unit[nready + 1]:
                        att_step()
                if 0 <= it - 1 < NT:
                    qpb.pop(it - 1)()
                if it - QDEPTH >= 0 and qpend:
                    qa_cons(*qpend.pop(0))
            while ast[0] < nun + 5:
                att_step()
            assert not due and not qpend and not qpb

            VB = AV(V0, 2080).rearrange("p (t k d) -> p t k d", t=NT, k=2)
            VS = AV(V0 + 2080, 1950).rearrange("p (t k d) -> p t k d", t=15, k=2)
            for hp in range(4):
                MEMSET(VB[:, :, :, 64:65], 1.0, [Ar(V0, 2080)])
                if dbg == 'stopP4%d' % hp:
                    break
                R.phase('P4z%d' % hp)
                s_b = load_slot([
                    (lambda sl: sl[:].rearrange("p (k n) -> p k n", k=8)[:, :, 0:128],
                     win_d[l, :, 768 + hp * 128: 768 + (hp + 1) * 128].rearrange("(k p) n -> p k n", p=128)),
                    (lambda sl: sl[:].rearrange("p (k n) -> p k n", k=8)[:, :, 128:256],
                     win_d[l, :, 1280 + hp * 128: 1280 + (hp + 1) * 128].rearrange("(k p) n -> p k n", p=128)),
                    (lambda sl: sl[:].rearrange("p (k n) -> p k n", k=8)[:, :, 256:384],
                     win_d[l, :, 1792 + hp * 128: 1792 + (hp + 1) * 128].rearrange("(k p) n -> p k n", p=128)),
                ])
                s_bias = load_slot([(lambda sl: sl[:], bias_d[l, hp])])
                for q4 in range(4 if (B_BIAS_MULT and not DBG_NOEXP) else 0):
                    ACTF(slots[s_bias][:, q4 * 1024:(q4 + 1) * 1024], slots[s_bias][:, q4 * 1024:(q4 + 1) * 1024], AF.Exp, [SL(s_bias)], [SL(s_bias)])
                def b_prod(t):
                    b = nbank()
                    z_matmuls(b, s_b, t * 128, 384)
                    jq = npt()
                    qkn = pts_[jq][:, 0:256]
                    pb_ = qk_norm(b, 0, 4, gains[:, 2:6, :], t, False, qkn, [PT(jq)],
                                  extra=lambda: CP("act", VB[:, t, :, 0:64], ps[b][:, 256:384].rearrange("p (k d) -> p k d", k=2),
                                                   [PB(b)], [Ar(V0 + t * 130, 130)]))
                    return ((t, qkn, jq), pb_)

                def b_cons(t, qkn, jq):
                    bq = nbank()
                    TR(psb[bq][:, 0:128], qkn[:, 0:128], ident[:], [PT(jq)], [PB(bq)])
                    TR(psb[bq][:, 128:256], qkn[:, 128:256], ident[:], [PT(jq)], [PB(bq)])
                    lo = QT0 + t * 128
                    dst = arena[:, lo: lo + 2 * (KT0 - QT0)].rearrange("p (c s) -> p c s", c=2)[:, :, 0:128]
                    CP("act" if t % 2 == 0 else "dve", dst, psb[bq][:, 0:256].rearrange("p (c s) -> p c s", c=2), [PB(bq)],
                       [Ar(QT0 + t * 128, 128), Ar(KT0 + t * 128, 128)])

                z_pipe(NT, b_prod, b_cons)
                if VS_DMA:
                    DMA("sp", arena[0:64, V0 + 2080: V0 + 2080 + 1950], arena[64:128, V0: V0 + 1950],
                        [Ar(V0, 2080)], [Ar(V0 + 2080, 1950)], "vs")
                    DMA("sp", arena[64:128, V0 + 2080: V0 + 2080 + 1950], arena[0:64, V0 + 130: V0 + 2080],
                        [Ar(V0, 2080)], [Ar(V0 + 2080, 1950)], "vs")
                else:
                    MEMSET(VS[:, :, :, 64:65], 1.0, [Ar(V0 + 2080, 1950)])
                    for t in range(15):
                        b = nbank()
                        for k in range(8):
                            MM(ps[b][:, 0:128], hT_ap(k, 64 + t * 128, 128), slots[s_b][:, k * 512 + 256: k * 512 + 384],
                               (k == 0), (k == 7), hT_res(64 + t * 128, 128) + [SL(s_b)], [PB(b)])
                        CP("act", VS[:, t, :, 0:64], ps[b][:, 0:128].rearrange("p (k d) -> p k d", k=2),
                           [PB(b)], [Ar(V0 + 2080 + t * 130, 130)])
                if dbg == 'stopP5%d' % hp:
                    break
                R.phase('P5a%d' % hp)
                btr = 7
                bctx = {}

                bk = {}

                def b_qk(rw, hh):
                    rs = min(max(rw - 4, 0), 24)
                    j0 = rs - rw + 7
                    if hh == 0:
                        bk[rw] = nbank()
                    b = bk[rw]
                    for i in range(4):
                        a_ = rs + 2 * i
                        MM(ps[b][:, hh * 256 + i * 64: hh * 256 + (i + 1) * 64],
                           arena[hh * 64:(hh + 1) * 64, KT0 + a_ * 64: KT0 + a_ * 64 + 128],
                           arena[hh * 64:(hh + 1) * 64, QT0 + rw * 64: QT0 + (rw + 1) * 64], (hh == 0 and i == 0), B_BIAS_MULT,
                           [Ar(KT0 + a_ * 64, 128), Ar(QT0 + rw * 64, 64)], [PB(b)], sgc=True)
                    if not B_BIAS_MULT:
                        boff = hh * 2048 + j0 * 256
                        MM(ps[b][:, hh * 256:(hh + 1) * 256], ident[:], slots[s_bias][:, boff:boff + 256], False, True,
                           [r("ident"), SL(s_bias)], [PB(b)], sgc=True)

                def b_s1a(rw):
                    b_qk(rw, 0)

                def b_sep():
                    bd = nbank()
                    MM(ps[bd][:, 0:128], ident[:], ident[:], True, True, [r("ident")], [PB(bd)], sgc=True)

                def b_s1b(rw):
                    b_qk(rw, 1)
                    rs = min(max(rw - 4, 0), 24)
                    j0 = rs - rw + 7
                    b = bk.pop(rw)
                    jp = npt()
                    ACTF(pts_[jp][:, 0:512], ps[b][:, 0:512], AF.Exp, [PB(b)], [PT(jp)])
                    if B_BIAS_MULT and not DBG_NOMULT:
                        ebv = slots[s_bias][:, :].rearrange("p (h x) -> p h x", h=2)[:, :, j0 * 256:(j0 + 1) * 256]
                        (TT if B_MULT_ENG == "dve" else GTT)(
                            pts_[jp][:, 0:512].rearrange("p (h x) -> p h x", h=2), pts_[jp][:, 0:512].rearrange("p (h x) -> p h x", h=2),
                            ebv, ALU.mult, [PT(jp), SL(s_bias)], [PT(jp)])
                    bctx[rw] = (rs, jp)

                def b_s3(rw):
                    (rs, jp) = bctx[rw]
                    bo = nbank()
                    for hh in range(2):
                        for i in range(4):
                            a_ = rs + 2 * i
                            if a_ % 2 == 0:
                                vap = VB[:, a_ // 2, hh, :]
                                vres = Ar(V0 + (a_ // 2) * 130, 130)
                            else:
                                vap = VS[:, (a_ - 1) // 2, hh, :]
                                vres = Ar(V0 + 2080 + ((a_ - 1) // 2) * 130, 130)
                            MM(ps[bo][0:64, hh * 65:(hh + 1) * 65], pts_[jp][:, hh * 256 + i * 64: hh * 256 + (i + 1) * 64], vap,
                               (i == 0), (i == 3), [PT(jp), vres], [PB(bo)], sgc=True)
                    pv = ps[bo][0:64, 0:130].rearrange("p (g d) -> p g d", g=2)
                    sd = nstat()
                    RECIP(stats[0:64, sd, 0:2].unsqueeze(2), pv[:, :, 64:65], [PB(bo)], [ST(sd)])
                    ja = nscr()
                    at = scrb[ja][0:64, 0:128]
                    TT(at.rearrange("p (g d) -> p g d", g=2), pv[:, :, 0:64], bc(stats[0:64, sd, 0:2].unsqueeze(2), [64, 2, 64]),
                       ALU.mult, [PB(bo), ST(sd)], [SC(ja)])
                    bctx[rw] = (at, ja)

                def b_s5(rw):
                    (at, ja) = bctx.pop(rw)
                    rr = rw % 8
                    TR(psb[btr][:, rr * 64:(rr + 1) * 64], at, ident[0:64, 0:64], [SC(ja)], [PB(btr)])
                    if rr == 7:
                        dlo = AB0 + hp * 2048 + (rw // 8) * 512
                        CP("act", AV(dlo, 512), psb[btr][:, 0:512], [PB(btr)], [Ar(dlo, 512)])

                for step in range(32 + BLAG + 1):
                    if step < 32:
                        b_s1a(step)
                    if BLAG <= step <= 31 + BLAG:
                        b_s3(step - BLAG)
                    elif step < 32 and B_BIAS_MULT:
                        b_sep()
                    if step < 32:
                        b_s1b(step)
                    if BLAG + 1 <= step:
                        b_s5(step - BLAG - 1)

            if dbg == 'stopP6':
                break
            R.phase('P6')
            for c in range(8):
                s_w = load_slot([
                    (lambda sl: sl[:, 0:1024].rearrange("p (k n) -> p k n", k=8),
                     win_d[l, :, 2304 + c * 128: 2304 + (c + 1) * 128].rearrange("(k p) n -> p k n", p=128)),
                    (lambda sl: sl[:, 1024:2048].rearrange("p (k n) -> p k n", k=8),
                     win_d[l, :, 3328 + c * 128: 3328 + (c + 1) * 128].rearrange("(k p) n -> p k n", p=128)),
                    (lambda sl: sl[:, 2048:2560].rearrange("p (k n) -> p k n", k=4),
                     wpa_d[l, :, c * 128:(c + 1) * 128].rearrange("(k p) n -> p k n", p=128)),
                    (lambda sl: sl[:, 2560:3072].rearrange("p (k n) -> p k n", k=4),
                     wpb_d[l, :, c * 128:(c + 1) * 128].rearrange("(k p) n -> p k n", p=128)),
                ])
                for tg in range(4):
                    bga, bgb, bya, byb = nbank(), nbank(), nbank(), nbank()
                    for (bb, woff) in ((bga, 0), (bgb, 1024)):
                        for k in range(8):
                            MM(ps[bb][:, 0:512], slots[s_w][:, woff + k * 128: woff + (k + 1) * 128], hT_ap(k, tg * 512, 512),
                               (k == 0), (k == 7), hT_res(tg * 512, 512) + [SL(s_w)], [PB(bb)])
                    for (bb, woff, a0) in ((bya, 2048, AA0), (byb, 2560, AB0)):
                        for k in range(4):
                            lo = a0 + k * 2048 + tg * 512
                            MM(ps[bb][:, 0:512], slots[s_w][:, woff + k * 128: woff + (k + 1) * 128], arena[:, lo:lo + 512],
                               (k == 0), (k == 3), [("A", lo, lo + 512), SL(s_w)], [PB(bb)])
                    j1, j2 = nscr(), nscr()
                    ACTF(scr[j1][:, 0:512], ps[bga][:, 0:512], AF.Sigmoid, [PB(bga)], [SC(j1)])
                    ACTF(scr[j2][:, 0:512], ps[bgb][:, 0:512], AF.Sigmoid, [PB(bgb)], [SC(j2)])
                    TT(scr[j1][:, 0:512], scr[j1][:, 0:512], ps[bya][:, 0:512], ALU.mult, [SC(j1), PB(bya)], [SC(j1)])
                    TT(scr[j2][:, 0:512], scr[j2][:, 0:512], ps[byb][:, 0:512], ALU.mult, [SC(j2), PB(byb)], [SC(j2)])
                    mlo = MX0 + c * 2048 + tg * 512
                    TT(AV(mlo, 512), scr[j1][:, 0:512], scr[j2][:, 0:512], ALU.add, [SC(j1), SC(j2)], [Ar(mlo, 512)])

            R.phase('P7')
            for hf in range(2):
                s_o = load_slot([(lambda sl: sl[:].rearrange("p (k n) -> p k n", k=8),
                                  wout_d[l, :, hf * 512:(hf + 1) * 512].rearrange("(k p) n -> p k n", p=128))])
                for t in range(NT):
                    b = nbank()
                    for k in range(8):
                        lo = MX0 + k * 2048 + t * 128
                        MM(ps[b][:, 0:512], arena[:, lo:lo + 128], slots[s_o][:, k * 512:(k + 1) * 512],
                           (k == 0), (k == 7), [("A", lo, lo + 128), SL(s_o)], [PB(b)])
                    xs = x_sb[:, t, hf * 512:(hf + 1) * 512]
                    TT(xs, xs, ps[b][:, 0:512], ALU.add, [X(t), PB(b)], [X(t)])

            if dbg == 'attn':
                dbg_d = dt('dbg_ab', [128, 16384], BF16, kind='ExternalOutput').ap()
                DMA('sp', dbg_d, AV(AA0, 16384), [Ar(AA0, 16384)], [], 'dbgout')
                continue
            R.phase('P8')
            norm_phase(l, fng_d, 1)

            R.phase('P9')
            f0 = 0
            for fgn in (8, 8, 6):
                for fp in range(fgn // 2):
                    fa = f0 + fp * 2
                    s_gu = load_slot([
                        (lambda sl: sl[:, 0:2048].rearrange("p (k n) -> p k n", k=8),
                         wg_d[l, :, fa * 128:(fa + 2) * 128].rearrange("(k p) n -> p k n", p=128)),
                        (lambda sl: sl[:, 2048:4096].rearrange("p (k n) -> p k n", k=8),
                         wu_d[l, :, fa * 128:(fa + 2) * 128].rearrange("(k p) n -> p k n", p=128)),
                    ])
                    for q2 in range(2):
                        fl = fp * 2 + q2
                        for tg in range(4):
                            bg, bu = nbank(), nbank()
                            for (bb, woff) in ((bg, 0), (bu, 2048)):
                                for k in range(8):
                                    wlo = woff + k * 256 + q2 * 128
                                    MM(ps[bb][:, 0:512], slots[s_gu][:, wlo:wlo + 128], hT_ap(k, tg * 512, 512),
                                       (k == 0), (k == 7), hT_res(tg * 512, 512) + [SL(s_gu)], [PB(bb)])
                            j1 = nscr()
                            ACTF(scr[j1][:, 0:512], ps[bg][:, 0:512], AF.Silu, [PB(bg)], [SC(j1)])
                            ulo = UT0 + fl * 2048 + tg * 512
                            TT(AV(ulo, 512), scr[j1][:, 0:512], ps[bu][:, 0:512], ALU.mult, [SC(j1), PB(bu)], [Ar(ulo, 512)])
                dsl = []
                for j in range(0, fgn, 4):
                    nch = min(4, fgn - j)
                    s_d = load_slot([(lambda sl, nch=nch: sl[:, 0:nch * 1024].rearrange("p (k n) -> p k n", k=nch),
                                      wd_d[l, (f0 + j) * 128:(f0 + j + nch) * 128, :].rearrange("(k p) n -> p k n", p=128))])
                    dsl.append(s_d)
                for t in range(NT):
                    for hf in range(2):
                        b = nbank()
                        for j in range(fgn):
                            s_d = dsl[j // 4]
                            lo = UT0 + j * 2048 + t * 128
                            wlo = (j % 4) * 1024 + hf * 512
                            MM(ps[b][:, 0:512], arena[:, lo:lo + 128], slots[s_d][:, wlo:wlo + 512],
                               (j == 0), (j == fgn - 1), [("A", lo, lo + 128), SL(s_d)], [PB(b)])
                        xs = x_sb[:, t, hf * 512:(hf + 1) * 512]
                        TT(xs, xs, ps[b][:, 0:512], ALU.add, [X(t), PB(b)], [X(t)])
                f0 += fgn

        for t in range(NT):
            DMA("sp", out_d[t * 128:(t + 1) * 128, :], x_sb[:, t, :], [X(t)], [], "out")
        R.phase('end')
        global PHASES
        PHASES = list(R.phases)
        R.emit(final_waits=[(R.dmasems["out"], R.dmacnt["out"])])
    return nc


def _consts():
    ident = np.eye(128, dtype=np.float32)
    j = np.arange(128)[:, None]
    rq = np.arange(128)[None, :]
    m_prev = np.where(j >= rq, 1.0, 0.0).astype(np.float32)
    m_next = np.where(j <= rq, 1.0, 0.0).astype(np.float32)
    mask = np.concatenate([np.tile(m_prev, (1, 4)), np.tile(m_next, (1, 4))], axis=1)
    half = 32
    inv = (10000.0 ** (-np.arange(half, dtype=np.float32) / np.float32(half))).astype(np.float32)
    invf = np.tile(inv[None, :], (128, 1)).astype(np.float32)
    return ident, np.ascontiguousarray(mask), invf


def _bias_tiles(rel_bias):
    L = rel_bias.shape[0]
    o = np.arange(2)
    j0 = np.arange(8)
    i = np.arange(4)
    DR = np.minimum(j0[None, :, None] + 2 * i[None, None, :] + o[:, None, None], 14)
    kc = np.arange(64)[:, None]
    qc = np.arange(64)[None, :]
    DC = np.clip(kc - qc + 15, 0, 30)
    cs = np.clip(qc - 8, 0, 48)
    valid = (kc >= cs) & (kc < cs + 16)
    g = rel_bias[:, :, DR[:, None, :, :, None], DC[None, :, None, None, :]]
    g = np.where(valid[None, None, None, :, None, None, :], g, np.float32(NEG)).astype(np.float32)
    g = g.reshape(L, 4, 2, 128, 8, 4, 64).transpose(0, 1, 3, 2, 4, 5, 6)
    return np.ascontiguousarray(g.reshape(L, 4, 128, 4096))


def _perm_w_in(w_in):
    L = w_in.shape[0]
    qa = w_in[:, :, 0:512].reshape(L, D, 2, 4, 64).transpose(0, 1, 3, 2, 4).reshape(L, D, 512)
    return np.ascontiguousarray(np.concatenate([qa, w_in[:, :, 512:]], axis=2))


def _gT(g):
    return np.ascontiguousarray(g.reshape(g.shape[0], 8, 128).transpose(0, 2, 1))


_NC_CACHE = {}


def _get_nc(nl):
    if nl not in _NC_CACHE:
        _NC_CACHE[nl] = build(nl)
    return _NC_CACHE[nl]


def kernel(x, positions, attn_norm_g, w_in, q_norm_a, k_norm_a, sink_a, q_norm_b, k_norm_b,
           rel_bias_b, w_proj_a, w_proj_b, w_out, ffn_norm_g, w_gate, w_up, w_down):
    f = lambda a: np.ascontiguousarray(np.asarray(a, dtype=np.float32))
    x = f(x)
    pos = np.ascontiguousarray(np.asarray(positions, dtype=np.int32).reshape(NT, 128).T)
    ident, mask, invf = _consts()
    w_in_p = _perm_w_in(f(w_in))
    bias_t = _bias_tiles(f(rel_bias_b))
    per_layer = {
        "attn_norm_g": _gT(f(attn_norm_g)), "w_in": w_in_p, "q_norm_a": f(q_norm_a), "k_norm_a": f(k_norm_a),
        "sink_a": f(sink_a), "q_norm_b": f(q_norm_b), "k_norm_b": f(k_norm_b), "bias_t": bias_t,
        "w_proj_a": f(w_proj_a), "w_proj_b": f(w_proj_b), "w_out": f(w_out), "ffn_norm_g": _gT(f(ffn_norm_g)),
        "w_gate": f(w_gate), "w_up": f(w_up), "w_down": f(w_down),
    }
    consts = {"pos": pos, "ident": ident, "mask_a": mask, "invf": invf}
    B = x.shape[0]
    if FUSED:
        nc = _get_nc(DEPTH)
        in_maps = []
        for b in range(B):
            m = {"x": x[b]}
            m.update(consts)
            m.update(per_layer)
            in_maps.append(m)
        res = run_bass_kernel_spmd(nc, in_maps, core_ids=list(range(B)))
        return np.stack([res.results[b]["out"] for b in range(B)], axis=0)
    nc = _get_nc(1)
    cur = [x[b] for b in range(B)]
    for l in range(DEPTH):
        in_maps = []
        for b in range(B):
            m = {"x": cur[b]}
            m.update(consts)
            m.update({k: np.ascontiguousarray(v[l:l + 1]) for k, v in per_layer.items()})
            in_maps.append(m)
        res = run_bass_kernel_spmd(nc, in_maps, core_ids=list(range(B)))
        cur = [np.ascontiguousarray(res.results[b]["out"]) for b in range(B)]
    return np.stack(cur, axis=0)
```
